# Optimizing a Trainium2 kernel written in Bass

```python
import math, functools
import jax
import jax.numpy as jnp
from jax import lax
import numpy as np

D_MODEL = 1024
BATCH = 32
SEQ = 2048
DEPTH = 4

GRID_W = 64
CTX_LEN = 256
N_BRANCH = 4
BRANCH_WIDTH = D_MODEL // 2
ATT_HEAD_DIM = 64
ATT_HEADS = BRANCH_WIDTH // ATT_HEAD_DIM
ATT_KV_HEADS = ATT_HEADS // 4
Q_BLOCK = 128
ROPE_THETA = 10000.0
DN_HEAD_DIM = 128
DN_HEADS = BRANCH_WIDTH // DN_HEAD_DIM
DN_CONV = 3
RET_HEADS = 4
RET_V_DIM = BRANCH_WIDTH // RET_HEADS
RET_K_DIM = RET_V_DIM // 2
SC_WIDTH = BRANCH_WIDTH
SC_CONV = 3
CHUNK = 64
MLP_HIDDEN = 4 * D_MODEL
EPS = 1e-6

ATT_Q_W = ATT_HEADS * ATT_HEAD_DIM
ATT_KV_W = ATT_KV_HEADS * ATT_HEAD_DIM
DN_W = DN_HEADS * DN_HEAD_DIM
RET_QK_W = RET_HEADS * RET_K_DIM
RET_V_W = RET_HEADS * RET_V_DIM
IN_SPLITS = (ATT_Q_W, ATT_KV_W, ATT_KV_W,
             DN_W, DN_W, DN_W, DN_W,
             DN_HEADS, DN_HEADS, DN_HEADS, DN_HEADS,
             RET_QK_W, RET_QK_W, RET_V_W, RET_V_W,
             SC_WIDTH, SC_WIDTH, SC_WIDTH,
             N_BRANCH * D_MODEL)
N_IN = sum(IN_SPLITS)

kernel_name = 'hybrid_parallel_diffusion_trunk'


def rms_norm(x, gain):
    xf = x.astype(jnp.float32)
    y = xf * lax.rsqrt(jnp.mean(xf * xf, axis=-1, keepdims=True) + EPS)
    return (y * gain.astype(jnp.float32)).astype(x.dtype)


def l2_norm(x):
    xf = x.astype(jnp.float32)
    return xf * lax.rsqrt(jnp.sum(xf * xf, axis=-1, keepdims=True) + EPS)


def head_norm(o):
    mu = jnp.mean(o, axis=-1, keepdims=True)
    var = jnp.mean(jnp.square(o - mu), axis=-1, keepdims=True)
    return (o - mu) * lax.rsqrt(var + EPS)


def rope_tables(rows, head_dim):
    r, col = jnp.meshgrid(jnp.arange(rows), jnp.arange(GRID_W), indexing='ij')
    quarter = head_dim // 4
    inv_freq = ROPE_THETA ** (-jnp.arange(quarter, dtype=jnp.float32) / quarter)
    ang = jnp.concatenate([r.reshape(-1, 1).astype(jnp.float32) * inv_freq,
                           col.reshape(-1, 1).astype(jnp.float32) * inv_freq], axis=-1)
    return jnp.cos(ang), jnp.sin(ang)


def apply_rope(x, cos, sin):
    half = x.shape[-1] // 2
    x1, x2 = x[..., :half], x[..., half:]
    c = cos[None, :, None, :].astype(x.dtype)
    s = sin[None, :, None, :].astype(x.dtype)
    return jnp.concatenate([x1 * c - x2 * s, x1 * s + x2 * c], axis=-1)


def dw_conv(x, w):
    width = w.shape[0]
    return lax.conv_general_dilated(x, w[:, None, :].astype(x.dtype), window_strides=(1,),
                                    padding=[(width // 2, width // 2)],
                                    dimension_numbers=('NWC', 'WIO', 'NWC'),
                                    feature_group_count=x.shape[-1])


def project_in(h, w_in, n_parts):
    bounds = np.cumsum(IN_SPLITS)[:-1].tolist()
    return [h @ w for w in jnp.split(w_in, bounds, axis=1)[:n_parts]]


def _sdpa(q, k, v):
    b, nq, hq, dh = q.shape
    hkv = k.shape[2]
    qg = q.reshape(b, nq, hkv, hq // hkv, dh)
    s = jnp.einsum('bqhgd,bkhd->bhgqk', qg, k).astype(jnp.float32) * (dh ** -0.5)
    p = jax.nn.softmax(s, axis=-1).astype(v.dtype)
    o = jnp.einsum('bhgqk,bkhd->bqhgd', p, v)
    return o.reshape(b, nq, hq * dh)


def attention_mixer(lat, ctx, q_gain, k_gain, rope, need_ctx):
    def heads(t, n_heads):
        return t.reshape(t.shape[0], t.shape[1], n_heads, ATT_HEAD_DIM)
    q_l, k_l, v_l = lat
    q_c, k_c, v_c = ctx
    bsz, n, _ = q_l.shape
    q_l = apply_rope(rms_norm(heads(q_l, ATT_HEADS), q_gain), *rope)
    k_l = apply_rope(rms_norm(heads(k_l, ATT_KV_HEADS), k_gain), *rope)
    q_c = rms_norm(heads(q_c, ATT_HEADS), q_gain)
    k_c = rms_norm(heads(k_c, ATT_KV_HEADS), k_gain)
    v_l, v_c = heads(v_l, ATT_KV_HEADS), heads(v_c, ATT_KV_HEADS)
    keys = jnp.concatenate([k_c, k_l], axis=1)
    vals = jnp.concatenate([v_c, v_l], axis=1)
    q_blocks = jnp.moveaxis(q_l.reshape(bsz, n // Q_BLOCK, Q_BLOCK, ATT_HEADS, ATT_HEAD_DIM), 1, 0)
    o_blocks = lax.map(lambda qb: _sdpa(qb, keys, vals), q_blocks)
    y_l = jnp.moveaxis(o_blocks, 0, 1).reshape(bsz, n, ATT_Q_W)
    y_c = _sdpa(q_c, k_c, v_c) if need_ctx else None
    return y_l, y_c


def _to_chunks(t, n):
    t = t.astype(jnp.float32).reshape((t.shape[0], n, CHUNK) + t.shape[2:])
    return jnp.moveaxis(t, 2, 3)


def _from_chunks(t):
    t = jnp.moveaxis(t, 3, 2)
    return t.reshape(t.shape[0], t.shape[1] * t.shape[2], t.shape[3], t.shape[4])


def gated_delta_rule(q, k, v, log_a, beta, s0):
    bsz, length, _, _ = q.shape
    dv = v.shape[-1]
    n = length // CHUNK
    q, k, v = _to_chunks(q, n), _to_chunks(k, n), _to_chunks(v, n)
    log_a, beta = _to_chunks(log_a, n), _to_chunks(beta, n)
    g = jnp.cumsum(log_a, axis=-1)
    diff = g[..., :, None] - g[..., None, :]
    incl = jnp.tril(jnp.ones((CHUNK, CHUNK), dtype=bool))
    strict = jnp.tril(jnp.ones((CHUNK, CHUNK), dtype=bool), -1)
    dec_incl = jnp.exp(jnp.where(incl, diff, -jnp.inf))
    dec_strict = jnp.where(strict, dec_incl, 0.0)
    kk = jnp.einsum('bnhid,bnhjd->bnhij', k, k)
    a_mat = beta[..., :, None] * kk * dec_strict + jnp.eye(CHUNK, dtype=jnp.float32)
    rhs = jnp.concatenate([beta[..., None] * v, (beta * jnp.exp(g))[..., None] * k], axis=-1)
    sol = lax.linalg.triangular_solve(a_mat, rhs, left_side=True, lower=True, unit_diagonal=True)
    w_v, w_k = sol[..., :dv], sol[..., dv:]
    a_qk = jnp.einsum('bnhid,bnhjd->bnhij', q, k) * dec_incl
    q_dec = q * jnp.exp(g)[..., None]
    g_last = g[..., -1]
    k_dec = k * jnp.exp(g_last[..., None] - g)[..., None]

    def step(s, inp):
        w_v_i, w_k_i, a_i, qd_i, kd_i, gl_i = inp
        u = w_v_i - jnp.einsum('bhcd,bhde->bhce', w_k_i, s)
        o = jnp.einsum('bhcd,bhde->bhce', qd_i, s) + jnp.einsum('bhij,bhje->bhie', a_i, u)
        s = s * jnp.exp(gl_i)[..., None, None] + jnp.einsum('bhcd,bhce->bhde', kd_i, u)
        return s, o

    xs = tuple(jnp.moveaxis(t, 1, 0) for t in (w_v, w_k, a_qk, q_dec, k_dec, g_last))
    s_fin, o = lax.scan(step, s0.astype(jnp.float32), xs)
    return _from_chunks(jnp.moveaxis(o, 0, 1)), s_fin


def retention_chunked(q, k, v, s0, log_g):
    length = q.shape[1]
    n = length // CHUNK
    q, k, v = _to_chunks(q, n), _to_chunks(k, n), _to_chunks(v, n)
    idx = jnp.arange(CHUNK, dtype=jnp.float32)
    lg = log_g.astype(jnp.float32)[:, None]
    rel = idx[:, None] - idx[None, :]
    dmask = jnp.exp(jnp.where(rel >= 0, rel[None] * lg[:, :, None], -jnp.inf))
    scores = jnp.einsum('bnhid,bnhjd->bnhij', q, k) * dmask
    o = jnp.einsum('bnhij,bnhje->bnhie', scores, v)
    k_dec = k * jnp.exp((CHUNK - 1 - idx)[None, :] * lg)[..., None]
    kv = jnp.einsum('bnhcd,bnhce->bnhde', k_dec, v)
    chunk_decay = jnp.exp(CHUNK * lg)[:, :, None]

    def step(s, kv_i):
        return s * chunk_decay + kv_i, s

    s_fin, s_prev = lax.scan(step, s0.astype(jnp.float32), jnp.moveaxis(kv, 1, 0))
    q_dec = q * jnp.exp((idx + 1)[None, :] * lg)[..., None]
    o = o + jnp.einsum('bnhcd,bnhde->bnhce', q_dec, jnp.moveaxis(s_prev, 0, 1))
    return _from_chunks(o), s_fin


def _ctx_then_latent(fn, ctx_args, lat_args, s0, reverse):
    def flip(ts):
        return tuple(jnp.flip(t, axis=1) for t in ts) if reverse else tuple(ts)
    o_c, s_c = fn(*flip(ctx_args), s0)
    o_l, _ = fn(*flip(lat_args), s_c)
    if reverse:
        o_c, o_l = jnp.flip(o_c, axis=1), jnp.flip(o_l, axis=1)
    return o_l, o_c


def deltanet_mixer(lat, ctx, conv_w, a_log, dt_bias, norm_gain, need_ctx):
    a_coef = -jnp.exp(a_log.astype(jnp.float32))

    def prep(q, k, v, z, a_f, a_b, b_f, b_b):
        bsz, n, _ = q.shape
        qkv = jax.nn.silu(dw_conv(jnp.concatenate([q, k, v], axis=-1), conv_w))
        q, k, v = jnp.split(qkv, 3, axis=-1)
        q = l2_norm(q.reshape(bsz, n, DN_HEADS, DN_HEAD_DIM)) * (DN_HEAD_DIM ** -0.5)
        k = l2_norm(k.reshape(bsz, n, DN_HEADS, DN_HEAD_DIM))
        v = v.reshape(bsz, n, DN_HEADS, DN_HEAD_DIM)
        dirs = []
        for d, (a_in, b_in) in enumerate(((a_f, b_f), (a_b, b_b))):
            log_a = a_coef[d] * jax.nn.softplus(a_in.astype(jnp.float32) + dt_bias[d].astype(jnp.float32))
            dirs.append((q, k, v, log_a, jax.nn.sigmoid(b_in.astype(jnp.float32))))
        return dirs, z

    lat_dirs, z_l = prep(*lat)
    ctx_dirs, z_c = prep(*ctx)
    bsz = z_l.shape[0]
    s0 = jnp.zeros((bsz, DN_HEADS, DN_HEAD_DIM, DN_HEAD_DIM), jnp.float32)
    fwd = _ctx_then_latent(gated_delta_rule, ctx_dirs[0], lat_dirs[0], s0, False)
    bwd = _ctx_then_latent(gated_delta_rule, ctx_dirs[1], lat_dirs[1], s0, True)

    def finish(o, z):
        zf = jax.nn.silu(z.astype(jnp.float32)).reshape(o.shape)
        y = rms_norm(o, norm_gain) * zf
        return y.reshape(o.shape[0], o.shape[1], DN_W).astype(z.dtype)

    y_l = finish(fwd[0] + bwd[0], z_l)
    y_c = finish(fwd[1] + bwd[1], z_c) if need_ctx else None
    return y_l, y_c


def retention_mixer(lat, ctx, ret_decay, rope, need_ctx):
    def prep(q, k, v, g, with_pos):
        bsz, n, _ = q.shape
        q = q.reshape(bsz, n, RET_HEADS, RET_K_DIM)
        k = k.reshape(bsz, n, RET_HEADS, RET_K_DIM) * (RET_K_DIM ** -0.5)
        if with_pos:
            q, k = apply_rope(q, *rope), apply_rope(k, *rope)
        v = v.reshape(bsz, n, RET_HEADS, RET_V_DIM)
        return (q, k, v), g

    lat_args, g_l = prep(*lat, True)
    ctx_args, g_c = prep(*ctx, False)
    log_g = -jnp.exp(ret_decay.astype(jnp.float32))
    bsz = g_l.shape[0]
    s0 = jnp.zeros((bsz, RET_HEADS, RET_K_DIM, RET_V_DIM), jnp.float32)
    fwd = _ctx_then_latent(functools.partial(retention_chunked, log_g=log_g[0]), ctx_args, lat_args, s0, False)
    bwd = _ctx_then_latent(functools.partial(retention_chunked, log_g=log_g[1]), ctx_args, lat_args, s0, True)

    def finish(o, g):
        y = head_norm(o).reshape(o.shape[0], o.shape[1], RET_V_W) * jax.nn.silu(g.astype(jnp.float32))
        return y.astype(g.dtype)

    y_l = finish(fwd[0] + bwd[0], g_l)
    y_c = finish(fwd[1] + bwd[1], g_c) if need_ctx else None
    return y_l, y_c


def shortconv_mixer(b_gate, c_gate, x_in, w):
    return b_gate * dw_conv(c_gate * x_in, w)


def merge_branches(ys, gates, w_branch, w_out):
    g = jax.nn.sigmoid(gates.reshape(gates.shape[:-1] + (N_BRANCH, D_MODEL)))
    u = g[..., 0, :] * (ys[0] @ w_branch[0])
    for i in range(1, N_BRANCH):
        u = u + g[..., i, :] * (ys[i] @ w_branch[i])
    return u @ w_out


def sq_relu_mlp(h, w1, w2):
    return jnp.square(jax.nn.relu(h @ w1)) @ w2


def layer(x, xc, c, c_ctx, w_mod, b_mod, g_norm, w_in, att_q_gain, att_k_gain, dn_conv, dn_a_log,
          dn_dt_bias, dn_norm_gain, ret_decay, sc_conv, w_branch, w_out, w_mlp_in, w_mlp_out,
          att_rope, ret_rope, need_ctx):
    mod = jax.nn.silu(c) @ w_mod + b_mod
    mod_c = jax.nn.silu(c_ctx) @ w_mod + b_mod
    sh1, sc1, ga1, sh2, sc2, ga2 = jnp.split(mod[:, None, :], 6, axis=-1)
    sh1c, sc1c, ga1c, sh2c, sc2c, ga2c = jnp.split(mod_c, 6, axis=-1)

    h = rms_norm(x, g_norm[0]) * (1 + sc1) + sh1
    hc = rms_norm(xc, g_norm[0]) * (1 + sc1c) + sh1c
    zl = project_in(h, w_in, len(IN_SPLITS))
    zc = project_in(hc, w_in, len(IN_SPLITS) if need_ctx else 15)
    y_att, yc_att = attention_mixer(zl[0:3], zc[0:3], att_q_gain, att_k_gain, att_rope, need_ctx)
    y_dn, yc_dn = deltanet_mixer(zl[3:11], zc[3:11], dn_conv, dn_a_log, dn_dt_bias, dn_norm_gain, need_ctx)
    y_ret, yc_ret = retention_mixer(zl[11:15], zc[11:15], ret_decay, ret_rope, need_ctx)
    y_sc = shortconv_mixer(zl[15], zl[16], zl[17], sc_conv)
    y = merge_branches((y_att, y_dn, y_ret, y_sc), zl[18], w_branch, w_out)
    x = x + ga1 * rms_norm(y, g_norm[1])

    h2 = rms_norm(x, g_norm[2]) * (1 + sc2) + sh2
    x = x + ga2 * rms_norm(sq_relu_mlp(h2, w_mlp_in, w_mlp_out), g_norm[3])

    if need_ctx:
        yc_sc = shortconv_mixer(zc[15], zc[16], zc[17], sc_conv)
        yc = merge_branches((yc_att, yc_dn, yc_ret, yc_sc), zc[18], w_branch, w_out)
        xc = xc + ga1c * rms_norm(yc, g_norm[1])
        h2c = rms_norm(xc, g_norm[2]) * (1 + sc2c) + sh2c
        xc = xc + ga2c * rms_norm(sq_relu_mlp(h2c, w_mlp_in, w_mlp_out), g_norm[3])
    return x, xc


def setup_inputs(seed: int = 0) -> dict:
    key = jax.random.key(seed)
    ks = jax.random.split(key, 24)
    f32 = jnp.float32

    def nrm(k, shape, scale):
        return jax.random.normal(k, shape, f32) * scale

    x = nrm(ks[0], (BATCH, SEQ, D_MODEL), 1.0)
    c = nrm(ks[1], (BATCH, D_MODEL), 1.0)
    ctx = nrm(ks[2], (BATCH, CTX_LEN, D_MODEL), 1.0)
    c_ctx = nrm(ks[3], (D_MODEL,), 1.0)
    w_mod = nrm(ks[4], (DEPTH, D_MODEL, 6 * D_MODEL), 0.5 * D_MODEL ** -0.5)
    b_mod = nrm(ks[5], (DEPTH, 6 * D_MODEL), 0.01)
    g_norm = 1.0 + nrm(ks[6], (DEPTH, 4, D_MODEL), 0.05)
    w_in = nrm(ks[7], (DEPTH, D_MODEL, N_IN), D_MODEL ** -0.5)
    att_q_gain = 1.0 + nrm(ks[8], (DEPTH, ATT_HEAD_DIM), 0.05)
    att_k_gain = 1.0 + nrm(ks[9], (DEPTH, ATT_HEAD_DIM), 0.05)
    dn_conv = nrm(ks[10], (DEPTH, DN_CONV, 3 * DN_W), DN_CONV ** -0.5)
    dn_a_log = jnp.log(jax.random.uniform(ks[11], (DEPTH, 2, DN_HEADS), f32, 1.0, 16.0))
    dt = jnp.exp(jax.random.uniform(ks[12], (DEPTH, 2, DN_HEADS), f32, math.log(1e-3), math.log(0.1)))
    dn_dt_bias = dt + jnp.log(-jnp.expm1(-dt))
    dn_norm_gain = 1.0 + nrm(ks[13], (DEPTH, DN_HEAD_DIM), 0.05)
    base = jnp.log(-jnp.log(1.0 - 2.0 ** (-5.0 - jnp.arange(RET_HEADS, dtype=f32))))
    ret_decay = base + nrm(ks[14], (DEPTH, 2, RET_HEADS), 0.05)
    sc_conv = nrm(ks[15], (DEPTH, SC_CONV, SC_WIDTH), SC_CONV ** -0.5)
    w_branch = nrm(ks[16], (DEPTH, N_BRANCH, BRANCH_WIDTH, D_MODEL), BRANCH_WIDTH ** -0.5)
    w_out = nrm(ks[17], (DEPTH, D_MODEL, D_MODEL), D_MODEL ** -0.5)
    w_mlp_in = nrm(ks[18], (DEPTH, D_MODEL, MLP_HIDDEN), D_MODEL ** -0.5)
    w_mlp_out = nrm(ks[19], (DEPTH, MLP_HIDDEN, D_MODEL), MLP_HIDDEN ** -0.5)
    return {'x': x, 'c': c, 'ctx': ctx, 'c_ctx': c_ctx, 'w_mod': w_mod, 'b_mod': b_mod,
            'g_norm': g_norm, 'w_in': w_in, 'att_q_gain': att_q_gain, 'att_k_gain': att_k_gain,
            'dn_conv': dn_conv, 'dn_a_log': dn_a_log, 'dn_dt_bias': dn_dt_bias,
            'dn_norm_gain': dn_norm_gain, 'ret_decay': ret_decay, 'sc_conv': sc_conv,
            'w_branch': w_branch, 'w_out': w_out, 'w_mlp_in': w_mlp_in, 'w_mlp_out': w_mlp_out}


def reference(x, c, ctx, c_ctx, w_mod, b_mod, g_norm, w_in, att_q_gain, att_k_gain, dn_conv,
              dn_a_log, dn_dt_bias, dn_norm_gain, ret_decay, sc_conv, w_branch, w_out,
              w_mlp_in, w_mlp_out):
    ROWS = x.shape[1] // GRID_W
    att_rope = rope_tables(ROWS, ATT_HEAD_DIM)
    ret_rope = rope_tables(ROWS, RET_K_DIM)
    xc = ctx
    for l in range(DEPTH):
        x, xc = layer(x, xc, c, c_ctx, w_mod[l], b_mod[l], g_norm[l], w_in[l], att_q_gain[l],
                      att_k_gain[l], dn_conv[l], dn_a_log[l], dn_dt_bias[l], dn_norm_gain[l],
                      ret_decay[l], sc_conv[l], w_branch[l], w_out[l], w_mlp_in[l], w_mlp_out[l],
                      att_rope, ret_rope, l < DEPTH - 1)
    return x
```

```python
import math
import os
from contextlib import ExitStack
import numpy as np
import concourse.bass as bass
import concourse.mybir as mybir
from concourse.bass_utils import run_bass_kernel_spmd

F32 = mybir.dt.float32
BF16 = mybir.dt.bfloat16
AF = mybir.ActivationFunctionType
ALU = mybir.AluOpType

ENGS = ("pe", "act", "dve", "pool", "sp")
NDMA_SEM = 12

D = 1024
T = 2304
TC = 256
TL = 2048
NBLK = [(0, 256), (256, 512), (768, 512), (1280, 512), (1792, 512)]
N_IN = 10000
EPS = 1e-6
BIG = 1.0e5


class Prog:
    def __init__(self, nc):
        self.nc = nc
        self.ops = {e: [] for e in ENGS}
        self.cnt = {e: 0 for e in ENGS}
        self.clock = {e: {} for e in ENGS}
        self.snap = {}
        self.lastw = {}
        self.readers = {}
        self.dma_rr = {e: 0 for e in ENGS}
        self.dma_cnt = {}
        self.n_wait = 0
        self.fence_ev = None
        self.fence_keep = None

    def _need(self, eng, ev, need):
        k, v = ev
        if eng == "pe" and k == "pe":
            return
        if self.clock[eng].get(k, 0) >= v:
            return
        if need.get(k, 0) < v:
            need[k] = v

    def op(self, eng, fn, r=(), w=(), wa=(), dma=False):
        need = {}
        for b in r:
            for ev in self.lastw.get(b, ()):
                self._need(eng, ev, need)
        for b in w:
            for ev in self.lastw.get(b, ()):
                self._need(eng, ev, need)
            for ev in self.readers.get(b, ()):
                self._need(eng, ev, need)
        for b in wa:
            for ev in self.readers.get(b, ()):
                self._need(eng, ev, need)
        if self.fence_ev is not None:
            if not all(self.fence_keep(k) for k in list(r) + list(w) + list(wa)):
                self._need(eng, self.fence_ev, need)
        if dma:
            i = self.dma_rr[eng]
            self.dma_rr[eng] = (i + 1) % NDMA_SEM
            key = ("dma", eng, i)
            prev = self.dma_cnt.get(key, 0)
            if prev:
                self._need(eng, (key, prev), need)
            val = prev + 16
            self.dma_cnt[key] = val
            inc = 16
        else:
            key = eng
            self.cnt[eng] += 1
            val = self.cnt[eng]
            inc = 1
        waits = list(need.items())
        if waits:
            ck = dict(self.clock[eng])
            for k, v in waits:
                if ck.get(k, 0) < v:
                    ck[k] = v
                for k2, v2 in self.snap.get((k, v), {}).items():
                    if ck.get(k2, 0) < v2:
                        ck[k2] = v2
            self.clock[eng] = ck
            self.n_wait += len(waits)
        ev = (key, val)
        self.snap[ev] = self.clock[eng]
        self.ops[eng].append((waits, fn, key, inc))
        for b in w:
            self.lastw[b] = [ev]
            self.readers[b] = []
        for b in wa:
            self.lastw.setdefault(b, []).append(ev)
        for b in r:
            self.readers.setdefault(b, []).append(ev)
        return ev

    def barrier(self, fn, eng="dve", keep=lambda k: False):
        keys = [k for k in set(self.lastw) | set(self.readers) if not keep(k)]
        ev = self.op(eng, fn, (), keys)
        self.fence_ev = ev
        self.fence_keep = keep
        return ev

    def finish(self, out_events, eng="sp"):
        need = {}
        for ev in out_events:
            self._need(eng, ev, need)
        self.ops[eng].append((list(need.items()), None, None, 0))

    def emit(self, stack):
        nc = self.nc
        keys = []
        for e in ENGS:
            for waits, fn, key, inc in self.ops[e]:
                if key is not None and key not in keys:
                    keys.append(key)
        semh = {}
        for k in keys:
            nm = "s_" + ("_".join(map(str, k)) if isinstance(k, tuple) else k)
            semh[k] = stack.enter_context(nc.semaphore(nm))
        block = stack.enter_context(nc.Block())
        engmap = {"pe": block.tensor, "act": block.scalar, "dve": block.vector,
                  "pool": block.gpsimd, "sp": block.sync}
        for e in ENGS:
            def body(eng, ops=self.ops[e]):
                for waits, fn, key, inc in ops:
                    for k, v in waits:
                        eng.wait_ge(semh[k], v)
                    if fn is not None:
                        fn(eng).then_inc(semh[key], inc)
            engmap[e](body)


def _consts():
    c = {}
    p = np.arange(128)[:, None]
    j = np.arange(128)[None, :]
    c["ident"] = (p == j).astype(np.float32)
    c["ones"] = np.ones((128, 128), np.float32)
    blk = ((p // 64) == (j // 64)).astype(np.float32)
    c["blk64m"] = blk / 64.0
    c["onesm128"] = np.ones((128, 128), np.float32) / 128.0
    R = np.zeros((128, 128), np.float32)
    for m in range(128):
        if m % 64 < 32:
            R[m + 32, m] = -1.0
        else:
            R[m - 32, m] = 1.0
    c["rrot"] = R
    c["tri_f"] = (p <= j).astype(np.float32)
    c["tri_b"] = (p >= j).astype(np.float32)
    c["offdiag"] = (p != j).astype(np.float32)
    mf = (j > p).astype(np.float32) * BIG
    mb = (j < p).astype(np.float32) * BIG
    c["mask_s"] = np.concatenate([np.tile(mf[:, None, :], (1, 4, 1)), np.tile(mb[:, None, :], (1, 4, 1))], 1).reshape(128, 1024)
    c["mask_t"] = np.concatenate([np.tile(mb[:, None, :], (1, 4, 1)), np.tile(mf[:, None, :], (1, 4, 1))], 1).reshape(128, 1024)
    jj = np.arange(512)[None, :]
    c["dbase"] = (jj - p).astype(np.float32)
    def bd(m):
        return ((p // m) == (j // m)).astype(np.float32)
    c["bd8"] = bd(8)
    c["off8"] = bd(16) - bd(8)
    c["off16"] = bd(32) - bd(16)
    c["off32"] = bd(64) - bd(32)
    c["off64"] = bd(128) - bd(64)
    e = np.zeros((128, 128), np.float32); e[:, :64] = 1
    c["ones_e"] = e
    c["ones_o"] = 1 - e
    return c


CONST_ORDER = ["ident", "ones", "blk64m", "onesm128", "rrot", "tri_f", "tri_b", "offdiag", "mask_s", "mask_t",
               "dbase", "ones_e", "ones_o", "bd8", "off8", "off16", "off32", "off64"]


def _rope_tables():
    rows = TL // 64
    r, col = np.meshgrid(np.arange(rows), np.arange(64), indexing="ij")
    quarter = 16
    inv_freq = (10000.0 ** (-np.arange(quarter, dtype=np.float32) / quarter)).astype(np.float32)
    ang = np.concatenate([r.reshape(-1, 1).astype(np.float32) * inv_freq,
                          col.reshape(-1, 1).astype(np.float32) * inv_freq], axis=-1)
    cos = np.cos(ang).astype(np.float32).T
    sin = np.sin(ang).astype(np.float32).T
    cosT = np.concatenate([cos, cos, cos, cos], 0)
    sinT = np.concatenate([sin, sin, sin, sin], 0)
    return np.ascontiguousarray(cosT), np.ascontiguousarray(sinT)


PV_G = 0
PV_DNC = 32
PV_SCC = 68
PV_QG = 80
PV_KG = 81
PV_DNG = 82
PV_BM = 83
PV_N = 131
BC_ALOG = 0
BC_DTB = 8
BC_RET = 16
BC_N = 24


def _host_params(inp, L):
    pv = np.zeros((128, L * PV_N), np.float32)
    bc = np.zeros((128, L * BC_N), np.float32)
    for l in range(L):
        o = l * PV_N
        pv[:, o + PV_G:o + PV_G + 32] = inp["g_norm"][l].reshape(32, 128).T
        pv[:, o + PV_DNC:o + PV_DNC + 36] = inp["dn_conv"][l].reshape(36, 128).T
        pv[:, o + PV_SCC:o + PV_SCC + 12] = inp["sc_conv"][l].reshape(12, 128).T
        pv[:, o + PV_QG] = np.concatenate([inp["att_q_gain"][l]] * 2)
        pv[:, o + PV_KG] = np.concatenate([inp["att_k_gain"][l]] * 2)
        pv[:, o + PV_DNG] = inp["dn_norm_gain"][l]
        pv[:, o + PV_BM:o + PV_BM + 48] = inp["b_mod"][l].reshape(48, 128).T
        ob = l * BC_N
        bc[:, ob + BC_ALOG:ob + BC_ALOG + 8] = inp["dn_a_log"][l].reshape(1, 8)
        bc[:, ob + BC_DTB:ob + BC_DTB + 8] = inp["dn_dt_bias"][l].reshape(1, 8)
        bc[:, ob + BC_RET:ob + BC_RET + 8] = inp["ret_decay"][l].reshape(1, 8)
    return pv, bc


def build(NB, L=4, enabled=("att", "dn", "ret", "sc"), dbg=()):
    nc = bass.Bass("TRN2", target_bir_lowering=False)
    P = Prog(nc)
    NCOL = NB + 1
    dram = lambda name, shape, dt=F32, kind="ExternalInput": nc.dram_tensor(name, shape, dt, kind=kind).ap()
    xT_in = dram("xT_in", [NB, D, T])
    cT_in = dram("cT", [128, 8, NCOL])
    pv_in = dram("pv", [128, L * PV_N])
    bc_in = dram("bc", [128, L * BC_N])
    cst = _consts()
    cwid = {k: cst[k].shape[1] for k in CONST_ORDER}
    coff = {}
    o = 0
    for k in CONST_ORDER:
        coff[k] = o
        o += cwid[k]
    NCONST = o
    const_in = dram("consts", [128, NCONST])
    cos_in = dram("cosT", [128, TL])
    sin_in = dram("sinT", [128, TL])
    w_mod = dram("w_mod", [L, D, 6 * D])
    w_in = dram("w_in", [L, D, N_IN])
    w_branch = dram("w_branch", [L, 4, 512, D])
    w_out = dram("w_out", [L, D, D])
    w_m1 = dram("w_mlp_in", [L, D, 4 * D])
    w_m2 = dram("w_mlp_out", [L, 4 * D, D])
    outT = dram("outT", [NB, D, TL], kind="ExternalOutput")
    dbg_out = {}
    for name, shape in dbg:
        dbg_out[name] = dram("dbg_" + name, shape, kind="ExternalOutput")
    wb_in = dram("wb_in", [L, D, N_IN], BF16, kind="Internal")
    wb_branch = dram("wb_branch", [L, 4, 512, D], BF16, kind="Internal")
    wb_out = dram("wb_out", [L, D, D], BF16, kind="Internal")
    wb_m1 = dram("wb_m1", [L, D, 4 * D], BF16, kind="Internal")
    wb_m2 = dram("wb_m2", [L, 4 * D, D], BF16, kind="Internal")
    xres = dram("xres", [D, T], F32, kind="Internal")
    ybr = dram("ybr", [4, 512, T], BF16, kind="Internal")
    dnq = dram("dnq", [12, 128, T], F32, kind="Internal")
    dno = dram("dno", [2, 4, 128, T], F32, kind="Internal")

    out_events = []
    with ExitStack() as st:
        def sb(name, shape, dt=F32):
            return st.enter_context(nc.sbuf_tensor(name, shape, dt))

        hT = sb("hT", [128, 8, T], BF16)
        consts = sb("consts_sb", [128, NCONST])
        C = {k: consts[:, coff[k]:coff[k] + cwid[k]] for k in CONST_ORDER}
        pv = sb("pv_sb", [128, L * PV_N])
        bc = sb("bc_sb", [128, L * BC_N])
        modT = sb("modT", [128, L, 48, NCOL])
        modA = sb("modA", [128, L, 4, 8, NCOL])
        cTs = sb("cTs", [128, 8, NCOL])
        eps_t = sb("eps_t", [128, 1])
        ones_bf = sb("ones_bf", [128, 128], BF16)
        ones_eo = sb("ones_eo", [128, 2, 128], BF16)
        NW = 4
        wslot = [sb("wslot%d" % i, [128, 2048], BF16) for i in range(NW)]
        AR_WORDS = 26624
        arena = sb("arena", [128, AR_WORDS])
        dummy = sb("fdummy", [128, 8])

        def aview(off_w, shape, dt=F32):
            n = int(np.prod(shape))
            words = n if dt == F32 else (n + 1) // 2
            assert off_w + words <= AR_WORDS, (off_w, words)
            v = arena[:, off_w:off_w + words]
            if dt != F32:
                v = v.bitcast(dt)
            if len(shape) == 2:
                v = v.rearrange("p (a b) -> p a b", b=shape[1])
            elif len(shape) == 3:
                v = v.rearrange("p (a b c) -> p a b c", b=shape[1], c=shape[2])
            elif len(shape) == 4:
                v = v.rearrange("p (a b c d) -> p a b c d", b=shape[1], c=shape[2], d=shape[3])
            return v

        def fence():
            keep = lambda k: k in ("consts", "cos", "sin", "pv", "bc", "eps", "ones_bf", "ones_eo", "hT") or (
                isinstance(k, tuple) and k[0] in ("ws", "wb_in", "wb_br", "wb_out", "wb_m1", "wb_m2", "modT", "modA", "xres", "ybr"))
            P.barrier(lambda e: e.memset(dummy[:], 0.0), "dve", keep)
            for kk_ in ("bank", "t5", "yb", "od", "odr", "sbk"):
                state[kk_] = 0

        wst = aview(0, [2, 8, 512])
        xblk = aview(0, [8, 512])
        wkA = aview(4096, [8, 512])
        wkB = aview(8192, [8, 512])
        ybl = aview(12288, [4, 4, 512], BF16)
        ubf = aview(16384, [8, 512], BF16)
        hid = aview(18432, [32, 512], BF16)
        rsA = sb("rsA", [128, 512])
        t512 = [sb("t512_%d" % i, [128, 512]) for i in range(6)]
        ps = [st.enter_context(nc.psum_tensor("ps%d" % i, [128, 512], F32)) for i in range(8)]
        state = {"bank": 0, "ws": 0, "t5": 0}

        def bank():
            b = state["bank"]
            state["bank"] = (b + 1) % 8
            return b

        def psk(b):
            return "ps%d" % b

        def t5():
            i = state["t5"]
            state["t5"] = (i + 1) % 6
            return t512[i], "t512_%d" % i

        def mm(out, lhsT, rhs, start=True, stop=True, r=(), w=()):
            return P.op("pe", lambda e: e.matmul(out, lhsT=lhsT, rhs=rhs, start=start, stop=stop), r, w)

        def tr(out, in_, ident, r=(), w=()):
            return P.op("pe", lambda e: e.transpose(out=out, in_=in_, identity=ident), r, w)

        def act(out, in_, func, r=(), w=(), scale=1.0, bias=None, wa=()):
            if bias is None:
                return P.op("act", lambda e: e.activation(out=out, in_=in_, func=func, scale=scale), r, w, wa)
            return P.op("act", lambda e: e.activation(out=out, in_=in_, func=func, scale=scale, bias=bias), r, w, wa)

        def tt(eng, out, in0, in1, op, r=(), w=(), wa=()):
            return P.op(eng, lambda e: e.tensor_tensor(out=out, in0=in0, in1=in1, op=op), r, w, wa)

        def tsc(eng, out, in0, s1, op0, s2=None, op1=None, r=(), w=(), wa=()):
            if op1 is None:
                return P.op(eng, lambda e: e.tensor_scalar(out=out, in0=in0, scalar1=s1, scalar2=None, op0=op0), r, w, wa)
            return P.op(eng, lambda e: e.tensor_scalar(out=out, in0=in0, scalar1=s1, scalar2=s2, op0=op0, op1=op1), r, w, wa)

        def stt(out, in0, scalar, in1, op0, op1, r=(), w=(), wa=()):
            return P.op("dve", lambda e: e.scalar_tensor_tensor(out=out, in0=in0, scalar=scalar, in1=in1, op0=op0, op1=op1), r, w, wa)

        def cp(eng, out, in_, r=(), w=(), wa=()):
            if eng == "act":
                return act(out, in_, AF.Copy, r, w, wa=wa)
            return P.op(eng, lambda e: e.tensor_copy(out=out, in_=in_), r, w, wa)

        def recip(out, in_, r=(), w=()):
            return P.op("dve", lambda e: e.reciprocal(out=out, in_=in_), r, w)

        def dma(q, out, in_, r=(), w=(), wa=()):
            return P.op(q, lambda e: e.dma_start(out=out, in_=in_), r, w, wa, dma=True)

        def memset(eng, ap, val, w=()):
            return P.op(eng, lambda e: e.memset(ap, val), (), w)

        dma("sp", consts[:], const_in, w=["consts"])
        dma("sp", pv[:], pv_in, w=["pv"])
        dma("sp", bc[:], bc_in, w=["bc"])
        dma("sp", cTs[:], cT_in, w=["cTs"])
        memset("dve", eps_t[:], EPS, w=["eps"])
        cp("dve", ones_bf[:], C["ones"], r=["consts"], w=["ones_bf"])
        cp("dve", ones_eo[:, 0, :], C["ones_e"], r=["consts"], w=["ones_eo"])
        cp("dve", ones_eo[:, 1, :], C["ones_o"], r=["consts"], w=["ones_eo"])

        for l in range(L):
            for k in range(8):
                dma("pool", wb_in[l, k * 128:(k + 1) * 128, :], w_in[l, k * 128:(k + 1) * 128, :], w=[("wb_in", l, k)])
        for l in range(L):
            for i in range(4):
                for k in range(4):
                    dma("pool", wb_branch[l, i, k * 128:(k + 1) * 128, :], w_branch[l, i, k * 128:(k + 1) * 128, :], w=[("wb_br", l, i, k)])
            for k in range(8):
                dma("pool", wb_out[l, k * 128:(k + 1) * 128, :], w_out[l, k * 128:(k + 1) * 128, :], w=[("wb_out", l, k)])
            for k in range(8):
                dma("pool", wb_m1[l, k * 128:(k + 1) * 128, :], w_m1[l, k * 128:(k + 1) * 128, :], w=[("wb_m1", l, k)])
            for k in range(32):
                dma("pool", wb_m2[l, k * 128:(k + 1) * 128, :], w_m2[l, k * 128:(k + 1) * 128, :], w=[("wb_m2", l, k)])

        act(cTs[:], cTs[:], AF.Silu, r=["cTs"], w=["cTs"])
        for l in range(L):
            mb = bank()
            mps = ps[mb][:, 0:48 * NCOL].rearrange("p (j c) -> p j c", c=NCOL)
            for g in range(12):
                s = g % 2
                dma("sp", wst[:, s], w_mod[l, :, g * 512:(g + 1) * 512].rearrange("(k p) m -> p k m", p=128), w=[("wst", s)])
                for j4 in range(4):
                    j = g * 4 + j4
                    for k in range(8):
                        mm(mps[:, j, :], wst[:, s, k, j4 * 128:(j4 + 1) * 128], cTs[:, k, :], start=(k == 0), stop=(k == 7),
                           r=[("wst", s), "cTs"], w=[psk(mb)])
            bm = pv[:, l * PV_N + PV_BM:l * PV_N + PV_BM + 48]
            tt("dve", modT[:, l], mps, bm.unsqueeze(2).to_broadcast([128, 48, NCOL]), ALU.add, r=[psk(mb), "pv"], w=[("modT", l)])
            gl = pv[:, l * PV_N + PV_G:l * PV_N + PV_G + 32]
            for idx, (gi, mo, plus1) in enumerate([(0, 8, True), (2, 32, True), (1, 16, False), (3, 40, False)]):
                src = modT[:, l, mo:mo + 8, :]
                gb = gl[:, gi * 8:(gi + 1) * 8].unsqueeze(2).to_broadcast([128, 8, NCOL])
                if plus1:
                    tsc("dve", modA[:, l, idx], src, 1.0, ALU.add, r=[("modT", l)], w=[("modA", l, idx)])
                    tt("dve", modA[:, l, idx], modA[:, l, idx], gb, ALU.mult, r=[("modA", l, idx), "pv"], w=[("modA", l, idx)])
                else:
                    tt("dve", modA[:, l, idx], src, gb, ALU.mult, r=[("modT", l), "pv"], w=[("modA", l, idx)])

        fence()
        if "modT" in dbg_out:
            out_events.append(dma("sp", dbg_out["modT"], modT[:], r=[("modT", l) for l in range(L)]))

        def wslot_next():
            s = state["ws"]
            state["ws"] = (s + 1) % NW
            return s

        def wload(parts):
            s = wslot_next()
            key = ("ws", s)
            views = []
            off = 0
            first = True
            for ap, nk, rk in parts:
                M = ap.shape[1]
                v = wslot[s][:, off:off + nk * M].rearrange("p (k m) -> p k m", m=M)
                src = ap.rearrange("(k p) m -> p k m", p=128)
                if first:
                    dma("sp", v, src, r=rk, w=[key])
                else:
                    dma("sp", v, src, r=rk, wa=[key])
                first = False
                views.append(v)
                off += nk * M
            assert off <= 2048
            return views, key

        def win_keys(l):
            return [("wb_in", l, k) for k in range(8)]

        def proj(b, M, wv, m0, t0, n, wkey):
            for k in range(8):
                mm(ps[b][:M, :n], wv[:, k, m0:m0 + M], hT[:, k, t0:t0 + n], start=(k == 0), stop=(k == 7),
                   r=[wkey, "hT"], w=[psk(b)])

        def rms_rstd(src, n, rkeys):
            act(wkB[:, :, :n], src, AF.Square, r=rkeys, w=["wkB"])
            b = bank()
            for k in range(8):
                mm(ps[b][:, :n], C["ones"], wkB[:, k, :n], start=(k == 0), stop=(k == 7), r=["wkB", "consts"], w=[psk(b)])
            act(rsA[:, :n], ps[b][:, :n], AF.Sqrt, r=[psk(b), "eps"], w=["rsA"], scale=1.0 / D, bias=eps_t[:])
            recip(rsA[:, :n], rsA[:, :n], r=["rsA"], w=["rsA"])

        def mod_col(l, idx, k, col):
            return modA[:, l, idx, k, col:col + 1]

        def norm_mod_block(l, col, which, src, t0, n, rkeys):
            rms_rstd(src, n, rkeys)
            tt("dve", wkB[:, :, :n], src, rsA[:, :n].unsqueeze(1).to_broadcast([128, 8, n]), ALU.mult,
               r=list(rkeys) + ["rsA"], w=["wkB"])
            sh0 = 0 if which == 0 else 24
            for k in range(8):
                act(hT[:, k, t0:t0 + n], wkB[:, k, :n], AF.Identity, r=["wkB", ("modA", l, which), ("modT", l)], w=["hT"],
                    scale=mod_col(l, which, k, col), bias=modT[:, l, sh0 + k, col:col + 1])

        def residual_block(l, col, which, ysrc, n, ykeys):
            rms_rstd(ysrc, n, ykeys)
            tt("dve", wkB[:, :, :n], ysrc, rsA[:, :n].unsqueeze(1).to_broadcast([128, 8, n]), ALU.mult,
               r=list(ykeys) + ["rsA"], w=["wkB"])
            for k in range(8):
                stt(xblk[:, k, :n], wkB[:, k, :n], mod_col(l, 2 + which, k, col), xblk[:, k, :n], ALU.mult, ALU.add,
                    r=["wkB", "xblk", ("modA", l, 2 + which)], w=["xblk"])

        xres_v = xres.rearrange("(k p) t -> p k t", p=128)

        from_mixers = {}

        def phase_a(l, b):
            for bi, (t0, n) in enumerate(NBLK):
                col = NB if bi == 0 else b
                dma("sp", xblk[:, :, :n], xres_v[:, :, t0:t0 + n], r=[("xres", bi)], w=["xblk"])
                norm_mod_block(l, col, 0, xblk[:, :, :n], t0, n, ["xblk"])

        tscv = aview(0, [T + 4])

        def seg_off(t0):
            return t0 + 1 if t0 < TC else t0 + 3

        def mixer_sc(l, b):
            base = 4368
            memset("pool", tscv, 0.0, w=["tsc"])
            for c in range(4):
                (wc, wx), k1 = wload([(wb_in[l, :, base + 512 + c * 128: base + 512 + (c + 1) * 128], 8, win_keys(l)),
                                      (wb_in[l, :, base + 1024 + c * 128: base + 1024 + (c + 1) * 128], 8, win_keys(l))])
                for (t0, n) in NBLK:
                    b1 = bank(); b2 = bank()
                    proj(b1, 128, wc, 0, t0, n, k1)
                    proj(b2, 128, wx, 0, t0, n, k1)
                    xs, xk = t5()
                    cp("act", xs[:, :n], ps[b2][:, :n], r=[psk(b2)], w=[xk])
                    so = seg_off(t0)
                    tt("dve", tscv[:, so:so + n], ps[b1][:, :n], xs[:, :n], ALU.mult, r=[psk(b1), xk], w=["tsc"])
                (wbv,), k2 = wload([(wb_in[l, :, base + c * 128: base + (c + 1) * 128], 8, win_keys(l))])
                wcol = lambda tap: pv[:, l * PV_N + PV_SCC + tap * 4 + c: l * PV_N + PV_SCC + tap * 4 + c + 1]
                for (t0, n) in NBLK:
                    b1 = bank()
                    proj(b1, 128, wbv, 0, t0, n, k2)
                    so = seg_off(t0)
                    cv, ck = t5()
                    tsc("dve", cv[:, :n], tscv[:, so - 1:so - 1 + n], wcol(0), ALU.mult, r=["tsc", "pv"], w=[ck])
                    stt(cv[:, :n], tscv[:, so:so + n], wcol(1), cv[:, :n], ALU.mult, ALU.add, r=["tsc", "pv", ck], w=[ck])
                    stt(cv[:, :n], tscv[:, so + 1:so + 1 + n], wcol(2), cv[:, :n], ALU.mult, ALU.add, r=["tsc", "pv", ck], w=[ck])
                    yb_, yk = ybuf()
                    tt("dve", yb_[:, :n], ps[b1][:, :n], cv[:, :n], ALU.mult, r=[psk(b1), ck], w=[yk])
                    dma("pool", ybr[3, c * 128:(c + 1) * 128, t0:t0 + n], yb_[:, :n], r=[yk], wa=[("ybr", 3)])

        def mixer_att(l, b):
            QT = aview(0, [4, T], BF16)
            KT = aview(4608, [2, T], BF16)
            VP = aview(6912, [18, 2, 2, 128], BF16)
            PT = [aview(11520 + i * 256, [512], BF16) for i in range(4)]
            qn = aview(12544, [512]); sqb = aview(13056, [512]); rsb = aview(13568, [512])
            t1 = aview(14080, [512]); t2 = aview(14592, [512]); rD = aview(15104, [512])
            cosT = aview(15616, [TL]); sinT = aview(17664, [TL])
            dma("sp", cosT, cos_in, w=["cos"])
            dma("sp", sinT, sin_in, w=["sin"])
            memset("pool", VP, 0.0, w=["VP"])
            keys = win_keys(l)
            for kind, ci in [("q", 0), ("q", 1), ("q", 2), ("q", 3), ("k", 0), ("k", 1)]:
                if kind == "q":
                    (wv,), wk = wload([(wb_in[l, :, ci * 128:(ci + 1) * 128], 8, keys)])
                    gain = pv[:, l * PV_N + PV_QG:l * PV_N + PV_QG + 1]
                    dst = QT[:, ci]; dkey = "QT"
                else:
                    s_ = wslot_next(); wk = ("ws", s_)
                    wv = wslot[s_][:, 0:1024].rearrange("p (k m) -> p k m", m=128)
                    src = wb_in[l, :, 512 + ci * 64:512 + (ci + 1) * 64].rearrange("(k p) m -> p k m", p=128)
                    dma("sp", wv[:, :, 0:64], src, r=keys, w=[wk])
                    dma("sp", wv[:, :, 64:128], src, r=keys, wa=[wk])
                    gain = pv[:, l * PV_N + PV_KG:l * PV_N + PV_KG + 1]
                    dst = KT[:, ci]; dkey = "KT"
                for bi, (t0, n) in enumerate(NBLK):
                    bq = bank()
                    proj(bq, 128, wv, 0, t0, n, wk)
                    act(sqb[:, :n], ps[bq][:, :n], AF.Square, r=[psk(bq)], w=["sqb"])
                    bs = bank()
                    mm(ps[bs][:, :n], C["blk64m"], sqb[:, :n], r=["sqb", "consts"], w=[psk(bs)])
                    act(rsb[:, :n], ps[bs][:, :n], AF.Sqrt, r=[psk(bs), "eps"], w=["rsb"], bias=eps_t[:])
                    recip(rsb[:, :n], rsb[:, :n], r=["rsb"], w=["rsb"])
                    if bi == 0:
                        stt(dst[:, t0:t0 + n], ps[bq][:, :n], gain, rsb[:, :n], ALU.mult, ALU.mult,
                            r=[psk(bq), "rsb", "pv"], wa=[dkey])
                    else:
                        stt(qn[:, :n], ps[bq][:, :n], gain, rsb[:, :n], ALU.mult, ALU.mult, r=[psk(bq), "rsb", "pv"], w=["qn"])
                        br = bank()
                        mm(ps[br][:, :n], C["rrot"], qn[:, :n], r=["qn", "consts"], w=[psk(br)])
                        lo = t0 - TC
                        tt("pool", t1[:, :n], qn[:, :n], cosT[:, lo:lo + n], ALU.mult, r=["qn", "cos"], w=["t1"])
                        tt("dve", t2[:, :n], ps[br][:, :n], sinT[:, lo:lo + n], ALU.mult, r=[psk(br), "sin"], w=["t2"])
                        tt("pool", dst[:, t0:t0 + n], t1[:, :n], t2[:, :n], ALU.add, r=["t1", "t2"], wa=[dkey])
            import os
            ATT_STOP = int(os.environ.get("ATT_STOP", "9"))
            if ATT_STOP <= 1:
                return
            (wvv,), wk = wload([(wb_in[l, :, 640:768], 8, keys)])
            for tg in range(5):
                tts = list(range(tg * 4, min(18, tg * 4 + 4)))
                bv = bank()
                for j, tti in enumerate(tts):
                    for k in range(8):
                        mm(ps[bv][:, j * 128:(j + 1) * 128], hT[:, k, tti * 128:(tti + 1) * 128], wvv[:, k, :],
                           start=(k == 0), stop=(k == 7), r=[wk, "hT"], w=[psk(bv)])
                nt = len(tts)
                pv4 = ps[bv][:, :nt * 128].rearrange("p (j m) -> p j m", m=128)
                VCOPY = os.environ.get("VCOPY", "act,act")
                for g in range(2):
                    for e in range(2):
                        if VCOPY == "none":
                            continue
                        cp(VCOPY.split(",")[(g + e) % 2], VP[:, tg * 4:tg * 4 + nt, g, e, e * 64:(e + 1) * 64],
                           pv4[:, :, g * 64:(g + 1) * 64], r=[psk(bv)], wa=["VP"])
            if ATT_STOP <= 2:
                return
            for bi, (t0, n) in enumerate(NBLK):
                ktiles = [0, 1] if bi == 0 else list(range(18))
                for c in range(4):
                    g = c // 2
                    od = state.get("od", 0); state["od"] = 1 - od
                    bo, bd = 4 + 2 * od, 5 + 2 * od
                    steps = [(kt, e) for kt in ktiles for e in (0, 1)]
                    ns = len(steps)
                    for i in range(ns + 1):
                        if i < ns:
                            kt, e = steps[i]
                            bsx = state.get("sbk", 0); state["sbk"] = (bsx + 1) % 4
                            mm(ps[bsx][:, :n], KT[e * 64:(e + 1) * 64, g, kt * 128:(kt + 1) * 128],
                               QT[e * 64:(e + 1) * 64, c, t0:t0 + n], r=["KT", "QT"], w=[psk(bsx)])
                            act(PT[i % 4][:, :n], ps[bsx][:, :n], AF.Exp, r=[psk(bsx)], w=[("PT", i % 4)], scale=0.125)
                        if i >= 1:
                            kt, e = steps[i - 1]
                            pt = PT[(i - 1) % 4]
                            mm(ps[bo][:, :n], VP[:, kt, g, e, :], pt[:, :n], start=(i == 1), stop=(i == ns),
                               r=[("PT", (i - 1) % 4), "VP"], w=[psk(bo)])
                            mm(ps[bd][:, :n], ones_eo[:, e, :], pt[:, :n], start=(i == 1), stop=(i == ns),
                               r=[("PT", (i - 1) % 4), "ones_eo"], w=[psk(bd)])
                    recip(rD[:, :n], ps[bd][:, :n], r=[psk(bd)], w=["rD"])
                    yb_, yk = ybuf()
                    tt("dve", yb_[:, :n], ps[bo][:, :n], rD[:, :n], ALU.mult, r=[psk(bo), "rD"], w=[yk])
                    dma("pool", ybr[0, c * 128:(c + 1) * 128, t0:t0 + n], yb_[:, :n], r=[yk], wa=[("ybr", 0)])

        lgE = sb("lgE", [128, 8]); lgN = sb("lgN", [128, 8])
        OFF = 1920
        FW = 3968

        def mixer_ret(l, b):
            RQ = aview(0, [2, T], BF16)
            RK = aview(2304, [2, T], BF16)
            RV = aview(4608, [18, 512], BF16)
            Fm = aview(9216, [FW])
            PT = [aview(13184 + i * 256, [512], BF16) for i in range(4)]
            o_ = 14208
            qn = aview(o_, [512]); t1 = aview(o_ + 512, [512]); t2 = aview(o_ + 1024, [512]); dd = aview(o_ + 1536, [512])
            u2 = aview(o_ + 2048, [512]); msk = aview(o_ + 2560, [512]); osb = aview(o_ + 3072, [512]); sqo = aview(o_ + 3584, [512])
            mean_s = aview(o_ + 4096, [512]); tmpv = aview(o_ + 4608, [512]); rso = aview(o_ + 5120, [512]); sg = aview(o_ + 5632, [512])
            dd2 = aview(o_ + 6144, [512]); e1 = aview(o_ + 6656, [512])
            cosT = aview(o_ + 7168, [TL]); sinT = aview(o_ + 7168 + TL, [TL])
            dma("sp", cosT, cos_in, w=["cos"])
            dma("sp", sinT, sin_in, w=["sin"])
            keys = win_keys(l)
            dbase = C["dbase"]
            bcr = bc[:, l * BC_N + BC_RET:l * BC_N + BC_RET + 8]
            act(lgE[:], bcr, AF.Exp, r=["bc"], w=["lgE"])
            tsc("dve", lgN[:], lgE[:], -1.0, ALU.mult, r=["lgE"], w=["lgN"])
            for kind, ci in [("q", 0), ("q", 1), ("k", 0), ("k", 1)]:
                c0 = (2832 if kind == "q" else 3088) + ci * 128
                (wv,), wk = wload([(wb_in[l, :, c0:c0 + 128], 8, keys)])
                dst = (RQ if kind == "q" else RK)[:, ci]
                dkey = "RQ" if kind == "q" else "RK"
                sc_ = 1.0 if kind == "q" else 0.125
                for bi, (t0, n) in enumerate(NBLK):
                    bq = bank()
                    proj(bq, 128, wv, 0, t0, n, wk)
                    if bi == 0:
                        act(dst[:, t0:t0 + n], ps[bq][:, :n], AF.Copy, r=[psk(bq)], wa=[dkey], scale=sc_)
                    else:
                        act(qn[:, :n], ps[bq][:, :n], AF.Copy, r=[psk(bq)], w=["qn"], scale=sc_)
                        br = bank()
                        mm(ps[br][:, :n], C["rrot"], qn[:, :n], r=["qn", "consts"], w=[psk(br)])
                        lo = t0 - TC
                        tt("pool", t1[:, :n], qn[:, :n], cosT[:, lo:lo + n], ALU.mult, r=["qn", "cos"], w=["t1"])
                        tt("dve", t2[:, :n], ps[br][:, :n], sinT[:, lo:lo + n], ALU.mult, r=[psk(br), "sin"], w=["t2"])
                        tt("pool", dst[:, t0:t0 + n], t1[:, :n], t2[:, :n], ALU.add, r=["t1", "t2"], wa=[dkey])
            for half in range(2):
                (wvv,), wk = wload([(wb_in[l, :, 3344 + half * 256:3344 + (half + 1) * 256], 8, keys)])
                for tti in range(18):
                    bv = bank()
                    for k in range(8):
                        mm(ps[bv][:, :256], hT[:, k, tti * 128:(tti + 1) * 128], wvv[:, k, :], start=(k == 0), stop=(k == 7),
                           r=[wk, "hT"], w=[psk(bv)])
                    cp("act", RV[:, tti, half * 256:(half + 1) * 256], ps[bv][:, :256], r=[psk(bv)], wa=["RV"])
            for h in range(4):
                cq, e = h // 2, h % 2
                lgf = lgN[:, h:h + 1]; lgb = lgN[:, 4 + h:5 + h]; nlgb = lgE[:, 4 + h:5 + h]
                for pi in range(8):
                    m0 = pi * 512
                    w_ = min(512, FW - m0)
                    tsc("pool", dd[:, :w_], dbase[:, :w_], float(m0 - OFF), ALU.add, r=["consts"], w=["dd"])
                    tsc("pool", u2[:, :w_], dd[:, :w_], nlgb, ALU.mult, r=["dd", "lgE"], w=["u2"])
                    stt(u2[:, :w_], dd[:, :w_], lgf, u2[:, :w_], ALU.mult, ALU.min, r=["dd", "u2", "lgN"], w=["u2"])
                    act(u2[:, :w_], u2[:, :w_], AF.Exp, r=["u2"], w=["u2"])
                    stt(Fm[:, m0:m0 + w_], dd[:, :w_], 0.0, u2[:, :w_], ALU.is_equal, ALU.add, r=["dd", "u2"], w=["Fm"])
                (wg,), wkg = wload([(wb_in[l, :, 3856 + h * 128:3856 + (h + 1) * 128], 8, keys)])
                for bi, (t0, n) in enumerate(NBLK):
                    ktiles = [0, 1] if bi == 0 else list(range(18))
                    t0l = t0 - TC
                    od = state.get("odr", 0); state["odr"] = 1 - od
                    bo = 4 + od
                    ns = len(ktiles)
                    info = {}
                    for i in range(ns + 1):
                        if i < ns:
                            kt = ktiles[i]
                            bsx = state.get("sbk", 0); state["sbk"] = (bsx + 1) % 4
                            mm(ps[bsx][:, :n], RK[e * 64:(e + 1) * 64, cq, kt * 128:(kt + 1) * 128],
                               RQ[e * 64:(e + 1) * 64, cq, t0:t0 + n], r=["RK", "RQ"], w=[psk(bsx)])
                            pt = PT[i % 4]; pk = ("PT", i % 4)
                            if bi == 0:
                                st_ = OFF - kt * 128
                                tt("dve", pt[:, :n], ps[bsx][:, :n], Fm[:, st_:st_ + n], ALU.mult, r=[psk(bsx), "Fm"], w=[pk])
                            elif kt >= 2:
                                st_ = OFF + t0l - (kt - 2) * 128
                                tt("dve", pt[:, :n], ps[bsx][:, :n], Fm[:, st_:st_ + n], ALU.mult, r=[psk(bsx), "Fm"], w=[pk])
                            else:
                                o1 = float(256 + t0l - kt * 128); o2 = float(2048 - t0l + kt * 128)
                                tsc("pool", dd[:, :n], dbase[:, :n], o1, ALU.add, r=["consts"], w=["dd"])
                                act(e1[:, :n], dd[:, :n], AF.Exp, r=["dd", "lgN"], w=["e1"], scale=lgf)
                                tsc("pool", dd2[:, :n], dbase[:, :n], -1.0, ALU.mult, o2, ALU.add, r=["consts"], w=["dd2"])
                                act(msk[:, :n], dd2[:, :n], AF.Exp, r=["dd2", "lgN"], w=["msk"], scale=lgb)
                                tt("pool", msk[:, :n], msk[:, :n], e1[:, :n], ALU.add, r=["msk", "e1"], w=["msk"])
                                tt("dve", pt[:, :n], ps[bsx][:, :n], msk[:, :n], ALU.mult, r=[psk(bsx), "msk"], w=[pk])
                        if i >= 1:
                            kt = ktiles[i - 1]
                            mm(ps[bo][:, :n], RV[:, kt, h * 128:(h + 1) * 128], PT[(i - 1) % 4][:, :n], start=(i == 1), stop=(i == ns),
                               r=[("PT", (i - 1) % 4), "RV"], w=[psk(bo)])
                    proj(6, 128, wg, 0, t0, n, wkg)
                    act(sg[:, :n], ps[6][:, :n], AF.Silu, r=[psk(6)], w=["sg"])
                    act(osb[:, :n], ps[bo][:, :n], AF.Copy, r=[psk(bo)], w=["osb"])
                    act(sqo[:, :n], ps[bo][:, :n], AF.Square, r=[psk(bo)], w=["sqo"])
                    mm(ps[7][:, :n], C["onesm128"], osb[:, :n], r=["osb", "consts"], w=[psk(7)])
                    bx = state.get("sbk", 0); state["sbk"] = (bx + 1) % 4
                    mm(ps[bx][:, :n], C["onesm128"], sqo[:, :n], r=["sqo", "consts"], w=[psk(bx)])
                    act(mean_s[:, :n], ps[7][:, :n], AF.Copy, r=[psk(7)], w=["mean_s"])
                    tt("pool", tmpv[:, :n], mean_s[:, :n], mean_s[:, :n], ALU.mult, r=["mean_s"], w=["tmpv"])
                    tt("dve", rso[:, :n], ps[bx][:, :n], tmpv[:, :n], ALU.subtract, r=[psk(bx), "tmpv"], w=["rso"])
                    tsc("dve", rso[:, :n], rso[:, :n], 0.0, ALU.max, r=["rso"], w=["rso"])
                    act(rso[:, :n], rso[:, :n], AF.Sqrt, r=["rso", "eps"], w=["rso"], bias=eps_t[:])
                    recip(rso[:, :n], rso[:, :n], r=["rso"], w=["rso"])
                    tt("pool", osb[:, :n], osb[:, :n], mean_s[:, :n], ALU.subtract, r=["osb", "mean_s"], w=["osb"])
                    tt("pool", osb[:, :n], osb[:, :n], rso[:, :n], ALU.mult, r=["osb", "rso"], w=["osb"])
                    yb_, yk = ybuf()
                    tt("dve", yb_[:, :n], osb[:, :n], sg[:, :n], ALU.mult, r=["osb", "sg"], w=[yk])
                    dma("pool", ybr[2, h * 128:(h + 1) * 128, t0:t0 + n], yb_[:, :n], r=[yk], wa=[("ybr", 2)])

        la_tm = sb("la_tm", [128, 18, 8]); beta_tm = sb("beta_tm", [128, 18, 8]); acoef = sb("acoef", [128, 8])
        FORD = list(range(18))
        BORD = [1, 0] + list(range(17, 1, -1))

        def mixer_dn(l, b):
            keys = win_keys(l)
            ident = C["ident"]; ones = C["ones"]
            pre = aview(0, [T + 4]); cvb = aview(2308, [512]); slb = aview(2820, [512]); sqb = aview(3332, [512])
            rsb = aview(3844, [512]); ab = aview(4356, [18, 16]); tmp8 = aview(4644, [18, 8]); tmp8b = aview(4788, [18, 8])
            stg = [aview(4932 + i * 512, [512]) for i in range(2)]
            memset("pool", pre, 0.0, w=["pre"])
            for j in range(12):
                kind, h = j // 4, j % 4
                c0 = 768 + kind * 512 + h * 128
                (wv,), wk = wload([(wb_in[l, :, c0:c0 + 128], 8, keys)])
                for (t0, n) in NBLK:
                    bq = bank()
                    proj(bq, 128, wv, 0, t0, n, wk)
                    so = seg_off(t0)
                    cp("act", pre[:, so:so + n], ps[bq][:, :n], r=[psk(bq)], w=["pre"])
                wcol = lambda tap: pv[:, l * PV_N + PV_DNC + tap * 12 + j: l * PV_N + PV_DNC + tap * 12 + j + 1]
                for bi, (t0, n) in enumerate(NBLK):
                    so = seg_off(t0)
                    tsc("dve", cvb[:, :n], pre[:, so - 1:so - 1 + n], wcol(0), ALU.mult, r=["pre", "pv"], w=["cvb"])
                    stt(cvb[:, :n], pre[:, so:so + n], wcol(1), cvb[:, :n], ALU.mult, ALU.add, r=["pre", "pv", "cvb"], w=["cvb"])
                    stt(cvb[:, :n], pre[:, so + 1:so + 1 + n], wcol(2), cvb[:, :n], ALU.mult, ALU.add, r=["pre", "pv", "cvb"], w=["cvb"])
                    sg_ = stg[bi % 2]; sgk = ("stg", bi % 2)
                    if kind == 2:
                        act(sg_[:, :n], cvb[:, :n], AF.Silu, r=["cvb"], w=[sgk])
                    else:
                        act(slb[:, :n], cvb[:, :n], AF.Silu, r=["cvb"], w=["slb"])
                        act(sqb[:, :n], slb[:, :n], AF.Square, r=["slb"], w=["sqb"])
                        bs = bank()
                        mm(ps[bs][:, :n], ones, sqb[:, :n], r=["sqb", "consts"], w=[psk(bs)])
                        act(rsb[:, :n], ps[bs][:, :n], AF.Sqrt, r=[psk(bs), "eps"], w=["rsb"], bias=eps_t[:])
                        recip(rsb[:, :n], rsb[:, :n], r=["rsb"], w=["rsb"])
                        stt(sg_[:, :n], slb[:, :n], (128.0 ** -0.5) if kind == 0 else 1.0, rsb[:, :n], ALU.mult, ALU.mult,
                            r=["slb", "rsb"], w=[sgk])
                    dma("pool", dnq[j, :, t0:t0 + n], sg_[:, :n], r=[sgk], w=[("dnq", j, bi)])
            (wab,), wk = wload([(wb_in[l, :, 2816:2832], 8, keys)])
            bab = bank()
            for tti in range(18):
                for k in range(8):
                    mm(ps[bab][:, tti * 16:(tti + 1) * 16], hT[:, k, tti * 128:(tti + 1) * 128], wab[:, k, :], start=(k == 0), stop=(k == 7),
                       r=[wk, "hT"], w=[psk(bab)])
            cp("dve", ab, ps[bab][:, :288].rearrange("p (t c) -> p t c", c=16), r=[psk(bab)], w=["ab"])
            bcl = bc[:, l * BC_N:(l + 1) * BC_N]
            tt("dve", tmp8, ab[:, :, 0:8], bcl[:, BC_DTB:BC_DTB + 8].unsqueeze(1).to_broadcast([128, 18, 8]), ALU.add, r=["ab", "bc"], w=["tmp8"])
            act(tmp8, tmp8, AF.Exp, r=["tmp8"], w=["tmp8"])
            act(tmp8, tmp8, AF.Ln, r=["tmp8", "consts"], w=["tmp8"], bias=ones[:, 0:1])
            act(acoef[:], bcl[:, BC_ALOG:BC_ALOG + 8], AF.Exp, r=["bc"], w=["acoef"])
            tsc("dve", acoef[:], acoef[:], -1.0, ALU.mult, r=["acoef"], w=["acoef"])
            tt("dve", la_tm[:], tmp8, acoef[:].unsqueeze(1).to_broadcast([128, 18, 8]), ALU.mult, r=["tmp8", "acoef"], w=["la_tm"])
            act(beta_tm[:], ab[:, :, 8:16], AF.Sigmoid, r=["ab"], w=["beta_tm"])
            fence()
            class BT:
                def __init__(self, i):
                    self.i = i
                    self.t = aview(i * 1024, [1024])
                    self.v3 = self.t.rearrange("p (c j) -> p c j", j=128)
                def h(self, d):
                    return self.t[:, d * 512:(d + 1) * 512]
                def c(self, c):
                    return self.v3[:, c, :]
                def k(self, d):
                    return ("b", self.i, d)
                def K(self):
                    return [("b", self.i, 0), ("b", self.i, 1)]
            Sst = BT(0); ktm = BT(4); vtm = BT(5); Dg = BT(6); Db = BT(7)
            Gm = BT(8); Gm2 = BT(9); E = BT(10); ET = BT(11); N_ = BT(12); NT_ = BT(13)
            Xs = [(BT(14), BT(15)), (BT(6), BT(7))]
            PTt = BT(16); aqkT = BT(17); wvu = BT(18); wkT = BT(19); qdT = BT(20); kdec = BT(21); Pm = BT(22); ost = BT(23)
            No = BT(8); NTo = BT(9); R1s = BT(10); R1ps = BT(11)
            qkv = aview(1024, [2, 12, 128])
            sm = aview(24576, [64])
            gt16 = sm[:, 0:16]; eg = sm[:, 16:24]; kdecs = sm[:, 24:32]; egl = sm[:, 32:40]; negbeta = sm[:, 40:48]
            bexp = sm[:, 48:56]; negg = sm[:, 56:64]
            betas = aview(24640, [8])
            hvc = lambda t2, d: t2[:, d * 512:(d + 1) * 512]
            bc3 = lambda small: small.unsqueeze(2).to_broadcast([128, 8, 128])
            b8 = lambda m: m.unsqueeze(1).to_broadcast([128, 8, 128])
            idb = b8(ident)
            offb = b8(C["offdiag"])
            A_, B_, C_, D_ = (0, 1), (2, 3), (4, 5), (6, 7)
            reg = lambda c: slice((c % 4) * 128, (c % 4 + 1) * 128)
            memset("pool", Sst.t, 0.0, w=Sst.K())
            for s_i in range(18):
                cd = (FORD[s_i], BORD[s_i])
                for d in range(2):
                    tok0 = cd[d] * 128
                    dma("sp", qkv[:, d], dnq[:, :, tok0:tok0 + 128].rearrange("j p t -> p j t"),
                        r=[("dnq", j, bi) for j in range(12) for bi in range(5)], w=[("qkv", d)])
                for d in range(2):
                    for h in range(4):
                        tr(ps[A_[d]][:, h * 128:(h + 1) * 128], qkv[:, d, 4 + h, :], ident, r=[("qkv", d), "consts"], w=[psk(A_[d])])
                    for h in range(4):
                        tr(ps[B_[d]][:, h * 128:(h + 1) * 128], qkv[:, d, 8 + h, :], ident, r=[("qkv", d), "consts"], w=[psk(B_[d])])
                    cp("act", ktm.h(d), ps[A_[d]][:, :], r=[psk(A_[d])], w=[ktm.k(d)])
                    cp("dve", vtm.h(d), ps[B_[d]][:, :], r=[psk(B_[d])], w=[vtm.k(d)])
                for d in range(2):
                    la_d = la_tm[:, cd[d], d * 4:(d + 1) * 4]
                    mm(ps[C_[0]][:, d * 4:(d + 1) * 4], C["tri_f"] if d == 0 else C["tri_b"], la_d, r=["la_tm", "consts"], w=[psk(C_[0])])
                    mm(ps[C_[0]][:, 8 + d * 4:8 + (d + 1) * 4], ones, la_d, r=["la_tm", "consts"], w=[psk(C_[0])])
                cp("dve", gt16, ps[C_[0]][:, 0:16], r=[psk(C_[0])], w=["gt16"])
                act(eg, gt16[:, 0:8], AF.Exp, r=["gt16"], w=["eg"])
                act(egl, gt16[:, 8:16], AF.Exp, r=["gt16"], w=["egl"])
                tsc("dve", negg, gt16[:, 0:8], -1.0, ALU.mult, r=["gt16"], w=["negg"])
                tt("dve", kdecs, gt16[:, 8:16], gt16[:, 0:8], ALU.subtract, r=["gt16"], w=["kdecs"])
                act(kdecs, kdecs, AF.Exp, r=["kdecs"], w=["kdecs"])
                cp("dve", betas[:, 0:4], beta_tm[:, cd[0], 0:4], r=["beta_tm"], w=["betas"])
                cp("dve", betas[:, 4:8], beta_tm[:, cd[1], 4:8], r=["beta_tm", "betas"], w=["betas"])
                tsc("dve", negbeta, betas, -1.0, ALU.mult, r=["betas"], w=["negbeta"])
                tt("dve", bexp, betas, eg, ALU.mult, r=["betas", "eg"], w=["bexp"])
                tt("dve", Dg.v3, idb, bc3(gt16[:, 0:8]), ALU.mult, r=["consts", "gt16"], w=Dg.K())
                tt("pool", Db.v3, idb, bc3(betas), ALU.mult, r=["consts", "betas"], w=Db.K())
                for d in range(2):
                    mm(ps[A_[d]][:, :], ones, Dg.h(d), r=[Dg.k(d), "consts"], w=[psk(A_[d])])
                    mm(ps[B_[d]][:, :], ones, Db.h(d), r=[Db.k(d), "consts"], w=[psk(B_[d])])
                for d in range(2):
                    tt("dve", Gm.h(d), ps[A_[d]][:, :], hvc(C["mask_s"], d), ALU.add, r=[psk(A_[d]), "consts"], w=[Gm.k(d)])
                    tt("dve", Gm2.h(d), ps[A_[d]][:, :], hvc(C["mask_t"], d), ALU.subtract, r=[psk(A_[d]), "consts"], w=[Gm2.k(d)])
                for c in range(8):
                    d = c // 4
                    act(E.c(c), Gm.c(c), AF.Exp, r=[Gm.k(d), "gt16"], w=[E.k(d)], scale=-1.0, bias=gt16[:, c:c + 1])
                for c in range(8):
                    d = c // 4
                    act(ET.c(c), Gm2.c(c), AF.Exp, r=[Gm2.k(d), "negg"], w=[ET.k(d)], scale=1.0, bias=negg[:, c:c + 1])
                for d in range(2):
                    for h in range(4):
                        mm(ps[C_[d]][:, h * 128:(h + 1) * 128], qkv[:, d, 4 + h, :], qkv[:, d, 4 + h, :], r=[("qkv", d)], w=[psk(C_[d])])
                    for h in range(4):
                        mm(ps[D_[d]][:, h * 128:(h + 1) * 128], qkv[:, d, 4 + h, :], qkv[:, d, h, :], r=[("qkv", d)], w=[psk(D_[d])])
                for d in range(2):
                    tt("dve", aqkT.h(d), ps[D_[d]][:, :], ET.h(d), ALU.mult, r=[psk(D_[d]), ET.k(d)], w=[aqkT.k(d)])
                tt("pool", E.v3, E.v3, offb, ALU.mult, r=E.K() + ["consts"], w=E.K())
                tt("pool", E.v3, E.v3, bc3(negbeta), ALU.mult, r=E.K() + ["negbeta"], w=E.K())
                tt("pool", ET.v3, ET.v3, offb, ALU.mult, r=ET.K() + ["consts"], w=ET.K())
                for d in range(2):
                    tt("dve", N_.h(d), ps[C_[d]][:, :], E.h(d), ALU.mult, r=[psk(C_[d]), E.k(d)], w=[N_.k(d)])
                    tt("dve", ET.h(d), ps[B_[d]][:, :], ET.h(d), ALU.mult, r=[psk(B_[d]), ET.k(d)], w=[ET.k(d)])
                    stt(NT_.h(d), ps[C_[d]][:, :], -1.0, ET.h(d), ALU.mult, ALU.mult, r=[psk(C_[d]), ET.k(d)], w=[NT_.k(d)])
                X, XT = Xs[0]
                tt("pool", X.v3, N_.v3, b8(C["bd8"]), ALU.mult, r=N_.K() + ["consts"], w=X.K())
                tt("pool", XT.v3, NT_.v3, b8(C["bd8"]), ALU.mult, r=NT_.K() + ["consts"], w=XT.K())
                tt("pool", Pm.v3, X.v3, idb, ALU.add, r=X.K() + ["consts"], w=Pm.K())
                tt("pool", PTt.v3, XT.v3, idb, ALU.add, r=XT.K() + ["consts"], w=PTt.K())
                cur = 0
                for it in range(2):
                    X, XT = Xs[cur]
                    Xn, XTn = Xs[1 - cur]
                    for c in range(8):
                        d = c // 4
                        mm(ps[A_[d]][:, reg(c)], XT.c(c), X.c(c), r=[X.k(d), XT.k(d)], w=[psk(A_[d])])
                    for c in range(8):
                        d = c // 4
                        mm(ps[B_[d]][:, reg(c)], X.c(c), XT.c(c), r=[X.k(d), XT.k(d)], w=[psk(B_[d])])
                    for d in range(2):
                        cp("act", Xn.h(d), ps[A_[d]][:, :], r=[psk(A_[d])], w=[Xn.k(d)])
                        cp("dve", XTn.h(d), ps[B_[d]][:, :], r=[psk(B_[d])], w=[XTn.k(d)])
                    for c in range(8):
                        d = c // 4
                        mm(ps[C_[d]][:, reg(c)], XTn.c(c), Pm.c(c), r=[XTn.k(d), Pm.k(d)], w=[psk(C_[d])])
                    for c in range(8):
                        d = c // 4
                        mm(ps[D_[d]][:, reg(c)], Xn.c(c), PTt.c(c), r=[Xn.k(d), PTt.k(d)], w=[psk(D_[d])])
                    for d in range(2):
                        tt("dve", Pm.h(d), ps[C_[d]][:, :], Pm.h(d), ALU.add, r=[psk(C_[d]), Pm.k(d)], w=[Pm.k(d)])
                        tt("dve", PTt.h(d), ps[D_[d]][:, :], PTt.h(d), ALU.add, r=[psk(D_[d]), PTt.k(d)], w=[PTt.k(d)])
                    cur = 1 - cur
                for lv, offn in enumerate(["off8", "off16", "off32", "off64"]):
                    lastlv = (lv == 3)
                    tt("pool", No.v3, N_.v3, b8(C[offn]), ALU.mult, r=N_.K() + ["consts"], w=No.K())
                    if not lastlv:
                        tt("pool", NTo.v3, NT_.v3, b8(C[offn]), ALU.mult, r=NT_.K() + ["consts"], w=NTo.K())
                        for c in range(8):
                            d = c // 4
                            mm(ps[A_[d]][:, reg(c)], NTo.c(c), Pm.c(c), r=[NTo.k(d), Pm.k(d)], w=[psk(A_[d])])
                    for c in range(8):
                        d = c // 4
                        mm(ps[B_[d]][:, reg(c)], No.c(c), PTt.c(c), r=[No.k(d), PTt.k(d)], w=[psk(B_[d])])
                    for d in range(2):
                        if not lastlv:
                            cp("act", R1s.h(d), ps[A_[d]][:, :], r=[psk(A_[d])], w=[R1s.k(d)])
                        cp("dve", R1ps.h(d), ps[B_[d]][:, :], r=[psk(B_[d])], w=[R1ps.k(d)])
                    if not lastlv:
                        for c in range(8):
                            d = c // 4
                            mm(ps[C_[d]][:, reg(c)], PTt.c(c), R1s.c(c), r=[PTt.k(d), R1s.k(d)], w=[psk(C_[d])])
                    for c in range(8):
                        d = c // 4
                        mm(ps[D_[d]][:, reg(c)], Pm.c(c), R1ps.c(c), r=[Pm.k(d), R1ps.k(d)], w=[psk(D_[d])])
                    for d in range(2):
                        if not lastlv:
                            tt("dve", Pm.h(d), ps[C_[d]][:, :], Pm.h(d), ALU.add, r=[psk(C_[d]), Pm.k(d)], w=[Pm.k(d)])
                        tt("dve", PTt.h(d), ps[D_[d]][:, :], PTt.h(d), ALU.add, r=[psk(D_[d]), PTt.k(d)], w=[PTt.k(d)])
                tt("pool", kdec.v3, ktm.v3, bc3(kdecs), ALU.mult, r=ktm.K() + ["kdecs"], w=kdec.K())
                tt("pool", ktm.v3, ktm.v3, bc3(bexp), ALU.mult, r=ktm.K() + ["bexp"], w=ktm.K())
                tt("pool", vtm.v3, vtm.v3, bc3(betas), ALU.mult, r=vtm.K() + ["betas"], w=vtm.K())
                for c in range(8):
                    d = c // 4
                    mm(ps[A_[d]][:, reg(c)], PTt.c(c), vtm.c(c), r=[PTt.k(d), vtm.k(d)], w=[psk(A_[d])])
                    mm(ps[B_[d]][:, reg(c)], ktm.c(c), PTt.c(c), r=[PTt.k(d), ktm.k(d)], w=[psk(B_[d])])
                for d in range(2):
                    cp("dve", wvu.h(d), ps[A_[d]][:, :], r=[psk(A_[d])], w=[wvu.k(d)])
                    cp("act", wkT.h(d), ps[B_[d]][:, :], r=[psk(B_[d])], w=[wkT.k(d)])
                tt("dve", Dg.v3, idb, bc3(eg), ALU.mult, r=["consts", "eg"], w=Dg.K())
                for d in range(2):
                    mm(ps[D_[d]][:, :], ones, Dg.h(d), r=[Dg.k(d), "consts"], w=[psk(D_[d])])
                    tt("dve", qdT.h(d), ps[D_[d]][:, :], qkv[:, d, 0:4, :], ALU.mult, r=[psk(D_[d]), ("qkv", d)], w=[qdT.k(d)])
                for c in range(8):
                    d = c // 4
                    mm(ps[C_[d]][:, reg(c)], wkT.c(c), Sst.c(c), r=[wkT.k(d), Sst.k(d)], w=[psk(C_[d])])
                for d in range(2):
                    tt("dve", wvu.h(d), wvu.h(d), ps[C_[d]][:, :], ALU.subtract, r=[psk(C_[d]), wvu.k(d)], w=[wvu.k(d)])
                for c in range(8):
                    d = c // 4
                    mm(ps[A_[d]][:, reg(c)], Sst.c(c), qdT.c(c), start=True, stop=False, r=[Sst.k(d), qdT.k(d)], w=[psk(A_[d])])
                    mm(ps[A_[d]][:, reg(c)], wvu.c(c), aqkT.c(c), start=False, stop=True, r=[wvu.k(d), aqkT.k(d)], w=[psk(A_[d])])
                    mm(ps[B_[d]][:, reg(c)], kdec.c(c), wvu.c(c), r=[kdec.k(d), wvu.k(d)], w=[psk(B_[d])])
                tt("pool", Sst.v3, Sst.v3, bc3(egl), ALU.mult, r=Sst.K() + ["egl"], w=Sst.K())
                for d in range(2):
                    tt("dve", Sst.h(d), Sst.h(d), ps[B_[d]][:, :], ALU.add, r=[psk(B_[d]), Sst.k(d)], w=[Sst.k(d)])
                    cp("act", ost.h(d), ps[A_[d]][:, :], r=[psk(A_[d])], w=[ost.k(d)])
                    tok0 = cd[d] * 128
                    dma("pool", dno[d, :, :, tok0:tok0 + 128].rearrange("h p t -> p h t"), ost.v3[:, d * 4:(d + 1) * 4, :],
                        r=[ost.k(d)], wa=[("dno", d)])
            fence()
            of_ = aview(0, [512]); ob_ = aview(512, [512]); sq3 = aview(1024, [512]); rs3 = aview(1536, [512]); sz = aview(2048, [512])
            gcol = pv[:, l * PV_N + PV_DNG:l * PV_N + PV_DNG + 1]
            for h in range(4):
                (wz,), wkz = wload([(wb_in[l, :, 2304 + h * 128:2304 + (h + 1) * 128], 8, keys)])
                for (t0, n) in NBLK:
                    dma("sp", of_[:, :n], dno[0, h, :, t0:t0 + n], r=[("dno", 0)], w=["of_"])
                    dma("sp", ob_[:, :n], dno[1, h, :, t0:t0 + n], r=[("dno", 1)], w=["ob_"])
                    tt("pool", of_[:, :n], of_[:, :n], ob_[:, :n], ALU.add, r=["of_", "ob_"], w=["of_"])
                    act(sq3[:, :n], of_[:, :n], AF.Square, r=["of_"], w=["sq3"])
                    bs = bank()
                    mm(ps[bs][:, :n], C["onesm128"], sq3[:, :n], r=["sq3", "consts"], w=[psk(bs)])
                    act(rs3[:, :n], ps[bs][:, :n], AF.Sqrt, r=[psk(bs), "eps"], w=["rs3"], bias=eps_t[:])
                    recip(rs3[:, :n], rs3[:, :n], r=["rs3"], w=["rs3"])
                    stt(of_[:, :n], of_[:, :n], gcol, rs3[:, :n], ALU.mult, ALU.mult, r=["of_", "rs3", "pv"], w=["of_"])
                    bz = bank()
                    proj(bz, 128, wz, 0, t0, n, wkz)
                    act(sz[:, :n], ps[bz][:, :n], AF.Silu, r=[psk(bz)], w=["sz"])
                    yb_, yk = ybuf()
                    tt("pool", yb_[:, :n], of_[:, :n], sz[:, :n], ALU.mult, r=["of_", "sz"], w=[yk])
                    dma("pool", ybr[1, h * 128:(h + 1) * 128, t0:t0 + n], yb_[:, :n], r=[yk], wa=[("ybr", 1)])

        ybufs = [sb("ybuf%d" % i, [128, 512], BF16) for i in range(3)]

        def ybuf():
            i = state.get("yb", 0)
            state["yb"] = (i + 1) % 3
            return ybufs[i], "ybuf%d" % i


        def phase_c(l, b, last):
            act_br = [i for i, nm in enumerate(("att", "dn", "ret", "sc")) if nm in enabled]
            for bi, (t0, n) in enumerate(NBLK):
                col = NB if bi == 0 else b
                for i in act_br:
                    dma("pool", ybl[:, i, :, :n], ybr[i, :, t0:t0 + n].rearrange("(c p) t -> p c t", p=128),
                        r=[("ybr", i)], w=[("ybl", i)])
                for m in range(8):
                    for ii, i in enumerate(act_br):
                        (wg, wbr), wk = wload([(wb_in[l, :, 5904 + i * 1024 + m * 128: 5904 + i * 1024 + (m + 1) * 128], 8, win_keys(l)),
                                               (wb_branch[l, i, :, m * 128:(m + 1) * 128], 4, [("wb_br", l, i, k) for k in range(4)])])
                        bg = bank(); by = bank()
                        proj(bg, 128, wg, 0, t0, n, wk)
                        for k in range(4):
                            mm(ps[by][:, :n], wbr[:, k, :], ybl[:, i, k, :n], start=(k == 0), stop=(k == 3), r=[wk, ("ybl", i)], w=[psk(by)])
                        sg, sk = t5()
                        act(sg[:, :n], ps[bg][:, :n], AF.Sigmoid, r=[psk(bg)], w=[sk])
                        lastb = (ii == len(act_br) - 1)
                        dst, dk = (ubf, ("ubf", m)) if lastb else (wkA, ("wkA", m))
                        if ii == 0:
                            tt("dve", dst[:, m, :n], ps[by][:, :n], sg[:, :n], ALU.mult, r=[psk(by), sk], w=[dk])
                        else:
                            tt("dve", sg[:, :n], ps[by][:, :n], sg[:, :n], ALU.mult, r=[psk(by), sk], w=[sk])
                            tt("pool", dst[:, m, :n], wkA[:, m, :n], sg[:, :n], ALU.add, r=[("wkA", m), sk], w=[dk])
                for mg in range(4):
                    (wo,), wk = wload([(wb_out[l, :, mg * 256:(mg + 1) * 256], 8, [("wb_out", l, k) for k in range(8)])])
                    for m2 in range(2):
                        m = mg * 2 + m2
                        bo = bank()
                        for k in range(8):
                            mm(ps[bo][:, :n], wo[:, k, m2 * 128:(m2 + 1) * 128], ubf[:, k, :n], start=(k == 0), stop=(k == 7),
                               r=[wk] + [("ubf", kk) for kk in range(8)], w=[psk(bo)])
                        cp("act", wkA[:, m, :n], ps[bo][:, :n], r=[psk(bo)], w=[("wkA", m)])
                dma("sp", xblk[:, :, :n], xres_v[:, :, t0:t0 + n], r=[("xres", bi)], w=["xblk"])
                residual_block(l, col, 0, wkA[:, :, :n], n, [("wkA", m) for m in range(8)])
                norm_mod_block(l, col, 1, xblk[:, :, :n], t0, n, ["xblk"])
                for pg in range(16):
                    (w1,), wk = wload([(wb_m1[l, :, pg * 256:(pg + 1) * 256], 8, [("wb_m1", l, k) for k in range(8)])])
                    for m2 in range(2):
                        m = pg * 2 + m2
                        bh = bank()
                        proj(bh, 128, w1, m2 * 128, t0, n, wk)
                        rr, rk = t5()
                        act(rr[:, :n], ps[bh][:, :n], AF.Relu, r=[psk(bh)], w=[rk])
                        tt("pool", hid[:, m, :n], rr[:, :n], rr[:, :n], ALU.mult, r=[rk], w=[("hid", m)])
                for mg in range(4):
                    b0 = bank(); b1 = bank()
                    bb = (b0, b1)
                    for kg in range(4):
                        (w2,), wk = wload([(wb_m2[l, kg * 1024:(kg + 1) * 1024, mg * 256:(mg + 1) * 256], 8,
                                            [("wb_m2", l, kg * 8 + k) for k in range(8)])])
                        for m2 in range(2):
                            for k in range(8):
                                mm(ps[bb[m2]][:, :n], w2[:, k, m2 * 128:(m2 + 1) * 128], hid[:, kg * 8 + k, :n],
                                   start=(kg == 0 and k == 0), stop=(kg == 3 and k == 7),
                                   r=[wk, ("hid", kg * 8 + k)], w=[psk(bb[m2])])
                    for m2 in range(2):
                        cp("act", wkA[:, mg * 2 + m2, :n], ps[bb[m2]][:, :n], r=[psk(bb[m2])], w=[("wkA", mg * 2 + m2)])
                residual_block(l, col, 1, wkA[:, :, :n], n, [("wkA", m) for m in range(8)])
                dma("sp", xres_v[:, :, t0:t0 + n], xblk[:, :, :n], r=["xblk"], w=[("xres", bi)])

        for b in range(NB):
            for bi, (t0, n) in enumerate(NBLK):
                dma("sp", xres[:, t0:t0 + n], xT_in[b, :, t0:t0 + n], w=[("xres", bi)])
            for l in range(L):
                if os.environ.get("RESET_ROT", "0") == "1":
                    for kk_ in ("bank", "ws", "t5", "yb", "od", "odr", "sbk"):
                        state[kk_] = 0
                phase_a(l, b)
                fence()
                if "att" in enabled:
                    mixer_att(l, b)
                    fence()
                if "dn" in enabled:
                    mixer_dn(l, b)
                    fence()
                if "ret" in enabled:
                    mixer_ret(l, b)
                    fence()
                if "sc" in enabled:
                    mixer_sc(l, b)
                    fence()
                phase_c(l, b, l == L - 1)
                if "xl" in dbg_out and b == 0:
                    for bi, (t0, n) in enumerate(NBLK):
                        out_events.append(dma("sp", dbg_out["xl"][l, :, t0:t0 + n], xres[:, t0:t0 + n], r=[("xres", bi)]))
                fence()
            for bi, (t0, n) in enumerate(NBLK[1:]):
                out_events.append(dma("sp", outT[b, :, t0 - TC:t0 - TC + n], xres[:, t0:t0 + n], r=[("xres", bi + 1)]))
        P.finish(out_events)
        P.emit(st)
    return nc, P


def make_in_maps(inp, NB, ncores, L=4):
    pv, bc = _host_params(inp, L)
    cst = _consts()
    const_arr = np.ascontiguousarray(np.concatenate([cst[k] for k in CONST_ORDER], axis=1))
    cosT, sinT = _rope_tables()
    maps = []
    for core in range(ncores):
        bs = slice(core * NB, (core + 1) * NB)
        xcat = np.concatenate([inp["ctx"][bs], inp["x"][bs]], axis=1)
        xT = np.ascontiguousarray(xcat.transpose(0, 2, 1))
        call = np.concatenate([inp["c"][bs], inp["c_ctx"][None, :]], axis=0)
        cT = np.ascontiguousarray(call.reshape(NB + 1, 8, 128).transpose(2, 1, 0))
        maps.append({"xT_in": xT, "cT": cT, "pv": pv, "bc": bc, "consts": const_arr, "cosT": cosT, "sinT": sinT,
                     "w_mod": inp["w_mod"][:L], "w_in": inp["w_in"][:L], "w_branch": inp["w_branch"][:L],
                     "w_out": inp["w_out"][:L], "w_mlp_in": inp["w_mlp_in"][:L], "w_mlp_out": inp["w_mlp_out"][:L]})
    return maps


_CACHE = {}


def kernel(**inputs):
    inp = {k: np.asarray(v) for k, v in inputs.items()}
    B = inp["x"].shape[0]
    ncores = 8
    NB = B // ncores
    if "nc" not in _CACHE:
        _CACHE["nc"] = build(NB)[0]
    nc = _CACHE["nc"]
    maps = make_in_maps(inp, NB, ncores)
    res = run_bass_kernel_spmd(nc, maps, core_ids=list(range(ncores)))
    out = np.concatenate([r["outT"] for r in res.results], axis=0)
    return np.ascontiguousarray(out.transpose(0, 2, 1)).astype(np.float32)
```

```python
import math
import os
from contextlib import ExitStack
import numpy as np
import concourse.bass as bass
import concourse.mybir as mybir
from concourse.bass_utils import run_bass_kernel_spmd

F32 = mybir.dt.float32
BF16 = mybir.dt.bfloat16
AF = mybir.ActivationFunctionType
ALU = mybir.AluOpType

ENGS = ("pe", "act", "dve", "pool", "sp")
NDMA_SEM = 12

D = 1024
T = 2304
TC = 256
TL = 2048
NBLK = [(0, 256), (256, 512), (768, 512), (1280, 512), (1792, 512)]
N_IN = 10000
EPS = 1e-6
BIG = 1.0e5


class Prog:
    def __init__(self, nc):
        self.nc = nc
        self.ops = {e: [] for e in ENGS}
        self.cnt = {e: 0 for e in ENGS}
        self.clock = {e: {} for e in ENGS}
        self.snap = {}
        self.lastw = {}
        self.readers = {}
        self.dma_rr = {e: 0 for e in ENGS}
        self.dma_cnt = {}
        self.n_wait = 0
        self.fence_ev = None
        self.fence_keep = None

    def _need(self, eng, ev, need):
        k, v = ev
        if eng == "pe" and k == "pe":
            return
        if self.clock[eng].get(k, 0) >= v:
            return
        if need.get(k, 0) < v:
            need[k] = v

    def op(self, eng, fn, r=(), w=(), wa=(), dma=False):
        need = {}
        for b in r:
            for ev in self.lastw.get(b, ()):
                self._need(eng, ev, need)
        for b in w:
            for ev in self.lastw.get(b, ()):
                self._need(eng, ev, need)
            for ev in self.readers.get(b, ()):
                self._need(eng, ev, need)
        for b in wa:
            for ev in self.readers.get(b, ()):
                self._need(eng, ev, need)
        if self.fence_ev is not None:
            if not all(self.fence_keep(k) for k in list(r) + list(w) + list(wa)):
                self._need(eng, self.fence_ev, need)
        if dma:
            i = self.dma_rr[eng]
            self.dma_rr[eng] = (i + 1) % NDMA_SEM
            key = ("dma", eng, i)
            prev = self.dma_cnt.get(key, 0)
            if prev:
                self._need(eng, (key, prev), need)
            val = prev + 16
            self.dma_cnt[key] = val
            inc = 16
        else:
            key = eng
            self.cnt[eng] += 1
            val = self.cnt[eng]
            inc = 1
        waits = list(need.items())
        if waits:
            ck = dict(self.clock[eng])
            for k, v in waits:
                if ck.get(k, 0) < v:
                    ck[k] = v
                for k2, v2 in self.snap.get((k, v), {}).items():
                    if ck.get(k2, 0) < v2:
                        ck[k2] = v2
            self.clock[eng] = ck
            self.n_wait += len(waits)
        ev = (key, val)
        self.snap[ev] = self.clock[eng]
        self.ops[eng].append((waits, fn, key, inc))
        for b in w:
            self.lastw[b] = [ev]
            self.readers[b] = []
        for b in wa:
            self.lastw.setdefault(b, []).append(ev)
        for b in r:
            self.readers.setdefault(b, []).append(ev)
        return ev

    def barrier(self, fn, eng="dve", keep=lambda k: False):
        keys = [k for k in set(self.lastw) | set(self.readers) if not keep(k)]
        ev = self.op(eng, fn, (), keys)
        self.fence_ev = ev
        self.fence_keep = keep
        return ev

    def finish(self, out_events, eng="sp"):
        need = {}
        for ev in out_events:
            self._need(eng, ev, need)
        self.ops[eng].append((list(need.items()), None, None, 0))

    def emit(self, stack):
        nc = self.nc
        keys = []
        for e in ENGS:
            for waits, fn, key, inc in self.ops[e]:
                if key is not None and key not in keys:
                    keys.append(key)
        semh = {}
        for k in keys:
            nm = "s_" + ("_".join(map(str, k)) if isinstance(k, tuple) else k)
            semh[k] = stack.enter_context(nc.semaphore(nm))
        block = stack.enter_context(nc.Block())
        engmap = {"pe": block.tensor, "act": block.scalar, "dve": block.vector,
                  "pool": block.gpsimd, "sp": block.sync}
        for e in ENGS:
            def body(eng, ops=self.ops[e]):
                for waits, fn, key, inc in ops:
                    for k, v in waits:
                        eng.wait_ge(semh[k], v)
                    if fn is not None:
                        fn(eng).then_inc(semh[key], inc)
            engmap[e](body)


def _consts():
    c = {}
    p = np.arange(128)[:, None]
    j = np.arange(128)[None, :]
    c["ident"] = (p == j).astype(np.float32)
    c["ones"] = np.ones((128, 128), np.float32)
    blk = ((p // 64) == (j // 64)).astype(np.float32)
    c["blk64m"] = blk / 64.0
    c["onesm128"] = np.ones((128, 128), np.float32) / 128.0
    R = np.zeros((128, 128), np.float32)
    for m in range(128):
        if m % 64 < 32:
            R[m + 32, m] = -1.0
        else:
            R[m - 32, m] = 1.0
    c["rrot"] = R
    c["tri_f"] = (p <= j).astype(np.float32)
    c["tri_b"] = (p >= j).astype(np.float32)
    c["offdiag"] = (p != j).astype(np.float32)
    mf = (j > p).astype(np.float32) * BIG
    mb = (j < p).astype(np.float32) * BIG
    c["mask_s"] = np.concatenate([np.tile(mf[:, None, :], (1, 4, 1)), np.tile(mb[:, None, :], (1, 4, 1))], 1).reshape(128, 1024)
    c["mask_t"] = np.concatenate([np.tile(mb[:, None, :], (1, 4, 1)), np.tile(mf[:, None, :], (1, 4, 1))], 1).reshape(128, 1024)
    jj = np.arange(512)[None, :]
    c["dbase"] = (jj - p).astype(np.float32)
    def bd(m):
        return ((p // m) == (j // m)).astype(np.float32)
    c["bd8"] = bd(8)
    c["off8"] = bd(16) - bd(8)
    c["off16"] = bd(32) - bd(16)
    c["off32"] = bd(64) - bd(32)
    c["off64"] = bd(128) - bd(64)
    e = np.zeros((128, 128), np.float32); e[:, :64] = 1
    c["ones_e"] = e
    c["ones_o"] = 1 - e
    return c


CONST_ORDER = ["ident", "ones", "blk64m", "onesm128", "rrot", "tri_f", "tri_b", "offdiag", "mask_s", "mask_t",
               "dbase", "ones_e", "ones_o", "bd8", "off8", "off16", "off32", "off64"]


def _rope_tables():
    rows = TL // 64
    r, col = np.meshgrid(np.arange(rows), np.arange(64), indexing="ij")
    quarter = 16
    inv_freq = (10000.0 ** (-np.arange(quarter, dtype=np.float32) / quarter)).astype(np.float32)
    ang = np.concatenate([r.reshape(-1, 1).astype(np.float32) * inv_freq,
                          col.reshape(-1, 1).astype(np.float32) * inv_freq], axis=-1)
    cos = np.cos(ang).astype(np.float32).T
    sin = np.sin(ang).astype(np.float32).T
    cosT = np.concatenate([cos, cos, cos, cos], 0)
    sinT = np.concatenate([sin, sin, sin, sin], 0)
    return np.ascontiguousarray(cosT), np.ascontiguousarray(sinT)


PV_G = 0
PV_DNC = 32
PV_SCC = 68
PV_QG = 80
PV_KG = 81
PV_DNG = 82
PV_BM = 83
PV_N = 131
BC_ALOG = 0
BC_DTB = 8
BC_RET = 16
BC_N = 24


def _host_params(inp, L):
    pv = np.zeros((128, L * PV_N), np.float32)
    bc = np.zeros((128, L * BC_N), np.float32)
    for l in range(L):
        o = l * PV_N
        pv[:, o + PV_G:o + PV_G + 32] = inp["g_norm"][l].reshape(32, 128).T
        pv[:, o + PV_DNC:o + PV_DNC + 36] = inp["dn_conv"][l].reshape(36, 128).T
        pv[:, o + PV_SCC:o + PV_SCC + 12] = inp["sc_conv"][l].reshape(12, 128).T
        pv[:, o + PV_QG] = np.concatenate([inp["att_q_gain"][l]] * 2)
        pv[:, o + PV_KG] = np.concatenate([inp["att_k_gain"][l]] * 2)
        pv[:, o + PV_DNG] = inp["dn_norm_gain"][l]
        pv[:, o + PV_BM:o + PV_BM + 48] = inp["b_mod"][l].reshape(48, 128).T
        ob = l * BC_N
        bc[:, ob + BC_ALOG:ob + BC_ALOG + 8] = inp["dn_a_log"][l].reshape(1, 8)
        bc[:, ob + BC_DTB:ob + BC_DTB + 8] = inp["dn_dt_bias"][l].reshape(1, 8)
        bc[:, ob + BC_RET:ob + BC_RET + 8] = inp["ret_decay"][l].reshape(1, 8)
    return pv, bc


def build(NB, L=4, enabled=("att", "dn", "ret", "sc"), dbg=()):
    nc = bass.Bass("TRN2", target_bir_lowering=False)
    P = Prog(nc)
    NCOL = NB + 1
    dram = lambda name, shape, dt=F32, kind="ExternalInput": nc.dram_tensor(name, shape, dt, kind=kind).ap()
    xT_in = dram("xT_in", [NB, D, T])
    cT_in = dram("cT", [128, 8, NCOL])
    pv_in = dram("pv", [128, L * PV_N])
    bc_in = dram("bc", [128, L * BC_N])
    cst = _consts()
    cwid = {k: cst[k].shape[1] for k in CONST_ORDER}
    coff = {}
    o = 0
    for k in CONST_ORDER:
        coff[k] = o
        o += cwid[k]
    NCONST = o
    const_in = dram("consts", [128, NCONST])
    cos_in = dram("cosT", [128, TL])
    sin_in = dram("sinT", [128, TL])
    w_mod = dram("w_mod", [L, D, 6 * D])
    w_in = dram("w_in", [L, D, N_IN])
    w_branch = dram("w_branch", [L, 4, 512, D])
    w_out = dram("w_out", [L, D, D])
    w_m1 = dram("w_mlp_in", [L, D, 4 * D])
    w_m2 = dram("w_mlp_out", [L, 4 * D, D])
    outT = dram("outT", [NB, D, TL], kind="ExternalOutput")
    dbg_out = {}
    for name, shape in dbg:
        dbg_out[name] = dram("dbg_" + name, shape, kind="ExternalOutput")
    wb_in = dram("wb_in", [L, D, N_IN], BF16, kind="Internal")
    wb_mg = dram("wb_mg", [L, 4, 8, 128, 12, 128], BF16, kind="Internal")
    wb_out = dram("wb_out", [L, 4, 128, 8, 256], BF16, kind="Internal")
    wb_m1 = dram("wb_m1", [L, 16, 128, 8, 256], BF16, kind="Internal")
    wb_m2 = dram("wb_m2", [L, 4, 4, 128, 8, 256], BF16, kind="Internal")
    xres = dram("xres", [D, T], F32, kind="Internal")
    ybr = dram("ybr", [4, 512, T], BF16, kind="Internal")
    dnq = dram("dnq", [12, 128, T], F32, kind="Internal")
    dno = dram("dno", [2, 4, 128, T], F32, kind="Internal")

    out_events = []
    with ExitStack() as st:
        def sb(name, shape, dt=F32):
            return st.enter_context(nc.sbuf_tensor(name, shape, dt))

        hT = sb("hT", [128, 8, T], BF16)
        consts = sb("consts_sb", [128, NCONST])
        C = {k: consts[:, coff[k]:coff[k] + cwid[k]] for k in CONST_ORDER}
        pv = sb("pv_sb", [128, L * PV_N])
        bc = sb("bc_sb", [128, L * BC_N])
        modT = sb("modT", [128, L, 48, NCOL])
        modA = sb("modA", [128, L, 4, 8, NCOL])
        cTs = sb("cTs", [128, 8, NCOL])
        eps_t = sb("eps_t", [128, 1])
        ones_bf = sb("ones_bf", [128, 128], BF16)
        ones_eo = sb("ones_eo", [128, 2, 128], BF16)
        NW = 4
        wslot = [sb("wslot%d" % i, [128, 2048], BF16) for i in range(NW)]
        AR_WORDS = 26624
        arena = sb("arena", [128, AR_WORDS])
        dummy = sb("fdummy", [128, 8])

        def aview(off_w, shape, dt=F32):
            n = int(np.prod(shape))
            words = n if dt == F32 else (n + 1) // 2
            assert off_w + words <= AR_WORDS, (off_w, words)
            v = arena[:, off_w:off_w + words]
            if dt != F32:
                v = v.bitcast(dt)
            if len(shape) == 2:
                v = v.rearrange("p (a b) -> p a b", b=shape[1])
            elif len(shape) == 3:
                v = v.rearrange("p (a b c) -> p a b c", b=shape[1], c=shape[2])
            elif len(shape) == 4:
                v = v.rearrange("p (a b c d) -> p a b c d", b=shape[1], c=shape[2], d=shape[3])
            return v

        def fence():
            keep = lambda k: k in ("consts", "cos", "sin", "pv", "bc", "eps", "ones_bf", "ones_eo", "hT") or (
                isinstance(k, tuple) and k[0] in ("ws", "wb_in", "wb_mg", "wb_out", "wb_m1", "wb_m2", "modT", "modA", "xres", "ybr"))
            P.barrier(lambda e: e.memset(dummy[:], 0.0), "dve", keep)
            for kk_ in ("bank", "t5", "yb", "od", "odr", "sbk"):
                state[kk_] = 0

        wst = aview(0, [2, 8, 512])
        xblk = aview(0, [8, 512])
        wkA = aview(4096, [8, 512])
        wkB = aview(8192, [8, 512])
        ybl = aview(12288, [4, 4, 512], BF16)
        ubf = aview(16384, [8, 512], BF16)
        hid = aview(18432, [32, 512], BF16)
        rsA = sb("rsA", [128, 512])
        t512 = [sb("t512_%d" % i, [128, 512]) for i in range(6)]
        ps = [st.enter_context(nc.psum_tensor("ps%d" % i, [128, 512], F32)) for i in range(8)]
        state = {"bank": 0, "ws": 0, "t5": 0}

        def bank():
            b = state["bank"]
            state["bank"] = (b + 1) % 8
            return b

        def psk(b):
            return "ps%d" % b

        def t5():
            i = state["t5"]
            state["t5"] = (i + 1) % 6
            return t512[i], "t512_%d" % i

        def mm(out, lhsT, rhs, start=True, stop=True, r=(), w=()):
            return P.op("pe", lambda e: e.matmul(out, lhsT=lhsT, rhs=rhs, start=start, stop=stop), r, w)

        F32R = mybir.dt.float32r

        def mmr(out, lhsT, rhs, start=True, stop=True, r=(), w=()):
            if True:
                return mm(out, lhsT, rhs, start, stop, r, w)
            l2 = lhsT.bitcast(F32R); r2 = rhs.bitcast(F32R)
            return P.op("pe", lambda e: e.matmul(out, lhsT=l2, rhs=r2, start=start, stop=stop), r, w)

        def tr(out, in_, ident, r=(), w=()):
            return P.op("pe", lambda e: e.transpose(out=out, in_=in_, identity=ident), r, w)

        def act(out, in_, func, r=(), w=(), scale=1.0, bias=None, wa=()):
            if bias is None:
                return P.op("act", lambda e: e.activation(out=out, in_=in_, func=func, scale=scale), r, w, wa)
            return P.op("act", lambda e: e.activation(out=out, in_=in_, func=func, scale=scale, bias=bias), r, w, wa)

        def tt(eng, out, in0, in1, op, r=(), w=(), wa=()):
            return P.op(eng, lambda e: e.tensor_tensor(out=out, in0=in0, in1=in1, op=op), r, w, wa)

        def tsc(eng, out, in0, s1, op0, s2=None, op1=None, r=(), w=(), wa=()):
            if op1 is None:
                return P.op(eng, lambda e: e.tensor_scalar(out=out, in0=in0, scalar1=s1, scalar2=None, op0=op0), r, w, wa)
            return P.op(eng, lambda e: e.tensor_scalar(out=out, in0=in0, scalar1=s1, scalar2=s2, op0=op0, op1=op1), r, w, wa)

        def stt(out, in0, scalar, in1, op0, op1, r=(), w=(), wa=()):
            return P.op("dve", lambda e: e.scalar_tensor_tensor(out=out, in0=in0, scalar=scalar, in1=in1, op0=op0, op1=op1), r, w, wa)

        def cp(eng, out, in_, r=(), w=(), wa=()):
            if eng == "act":
                return act(out, in_, AF.Copy, r, w, wa=wa)
            return P.op(eng, lambda e: e.tensor_copy(out=out, in_=in_), r, w, wa)

        def recip(out, in_, r=(), w=()):
            return P.op("dve", lambda e: e.reciprocal(out=out, in_=in_), r, w)

        def dma(q, out, in_, r=(), w=(), wa=()):
            return P.op(q, lambda e: e.dma_start(out=out, in_=in_), r, w, wa, dma=True)

        def memset(eng, ap, val, w=()):
            return P.op(eng, lambda e: e.memset(ap, val), (), w)

        dma("sp", consts[:], const_in, w=["consts"])
        dma("sp", pv[:], pv_in, w=["pv"])
        dma("sp", bc[:], bc_in, w=["bc"])
        dma("sp", cTs[:], cT_in, w=["cTs"])
        memset("dve", eps_t[:], EPS, w=["eps"])
        cp("dve", ones_bf[:], C["ones"], r=["consts"], w=["ones_bf"])
        cp("dve", ones_eo[:, 0, :], C["ones_e"], r=["consts"], w=["ones_eo"])
        cp("dve", ones_eo[:, 1, :], C["ones_o"], r=["consts"], w=["ones_eo"])

        for l in range(L):
            for k in range(8):
                rows = slice(k * 128, (k + 1) * 128)
                dma("pool", wb_in[l, rows, 0:5904], w_in[l, rows, 0:5904], w=[("wb_in", l, k)])
        for l in range(L):
            for k in range(8):
                rows = slice(k * 128, (k + 1) * 128)
                for i in range(4):
                    dma("pool", wb_mg[l, i, :, :, k, :].rearrange("m p c -> p m c"),
                        w_in[l, rows, 5904 + i * 1024:5904 + (i + 1) * 1024].rearrange("p (m c) -> p m c", c=128),
                        w=[("wb_mg", l, i, k)])
            for i in range(4):
                for k in range(4):
                    dma("pool", wb_mg[l, i, :, :, 8 + k, :].rearrange("m p c -> p m c"),
                        w_branch[l, i, k * 128:(k + 1) * 128, :].rearrange("p (m c) -> p m c", c=128),
                        w=[("wb_mg", l, i, 8 + k)])
            for k in range(8):
                rows = slice(k * 128, (k + 1) * 128)
                dma("pool", wb_out[l, :, :, k, :].rearrange("g p m -> p g m"),
                    w_out[l, rows, :].rearrange("p (g m) -> p g m", m=256), w=[("wb_out", l, k)])
            for k in range(8):
                rows = slice(k * 128, (k + 1) * 128)
                for hf in range(2):
                    dma("pool", wb_m1[l, hf * 8:(hf + 1) * 8, :, k, :].rearrange("g p m -> p g m"),
                        w_m1[l, rows, hf * 2048:(hf + 1) * 2048].rearrange("p (g m) -> p g m", m=256), w=[("wb_m1", l, k, hf)])
            for r_ in range(32):
                kg, k = r_ // 8, r_ % 8
                dma("pool", wb_m2[l, kg, :, :, k, :].rearrange("g p m -> p g m"),
                    w_m2[l, r_ * 128:(r_ + 1) * 128, :].rearrange("p (g m) -> p g m", m=256), w=[("wb_m2", l, r_)])

        act(cTs[:], cTs[:], AF.Silu, r=["cTs"], w=["cTs"])
        for l in range(L):
            mb = bank()
            mps = ps[mb][:, 0:48 * NCOL].rearrange("p (j c) -> p j c", c=NCOL)
            for g in range(12):
                s = g % 2
                dma("sp", wst[:, s], w_mod[l, :, g * 512:(g + 1) * 512].rearrange("(k p) m -> p k m", p=128), w=[("wst", s)])
                for j4 in range(4):
                    j = g * 4 + j4
                    for k in range(8):
                        mm(mps[:, j, :], wst[:, s, k, j4 * 128:(j4 + 1) * 128], cTs[:, k, :], start=(k == 0), stop=(k == 7),
                           r=[("wst", s), "cTs"], w=[psk(mb)])
            bm = pv[:, l * PV_N + PV_BM:l * PV_N + PV_BM + 48]
            tt("dve", modT[:, l], mps, bm.unsqueeze(2).to_broadcast([128, 48, NCOL]), ALU.add, r=[psk(mb), "pv"], w=[("modT", l)])
            gl = pv[:, l * PV_N + PV_G:l * PV_N + PV_G + 32]
            for idx, (gi, mo, plus1) in enumerate([(0, 8, True), (2, 32, True), (1, 16, False), (3, 40, False)]):
                src = modT[:, l, mo:mo + 8, :]
                gb = gl[:, gi * 8:(gi + 1) * 8].unsqueeze(2).to_broadcast([128, 8, NCOL])
                if plus1:
                    tsc("dve", modA[:, l, idx], src, 1.0, ALU.add, r=[("modT", l)], w=[("modA", l, idx)])
                    tt("dve", modA[:, l, idx], modA[:, l, idx], gb, ALU.mult, r=[("modA", l, idx), "pv"], w=[("modA", l, idx)])
                else:
                    tt("dve", modA[:, l, idx], src, gb, ALU.mult, r=[("modT", l), "pv"], w=[("modA", l, idx)])

        fence()
        if "modT" in dbg_out:
            out_events.append(dma("sp", dbg_out["modT"], modT[:], r=[("modT", l) for l in range(L)]))

        def wslot_next():
            s = state["ws"]
            state["ws"] = (s + 1) % NW
            return s

        def wload(parts):
            s = wslot_next()
            key = ("ws", s)
            views = []
            off = 0
            first = True
            for ap, nk, rk in parts:
                M = ap.shape[1]
                v = wslot[s][:, off:off + nk * M].rearrange("p (k m) -> p k m", m=M)
                src = ap.rearrange("(k p) m -> p k m", p=128)
                if first:
                    dma("sp", v, src, r=rk, w=[key])
                else:
                    dma("sp", v, src, r=rk, wa=[key])
                first = False
                views.append(v)
                off += nk * M
            assert off <= 2048
            return views, key

        def wload_c(src, nk, M, rk):
            s_ = wslot_next()
            key = ("ws", s_)
            v = wslot[s_][:, 0:nk * M].rearrange("p (k m) -> p k m", m=M)
            dma("sp", v, src, r=rk, w=[key])
            return v, key

        def win_keys(l):
            return [("wb_in", l, k) for k in range(8)]

        def proj(b, M, wv, m0, t0, n, wkey):
            for k in range(8):
                mm(ps[b][:M, :n], wv[:, k, m0:m0 + M], hT[:, k, t0:t0 + n], start=(k == 0), stop=(k == 7),
                   r=[wkey, "hT"], w=[psk(b)])

        def rms_rstd(src, n, rkeys):
            act(wkB[:, :, :n], src, AF.Square, r=rkeys, w=["wkB"])
            b = bank()
            for k in range(8):
                mmr(ps[b][:, :n], C["ones"], wkB[:, k, :n], start=(k == 0), stop=(k == 7), r=["wkB", "consts"], w=[psk(b)])
            act(rsA[:, :n], ps[b][:, :n], AF.Sqrt, r=[psk(b), "eps"], w=["rsA"], scale=1.0 / D, bias=eps_t[:])
            recip(rsA[:, :n], rsA[:, :n], r=["rsA"], w=["rsA"])

        def mod_col(l, idx, k, col):
            return modA[:, l, idx, k, col:col + 1]

        def norm_mod_block(l, col, which, src, t0, n, rkeys):
            rms_rstd(src, n, rkeys)
            tt("dve", wkB[:, :, :n], src, rsA[:, :n].unsqueeze(1).to_broadcast([128, 8, n]), ALU.mult,
               r=list(rkeys) + ["rsA"], w=["wkB"])
            sh0 = 0 if which == 0 else 24
            for k in range(8):
                act(hT[:, k, t0:t0 + n], wkB[:, k, :n], AF.Identity, r=["wkB", ("modA", l, which), ("modT", l)], w=["hT"],
                    scale=mod_col(l, which, k, col), bias=modT[:, l, sh0 + k, col:col + 1])

        def residual_block(l, col, which, ysrc, n, ykeys):
            rms_rstd(ysrc, n, ykeys)
            tt("dve", wkB[:, :, :n], ysrc, rsA[:, :n].unsqueeze(1).to_broadcast([128, 8, n]), ALU.mult,
               r=list(ykeys) + ["rsA"], w=["wkB"])
            for k in range(8):
                stt(xblk[:, k, :n], wkB[:, k, :n], mod_col(l, 2 + which, k, col), xblk[:, k, :n], ALU.mult, ALU.add,
                    r=["wkB", "xblk", ("modA", l, 2 + which)], w=["xblk"])

        xres_v = xres.rearrange("(k p) t -> p k t", p=128)

        from_mixers = {}

        def phase_a(l, b):
            for bi, (t0, n) in enumerate(NBLK):
                col = NB if bi == 0 else b
                dma("sp", xblk[:, :, :n], xres_v[:, :, t0:t0 + n], r=[("xres", bi)], w=["xblk"])
                norm_mod_block(l, col, 0, xblk[:, :, :n], t0, n, ["xblk"])

        tscv = aview(0, [T + 4])

        def seg_off(t0):
            return t0 + 1 if t0 < TC else t0 + 3

        def mixer_sc(l, b):
            base = 4368
            memset("pool", tscv, 0.0, w=["tsc"])
            for c in range(4):
                (wc, wx), k1 = wload([(wb_in[l, :, base + 512 + c * 128: base + 512 + (c + 1) * 128], 8, win_keys(l)),
                                      (wb_in[l, :, base + 1024 + c * 128: base + 1024 + (c + 1) * 128], 8, win_keys(l))])
                for (t0, n) in NBLK:
                    b1 = bank(); b2 = bank()
                    proj(b1, 128, wc, 0, t0, n, k1)
                    proj(b2, 128, wx, 0, t0, n, k1)
                    xs, xk = t5()
                    cp("act", xs[:, :n], ps[b2][:, :n], r=[psk(b2)], w=[xk])
                    so = seg_off(t0)
                    tt("dve", tscv[:, so:so + n], ps[b1][:, :n], xs[:, :n], ALU.mult, r=[psk(b1), xk], w=["tsc"])
                (wbv,), k2 = wload([(wb_in[l, :, base + c * 128: base + (c + 1) * 128], 8, win_keys(l))])
                wcol = lambda tap: pv[:, l * PV_N + PV_SCC + tap * 4 + c: l * PV_N + PV_SCC + tap * 4 + c + 1]
                for (t0, n) in NBLK:
                    b1 = bank()
                    proj(b1, 128, wbv, 0, t0, n, k2)
                    so = seg_off(t0)
                    cv, ck = t5()
                    tsc("dve", cv[:, :n], tscv[:, so - 1:so - 1 + n], wcol(0), ALU.mult, r=["tsc", "pv"], w=[ck])
                    stt(cv[:, :n], tscv[:, so:so + n], wcol(1), cv[:, :n], ALU.mult, ALU.add, r=["tsc", "pv", ck], w=[ck])
                    stt(cv[:, :n], tscv[:, so + 1:so + 1 + n], wcol(2), cv[:, :n], ALU.mult, ALU.add, r=["tsc", "pv", ck], w=[ck])
                    yb_, yk = ybuf()
                    tt("dve", yb_[:, :n], ps[b1][:, :n], cv[:, :n], ALU.mult, r=[psk(b1), ck], w=[yk])
                    dma("pool", ybr[3, c * 128:(c + 1) * 128, t0:t0 + n], yb_[:, :n], r=[yk], wa=[("ybr", 3)])

        def mixer_att(l, b):
            QT = aview(0, [4, T], BF16)
            KT = aview(4608, [2, T], BF16)
            VP = aview(6912, [18, 2, 2, 128], BF16)
            PT = [aview(11520 + i * 256, [512], BF16) for i in range(4)]
            qn = aview(12544, [512]); sqb = aview(13056, [512]); rsb = aview(13568, [512])
            t1 = aview(14080, [512]); t2 = aview(14592, [512]); rD = aview(15104, [512])
            cosT = aview(15616, [TL]); sinT = aview(17664, [TL])
            dma("sp", cosT, cos_in, w=["cos"])
            dma("sp", sinT, sin_in, w=["sin"])
            memset("pool", VP, 0.0, w=["VP"])
            keys = win_keys(l)
            for kind, ci in [("q", 0), ("q", 1), ("q", 2), ("q", 3), ("k", 0), ("k", 1)]:
                if kind == "q":
                    (wv,), wk = wload([(wb_in[l, :, ci * 128:(ci + 1) * 128], 8, keys)])
                    gain = pv[:, l * PV_N + PV_QG:l * PV_N + PV_QG + 1]
                    dst = QT[:, ci]; dkey = "QT"
                else:
                    s_ = wslot_next(); wk = ("ws", s_)
                    wv = wslot[s_][:, 0:1024].rearrange("p (k m) -> p k m", m=128)
                    src = wb_in[l, :, 512 + ci * 64:512 + (ci + 1) * 64].rearrange("(k p) m -> p k m", p=128)
                    dma("sp", wv[:, :, 0:64], src, r=keys, w=[wk])
                    dma("sp", wv[:, :, 64:128], src, r=keys, wa=[wk])
                    gain = pv[:, l * PV_N + PV_KG:l * PV_N + PV_KG + 1]
                    dst = KT[:, ci]; dkey = "KT"
                for bi, (t0, n) in enumerate(NBLK):
                    bq = bank()
                    proj(bq, 128, wv, 0, t0, n, wk)
                    act(sqb[:, :n], ps[bq][:, :n], AF.Square, r=[psk(bq)], w=["sqb"])
                    bs = bank()
                    mm(ps[bs][:, :n], C["blk64m"], sqb[:, :n], r=["sqb", "consts"], w=[psk(bs)])
                    act(rsb[:, :n], ps[bs][:, :n], AF.Sqrt, r=[psk(bs), "eps"], w=["rsb"], bias=eps_t[:])
                    recip(rsb[:, :n], rsb[:, :n], r=["rsb"], w=["rsb"])
                    if bi == 0:
                        stt(dst[:, t0:t0 + n], ps[bq][:, :n], gain, rsb[:, :n], ALU.mult, ALU.mult,
                            r=[psk(bq), "rsb", "pv"], wa=[dkey])
                    else:
                        stt(qn[:, :n], ps[bq][:, :n], gain, rsb[:, :n], ALU.mult, ALU.mult, r=[psk(bq), "rsb", "pv"], w=["qn"])
                        br = bank()
                        mm(ps[br][:, :n], C["rrot"], qn[:, :n], r=["qn", "consts"], w=[psk(br)])
                        lo = t0 - TC
                        tt("pool", t1[:, :n], qn[:, :n], cosT[:, lo:lo + n], ALU.mult, r=["qn", "cos"], w=["t1"])
                        tt("dve", t2[:, :n], ps[br][:, :n], sinT[:, lo:lo + n], ALU.mult, r=[psk(br), "sin"], w=["t2"])
                        tt("pool", dst[:, t0:t0 + n], t1[:, :n], t2[:, :n], ALU.add, r=["t1", "t2"], wa=[dkey])
            import os
            ATT_STOP = int(os.environ.get("ATT_STOP", "9"))
            if ATT_STOP <= 1:
                return
            (wvv,), wk = wload([(wb_in[l, :, 640:768], 8, keys)])
            for tg in range(5):
                tts = list(range(tg * 4, min(18, tg * 4 + 4)))
                bv = bank()
                for j, tti in enumerate(tts):
                    for k in range(8):
                        mm(ps[bv][:, j * 128:(j + 1) * 128], hT[:, k, tti * 128:(tti + 1) * 128], wvv[:, k, :],
                           start=(k == 0), stop=(k == 7), r=[wk, "hT"], w=[psk(bv)])
                nt = len(tts)
                pv4 = ps[bv][:, :nt * 128].rearrange("p (j m) -> p j m", m=128)
                VCOPY = os.environ.get("VCOPY", "act,act")
                for g in range(2):
                    for e in range(2):
                        if VCOPY == "none":
                            continue
                        cp(VCOPY.split(",")[(g + e) % 2], VP[:, tg * 4:tg * 4 + nt, g, e, e * 64:(e + 1) * 64],
                           pv4[:, :, g * 64:(g + 1) * 64], r=[psk(bv)], wa=["VP"])
            if ATT_STOP <= 2:
                return
            for bi, (t0, n) in enumerate(NBLK):
                ktiles = [0, 1] if bi == 0 else list(range(18))
                for c in range(4):
                    g = c // 2
                    od = state.get("od", 0); state["od"] = 1 - od
                    bo, bd = 4 + 2 * od, 5 + 2 * od
                    steps = [(kt, e) for kt in ktiles for e in (0, 1)]
                    ns = len(steps)
                    SK = 2
                    for i in range(ns + SK):
                        if i < ns:
                            kt, e = steps[i]
                            bsx = state.get("sbk", 0); state["sbk"] = (bsx + 1) % 4
                            mm(ps[bsx][:, :n], KT[e * 64:(e + 1) * 64, g, kt * 128:(kt + 1) * 128],
                               QT[e * 64:(e + 1) * 64, c, t0:t0 + n], r=["KT", "QT"], w=[psk(bsx)])
                            act(PT[i % 4][:, :n], ps[bsx][:, :n], AF.Exp, r=[psk(bsx)], w=[("PT", i % 4)], scale=0.125)
                        if i >= SK:
                            ii = i - SK
                            kt, e = steps[ii]
                            pt = PT[ii % 4]
                            mm(ps[bo][:, :n], VP[:, kt, g, e, :], pt[:, :n], start=(ii == 0), stop=(ii == ns - 1),
                               r=[("PT", ii % 4), "VP"], w=[psk(bo)])
                            mm(ps[bd][:, :n], ones_eo[:, e, :], pt[:, :n], start=(ii == 0), stop=(ii == ns - 1),
                               r=[("PT", ii % 4), "ones_eo"], w=[psk(bd)])
                    recip(rD[:, :n], ps[bd][:, :n], r=[psk(bd)], w=["rD"])
                    yb_, yk = ybuf()
                    tt("dve", yb_[:, :n], ps[bo][:, :n], rD[:, :n], ALU.mult, r=[psk(bo), "rD"], w=[yk])
                    dma("pool", ybr[0, c * 128:(c + 1) * 128, t0:t0 + n], yb_[:, :n], r=[yk], wa=[("ybr", 0)])

        lgE = sb("lgE", [128, 8]); lgN = sb("lgN", [128, 8])
        OFF = 1920
        FW = 3968

        def mixer_ret(l, b):
            RQ = aview(0, [2, T], BF16)
            RK = aview(2304, [2, T], BF16)
            RV = aview(4608, [18, 512], BF16)
            Fm = aview(9216, [FW])
            PT = [aview(13184 + i * 256, [512], BF16) for i in range(4)]
            o_ = 14208
            qn = aview(o_, [512]); t1 = aview(o_ + 512, [512]); t2 = aview(o_ + 1024, [512]); dd = aview(o_ + 1536, [512])
            u2 = aview(o_ + 2048, [512]); msk = aview(o_ + 2560, [512]); osb = aview(o_ + 3072, [512]); sqo = aview(o_ + 3584, [512])
            mean_s = aview(o_ + 4096, [512]); tmpv = aview(o_ + 4608, [512]); rso = aview(o_ + 5120, [512]); sg = aview(o_ + 5632, [512])
            dd2 = aview(o_ + 6144, [512]); e1 = aview(o_ + 6656, [512])
            cmsk = [msk, aview(o_ + 7168 + 2 * TL, [512])]
            cosT = aview(o_ + 7168, [TL]); sinT = aview(o_ + 7168 + TL, [TL])
            dma("sp", cosT, cos_in, w=["cos"])
            dma("sp", sinT, sin_in, w=["sin"])
            keys = win_keys(l)
            dbase = C["dbase"]
            bcr = bc[:, l * BC_N + BC_RET:l * BC_N + BC_RET + 8]
            act(lgE[:], bcr, AF.Exp, r=["bc"], w=["lgE"])
            tsc("dve", lgN[:], lgE[:], -1.0, ALU.mult, r=["lgE"], w=["lgN"])
            for kind, ci in [("q", 0), ("q", 1), ("k", 0), ("k", 1)]:
                c0 = (2832 if kind == "q" else 3088) + ci * 128
                (wv,), wk = wload([(wb_in[l, :, c0:c0 + 128], 8, keys)])
                dst = (RQ if kind == "q" else RK)[:, ci]
                dkey = "RQ" if kind == "q" else "RK"
                sc_ = 1.0 if kind == "q" else 0.125
                for bi, (t0, n) in enumerate(NBLK):
                    bq = bank()
                    proj(bq, 128, wv, 0, t0, n, wk)
                    if bi == 0:
                        act(dst[:, t0:t0 + n], ps[bq][:, :n], AF.Copy, r=[psk(bq)], wa=[dkey], scale=sc_)
                    else:
                        act(qn[:, :n], ps[bq][:, :n], AF.Copy, r=[psk(bq)], w=["qn"], scale=sc_)
                        br = bank()
                        mm(ps[br][:, :n], C["rrot"], qn[:, :n], r=["qn", "consts"], w=[psk(br)])
                        lo = t0 - TC
                        tt("pool", t1[:, :n], qn[:, :n], cosT[:, lo:lo + n], ALU.mult, r=["qn", "cos"], w=["t1"])
                        tt("dve", t2[:, :n], ps[br][:, :n], sinT[:, lo:lo + n], ALU.mult, r=[psk(br), "sin"], w=["t2"])
                        tt("pool", dst[:, t0:t0 + n], t1[:, :n], t2[:, :n], ALU.add, r=["t1", "t2"], wa=[dkey])
            for half in range(2):
                (wvv,), wk = wload([(wb_in[l, :, 3344 + half * 256:3344 + (half + 1) * 256], 8, keys)])
                for tti in range(18):
                    bv = bank()
                    for k in range(8):
                        mm(ps[bv][:, :256], hT[:, k, tti * 128:(tti + 1) * 128], wvv[:, k, :], start=(k == 0), stop=(k == 7),
                           r=[wk, "hT"], w=[psk(bv)])
                    cp("act", RV[:, tti, half * 256:(half + 1) * 256], ps[bv][:, :256], r=[psk(bv)], wa=["RV"])
            for h in range(4):
                cq, e = h // 2, h % 2
                lgf = lgN[:, h:h + 1]; lgb = lgN[:, 4 + h:5 + h]; nlgb = lgE[:, 4 + h:5 + h]
                for pi in range(8):
                    m0 = pi * 512
                    w_ = min(512, FW - m0)
                    tsc("pool", dd[:, :w_], dbase[:, :w_], float(m0 - OFF), ALU.add, r=["consts"], w=["dd"])
                    tsc("pool", u2[:, :w_], dd[:, :w_], nlgb, ALU.mult, r=["dd", "lgE"], w=["u2"])
                    stt(u2[:, :w_], dd[:, :w_], lgf, u2[:, :w_], ALU.mult, ALU.min, r=["dd", "u2", "lgN"], w=["u2"])
                    act(u2[:, :w_], u2[:, :w_], AF.Exp, r=["u2"], w=["u2"])
                    stt(Fm[:, m0:m0 + w_], dd[:, :w_], 0.0, u2[:, :w_], ALU.is_equal, ALU.add, r=["dd", "u2"], w=["Fm"])
                (wg,), wkg = wload([(wb_in[l, :, 3856 + h * 128:3856 + (h + 1) * 128], 8, keys)])
                for bi, (t0, n) in enumerate(NBLK):
                    ktiles = [0, 1] if bi == 0 else list(range(2, 18)) + [0, 1]
                    t0l = t0 - TC
                    od = state.get("odr", 0); state["odr"] = 1 - od
                    bo = 4 + od
                    ns = len(ktiles)
                    SK = 2
                    if bi > 0:
                        for kt in (0, 1):
                            o1 = float(256 + t0l - kt * 128); o2 = float(2048 - t0l + kt * 128)
                            mk_ = cmsk[kt]; mkk = ("cmsk", kt)
                            tsc("pool", dd[:, :n], dbase[:, :n], o1, ALU.add, r=["consts"], w=["dd"])
                            act(e1[:, :n], dd[:, :n], AF.Exp, r=["dd", "lgN"], w=["e1"], scale=lgf)
                            tsc("pool", dd2[:, :n], dbase[:, :n], -1.0, ALU.mult, o2, ALU.add, r=["consts"], w=["dd2"])
                            act(mk_[:, :n], dd2[:, :n], AF.Exp, r=["dd2", "lgN"], w=[mkk], scale=lgb)
                            tt("pool", mk_[:, :n], mk_[:, :n], e1[:, :n], ALU.add, r=[mkk, "e1"], w=[mkk])
                    for i in range(ns + SK):
                        if i < ns:
                            kt = ktiles[i]
                            bsx = state.get("sbk", 0); state["sbk"] = (bsx + 1) % 4
                            mm(ps[bsx][:, :n], RK[e * 64:(e + 1) * 64, cq, kt * 128:(kt + 1) * 128],
                               RQ[e * 64:(e + 1) * 64, cq, t0:t0 + n], r=["RK", "RQ"], w=[psk(bsx)])
                            pt = PT[i % 4]; pk = ("PT", i % 4)
                            if bi == 0:
                                st_ = OFF - kt * 128
                                tt("dve", pt[:, :n], ps[bsx][:, :n], Fm[:, st_:st_ + n], ALU.mult, r=[psk(bsx), "Fm"], w=[pk])
                            elif kt >= 2:
                                st_ = OFF + t0l - (kt - 2) * 128
                                tt("dve", pt[:, :n], ps[bsx][:, :n], Fm[:, st_:st_ + n], ALU.mult, r=[psk(bsx), "Fm"], w=[pk])
                            else:
                                tt("dve", pt[:, :n], ps[bsx][:, :n], cmsk[kt][:, :n], ALU.mult, r=[psk(bsx), ("cmsk", kt)], w=[pk])
                        if i >= SK:
                            ii = i - SK
                            kt = ktiles[ii]
                            mm(ps[bo][:, :n], RV[:, kt, h * 128:(h + 1) * 128], PT[ii % 4][:, :n], start=(ii == 0), stop=(ii == ns - 1),
                               r=[("PT", ii % 4), "RV"], w=[psk(bo)])
                    proj(6, 128, wg, 0, t0, n, wkg)
                    act(sg[:, :n], ps[6][:, :n], AF.Silu, r=[psk(6)], w=["sg"])
                    act(osb[:, :n], ps[bo][:, :n], AF.Copy, r=[psk(bo)], w=["osb"])
                    act(sqo[:, :n], ps[bo][:, :n], AF.Square, r=[psk(bo)], w=["sqo"])
                    mm(ps[7][:, :n], C["onesm128"], osb[:, :n], r=["osb", "consts"], w=[psk(7)])
                    bx = state.get("sbk", 0); state["sbk"] = (bx + 1) % 4
                    mm(ps[bx][:, :n], C["onesm128"], sqo[:, :n], r=["sqo", "consts"], w=[psk(bx)])
                    act(mean_s[:, :n], ps[7][:, :n], AF.Copy, r=[psk(7)], w=["mean_s"])
                    tt("pool", tmpv[:, :n], mean_s[:, :n], mean_s[:, :n], ALU.mult, r=["mean_s"], w=["tmpv"])
                    tt("dve", rso[:, :n], ps[bx][:, :n], tmpv[:, :n], ALU.subtract, r=[psk(bx), "tmpv"], w=["rso"])
                    tsc("dve", rso[:, :n], rso[:, :n], 0.0, ALU.max, r=["rso"], w=["rso"])
                    act(rso[:, :n], rso[:, :n], AF.Sqrt, r=["rso", "eps"], w=["rso"], bias=eps_t[:])
                    recip(rso[:, :n], rso[:, :n], r=["rso"], w=["rso"])
                    tt("pool", osb[:, :n], osb[:, :n], mean_s[:, :n], ALU.subtract, r=["osb", "mean_s"], w=["osb"])
                    tt("pool", osb[:, :n], osb[:, :n], rso[:, :n], ALU.mult, r=["osb", "rso"], w=["osb"])
                    yb_, yk = ybuf()
                    tt("dve", yb_[:, :n], osb[:, :n], sg[:, :n], ALU.mult, r=["osb", "sg"], w=[yk])
                    dma("pool", ybr[2, h * 128:(h + 1) * 128, t0:t0 + n], yb_[:, :n], r=[yk], wa=[("ybr", 2)])

        la_tm = sb("la_tm", [128, 18, 8]); beta_tm = sb("beta_tm", [128, 18, 8]); acoef = sb("acoef", [128, 8])
        FORD = list(range(18))
        BORD = [1, 0] + list(range(17, 1, -1))

        def mixer_dn(l, b):
            keys = win_keys(l)
            ident = C["ident"]; ones = C["ones"]
            pre = aview(0, [T + 4]); cvb = aview(2308, [512]); slb = aview(2820, [512]); sqb = aview(3332, [512])
            rsb = aview(3844, [512]); ab = aview(4356, [18, 16]); tmp8 = aview(4644, [18, 8]); tmp8b = aview(4788, [18, 8])
            stg = [aview(4932 + i * 512, [512]) for i in range(2)]
            memset("pool", pre, 0.0, w=["pre"])
            for j in range(12):
                kind, h = j // 4, j % 4
                c0 = 768 + kind * 512 + h * 128
                (wv,), wk = wload([(wb_in[l, :, c0:c0 + 128], 8, keys)])
                for (t0, n) in NBLK:
                    bq = bank()
                    proj(bq, 128, wv, 0, t0, n, wk)
                    so = seg_off(t0)
                    cp("act", pre[:, so:so + n], ps[bq][:, :n], r=[psk(bq)], w=["pre"])
                wcol = lambda tap: pv[:, l * PV_N + PV_DNC + tap * 12 + j: l * PV_N + PV_DNC + tap * 12 + j + 1]
                for bi, (t0, n) in enumerate(NBLK):
                    so = seg_off(t0)
                    tsc("dve", cvb[:, :n], pre[:, so - 1:so - 1 + n], wcol(0), ALU.mult, r=["pre", "pv"], w=["cvb"])
                    stt(cvb[:, :n], pre[:, so:so + n], wcol(1), cvb[:, :n], ALU.mult, ALU.add, r=["pre", "pv", "cvb"], w=["cvb"])
                    stt(cvb[:, :n], pre[:, so + 1:so + 1 + n], wcol(2), cvb[:, :n], ALU.mult, ALU.add, r=["pre", "pv", "cvb"], w=["cvb"])
                    sg_ = stg[bi % 2]; sgk = ("stg", bi % 2)
                    if kind == 2:
                        act(sg_[:, :n], cvb[:, :n], AF.Silu, r=["cvb"], w=[sgk])
                    else:
                        act(slb[:, :n], cvb[:, :n], AF.Silu, r=["cvb"], w=["slb"])
                        act(sqb[:, :n], slb[:, :n], AF.Square, r=["slb"], w=["sqb"])
                        bs = bank()
                        mm(ps[bs][:, :n], ones, sqb[:, :n], r=["sqb", "consts"], w=[psk(bs)])
                        act(rsb[:, :n], ps[bs][:, :n], AF.Sqrt, r=[psk(bs), "eps"], w=["rsb"], bias=eps_t[:])
                        recip(rsb[:, :n], rsb[:, :n], r=["rsb"], w=["rsb"])
                        stt(sg_[:, :n], slb[:, :n], (128.0 ** -0.5) if kind == 0 else 1.0, rsb[:, :n], ALU.mult, ALU.mult,
                            r=["slb", "rsb"], w=[sgk])
                    dma("pool", dnq[j, :, t0:t0 + n], sg_[:, :n], r=[sgk], w=[("dnq", j, bi)])
            (wab,), wk = wload([(wb_in[l, :, 2816:2832], 8, keys)])
            bab = bank()
            for tti in range(18):
                for k in range(8):
                    mm(ps[bab][:, tti * 16:(tti + 1) * 16], hT[:, k, tti * 128:(tti + 1) * 128], wab[:, k, :], start=(k == 0), stop=(k == 7),
                       r=[wk, "hT"], w=[psk(bab)])
            cp("dve", ab, ps[bab][:, :288].rearrange("p (t c) -> p t c", c=16), r=[psk(bab)], w=["ab"])
            bcl = bc[:, l * BC_N:(l + 1) * BC_N]
            tt("dve", tmp8, ab[:, :, 0:8], bcl[:, BC_DTB:BC_DTB + 8].unsqueeze(1).to_broadcast([128, 18, 8]), ALU.add, r=["ab", "bc"], w=["tmp8"])
            act(tmp8, tmp8, AF.Exp, r=["tmp8"], w=["tmp8"])
            act(tmp8, tmp8, AF.Ln, r=["tmp8", "consts"], w=["tmp8"], bias=ones[:, 0:1])
            act(acoef[:], bcl[:, BC_ALOG:BC_ALOG + 8], AF.Exp, r=["bc"], w=["acoef"])
            tsc("dve", acoef[:], acoef[:], -1.0, ALU.mult, r=["acoef"], w=["acoef"])
            tt("dve", la_tm[:], tmp8, acoef[:].unsqueeze(1).to_broadcast([128, 18, 8]), ALU.mult, r=["tmp8", "acoef"], w=["la_tm"])
            act(beta_tm[:], ab[:, :, 8:16], AF.Sigmoid, r=["ab"], w=["beta_tm"])
            fence()
            class BT:
                def __init__(self, i):
                    self.i = i
                    self.t = aview(i * 1024, [1024])
                    self.v3 = self.t.rearrange("p (c j) -> p c j", j=128)
                    self.tr_ = self.t.bitcast(F32R)
                    self.v3r = self.tr_.rearrange("p (c j) -> p c j", j=128)
                def hr(self, d):
                    return self.tr_[:, d * 512:(d + 1) * 512]
                def h(self, d):
                    return self.t[:, d * 512:(d + 1) * 512]
                def c(self, c):
                    return self.v3[:, c, :]
                def k(self, d):
                    return ("b", self.i, d)
                def K(self):
                    return [("b", self.i, 0), ("b", self.i, 1)]
            Sst = BT(0); ktm = BT(4); vtm = BT(5); Dg = BT(6); Db = BT(7)
            Gm = BT(8); Gm2 = BT(9); E = BT(10); ET = BT(11); N_ = BT(12); NT_ = BT(13)
            Xs = [(BT(14), BT(15)), (BT(6), BT(7))]
            PTt = BT(16); aqkT = BT(17); wvu = BT(18); wkT = BT(19); qdT = BT(20); kdec = BT(21); Pm = BT(22); ost = BT(23)
            No = BT(8); NTo = BT(9); R1s = BT(10); R1ps = BT(11)
            qkv = aview(1024, [2, 12, 128])
            sm = aview(24576, [64])
            gt16 = sm[:, 0:16]; eg = sm[:, 16:24]; kdecs = sm[:, 24:32]; egl = sm[:, 32:40]; negbeta = sm[:, 40:48]
            bexp = sm[:, 48:56]; negg = sm[:, 56:64]
            betas = aview(24640, [8])
            hvc = lambda t2, d: t2[:, d * 512:(d + 1) * 512]
            bc3 = lambda small: small.unsqueeze(2).to_broadcast([128, 8, 128])
            b8 = lambda m: m.unsqueeze(1).to_broadcast([128, 8, 128])
            idb = b8(ident)
            offb = b8(C["offdiag"])
            A_, B_, C_, D_ = (0, 1), (2, 3), (4, 5), (6, 7)
            reg = lambda c: slice((c % 4) * 128, (c % 4 + 1) * 128)
            memset("pool", Sst.t, 0.0, w=Sst.K())
            for s_i in range(18):
                cd = (FORD[s_i], BORD[s_i])
                for d in range(2):
                    tok0 = cd[d] * 128
                    dma("sp", qkv[:, d], dnq[:, :, tok0:tok0 + 128].rearrange("j p t -> p j t"),
                        r=[("dnq", j, bi) for j in range(12) for bi in range(5)], w=[("qkv", d)])
                for d in range(2):
                    for h in range(4):
                        tr(ps[A_[d]][:, h * 128:(h + 1) * 128], qkv[:, d, 4 + h, :], ident, r=[("qkv", d), "consts"], w=[psk(A_[d])])
                    for h in range(4):
                        tr(ps[B_[d]][:, h * 128:(h + 1) * 128], qkv[:, d, 8 + h, :], ident, r=[("qkv", d), "consts"], w=[psk(B_[d])])
                    cp("act", ktm.h(d), ps[A_[d]][:, :], r=[psk(A_[d])], w=[ktm.k(d)])
                    cp("dve", vtm.h(d), ps[B_[d]][:, :], r=[psk(B_[d])], w=[vtm.k(d)])
                for d in range(2):
                    la_d = la_tm[:, cd[d], d * 4:(d + 1) * 4]
                    mm(ps[C_[0]][:, d * 4:(d + 1) * 4], C["tri_f"] if d == 0 else C["tri_b"], la_d, r=["la_tm", "consts"], w=[psk(C_[0])])
                    mm(ps[C_[0]][:, 8 + d * 4:8 + (d + 1) * 4], ones, la_d, r=["la_tm", "consts"], w=[psk(C_[0])])
                cp("dve", gt16, ps[C_[0]][:, 0:16], r=[psk(C_[0])], w=["gt16"])
                act(eg, gt16[:, 0:8], AF.Exp, r=["gt16"], w=["eg"])
                act(egl, gt16[:, 8:16], AF.Exp, r=["gt16"], w=["egl"])
                tsc("dve", negg, gt16[:, 0:8], -1.0, ALU.mult, r=["gt16"], w=["negg"])
                tt("dve", kdecs, gt16[:, 8:16], gt16[:, 0:8], ALU.subtract, r=["gt16"], w=["kdecs"])
                act(kdecs, kdecs, AF.Exp, r=["kdecs"], w=["kdecs"])
                cp("dve", betas[:, 0:4], beta_tm[:, cd[0], 0:4], r=["beta_tm"], w=["betas"])
                cp("dve", betas[:, 4:8], beta_tm[:, cd[1], 4:8], r=["beta_tm", "betas"], w=["betas"])
                tsc("dve", negbeta, betas, -1.0, ALU.mult, r=["betas"], w=["negbeta"])
                tt("dve", bexp, betas, eg, ALU.mult, r=["betas", "eg"], w=["bexp"])
                tt("dve", Dg.v3, idb, bc3(gt16[:, 0:8]), ALU.mult, r=["consts", "gt16"], w=Dg.K())
                tt("pool", Db.v3, idb, bc3(betas), ALU.mult, r=["consts", "betas"], w=Db.K())
                for d in range(2):
                    mm(ps[A_[d]][:, :], ones, Dg.h(d), r=[Dg.k(d), "consts"], w=[psk(A_[d])])
                    mm(ps[B_[d]][:, :], ones, Db.h(d), r=[Db.k(d), "consts"], w=[psk(B_[d])])
                for d in range(2):
                    tt("dve", Gm.h(d), ps[A_[d]][:, :], hvc(C["mask_s"], d), ALU.add, r=[psk(A_[d]), "consts"], w=[Gm.k(d)])
                    tt("dve", Gm2.h(d), ps[A_[d]][:, :], hvc(C["mask_t"], d), ALU.subtract, r=[psk(A_[d]), "consts"], w=[Gm2.k(d)])
                for c in range(8):
                    d = c // 4
                    act(E.c(c), Gm.c(c), AF.Exp, r=[Gm.k(d), "gt16"], w=[E.k(d)], scale=-1.0, bias=gt16[:, c:c + 1])
                for c in range(8):
                    d = c // 4
                    act(ET.c(c), Gm2.c(c), AF.Exp, r=[Gm2.k(d), "negg"], w=[ET.k(d)], scale=1.0, bias=negg[:, c:c + 1])
                for d in range(2):
                    for h in range(4):
                        mm(ps[C_[d]][:, h * 128:(h + 1) * 128], qkv[:, d, 4 + h, :], qkv[:, d, 4 + h, :], r=[("qkv", d)], w=[psk(C_[d])])
                    for h in range(4):
                        mm(ps[D_[d]][:, h * 128:(h + 1) * 128], qkv[:, d, 4 + h, :], qkv[:, d, h, :], r=[("qkv", d)], w=[psk(D_[d])])
                for d in range(2):
                    tt("dve", aqkT.h(d), ps[D_[d]][:, :], ET.h(d), ALU.mult, r=[psk(D_[d]), ET.k(d)], w=[aqkT.k(d)])
                tt("pool", E.v3, E.v3, offb, ALU.mult, r=E.K() + ["consts"], w=E.K())
                tt("pool", E.v3, E.v3, bc3(negbeta), ALU.mult, r=E.K() + ["negbeta"], w=E.K())
                tt("pool", ET.v3, ET.v3, offb, ALU.mult, r=ET.K() + ["consts"], w=ET.K())
                for d in range(2):
                    tt("dve", N_.h(d), ps[C_[d]][:, :], E.h(d), ALU.mult, r=[psk(C_[d]), E.k(d)], w=[N_.k(d)])
                    tt("dve", ET.h(d), ps[B_[d]][:, :], ET.h(d), ALU.mult, r=[psk(B_[d]), ET.k(d)], w=[ET.k(d)])
                    stt(NT_.h(d), ps[C_[d]][:, :], -1.0, ET.h(d), ALU.mult, ALU.mult, r=[psk(C_[d]), ET.k(d)], w=[NT_.k(d)])
                X, XT = Xs[0]
                tt("pool", X.v3, N_.v3, b8(C["bd8"]), ALU.mult, r=N_.K() + ["consts"], w=X.K())
                tt("pool", XT.v3, NT_.v3, b8(C["bd8"]), ALU.mult, r=NT_.K() + ["consts"], w=XT.K())
                tt("pool", Pm.v3, X.v3, idb, ALU.add, r=X.K() + ["consts"], w=Pm.K())
                tt("pool", PTt.v3, XT.v3, idb, ALU.add, r=XT.K() + ["consts"], w=PTt.K())
                cur = 0
                for it in range(2):
                    X, XT = Xs[cur]
                    Xn, XTn = Xs[1 - cur]
                    for c in range(8):
                        d = c // 4
                        mmr(ps[A_[d]][:, reg(c)], XT.c(c), X.c(c), r=[X.k(d), XT.k(d)], w=[psk(A_[d])])
                    for c in range(8):
                        d = c // 4
                        mmr(ps[B_[d]][:, reg(c)], X.c(c), XT.c(c), r=[X.k(d), XT.k(d)], w=[psk(B_[d])])
                    for d in range(2):
                        cp("act", Xn.h(d), ps[A_[d]][:, :], r=[psk(A_[d])], w=[Xn.k(d)])
                        cp("dve", XTn.h(d), ps[B_[d]][:, :], r=[psk(B_[d])], w=[XTn.k(d)])
                    for c in range(8):
                        d = c // 4
                        mmr(ps[C_[d]][:, reg(c)], XTn.c(c), Pm.c(c), r=[XTn.k(d), Pm.k(d)], w=[psk(C_[d])])
                    for c in range(8):
                        d = c // 4
                        mmr(ps[D_[d]][:, reg(c)], Xn.c(c), PTt.c(c), r=[Xn.k(d), PTt.k(d)], w=[psk(D_[d])])
                    for d in range(2):
                        tt("dve", Pm.h(d), ps[C_[d]][:, :], Pm.h(d), ALU.add, r=[psk(C_[d]), Pm.k(d)], w=[Pm.k(d)])
                        tt("dve", PTt.h(d), ps[D_[d]][:, :], PTt.h(d), ALU.add, r=[psk(D_[d]), PTt.k(d)], w=[PTt.k(d)])
                    cur = 1 - cur
                for lv, offn in enumerate(["off8", "off16", "off32", "off64"]):
                    lastlv = (lv == 3)
                    tt("pool", No.v3, N_.v3, b8(C[offn]), ALU.mult, r=N_.K() + ["consts"], w=No.K())
                    if not lastlv:
                        tt("pool", NTo.v3, NT_.v3, b8(C[offn]), ALU.mult, r=NT_.K() + ["consts"], w=NTo.K())
                        for c in range(8):
                            d = c // 4
                            mmr(ps[A_[d]][:, reg(c)], NTo.c(c), Pm.c(c), r=[NTo.k(d), Pm.k(d)], w=[psk(A_[d])])
                    for c in range(8):
                        d = c // 4
                        mmr(ps[B_[d]][:, reg(c)], No.c(c), PTt.c(c), r=[No.k(d), PTt.k(d)], w=[psk(B_[d])])
                    for d in range(2):
                        if not lastlv:
                            cp("act", R1s.h(d), ps[A_[d]][:, :], r=[psk(A_[d])], w=[R1s.k(d)])
                        cp("dve", R1ps.h(d), ps[B_[d]][:, :], r=[psk(B_[d])], w=[R1ps.k(d)])
                    if not lastlv:
                        for c in range(8):
                            d = c // 4
                            mmr(ps[C_[d]][:, reg(c)], PTt.c(c), R1s.c(c), r=[PTt.k(d), R1s.k(d)], w=[psk(C_[d])])
                    for c in range(8):
                        d = c // 4
                        mmr(ps[D_[d]][:, reg(c)], Pm.c(c), R1ps.c(c), r=[Pm.k(d), R1ps.k(d)], w=[psk(D_[d])])
                    for d in range(2):
                        if not lastlv:
                            tt("dve", Pm.h(d), ps[C_[d]][:, :], Pm.h(d), ALU.add, r=[psk(C_[d]), Pm.k(d)], w=[Pm.k(d)])
                        tt("dve", PTt.h(d), ps[D_[d]][:, :], PTt.h(d), ALU.add, r=[psk(D_[d]), PTt.k(d)], w=[PTt.k(d)])
                tt("pool", kdec.v3, ktm.v3, bc3(kdecs), ALU.mult, r=ktm.K() + ["kdecs"], w=kdec.K())
                tt("pool", ktm.v3, ktm.v3, bc3(bexp), ALU.mult, r=ktm.K() + ["bexp"], w=ktm.K())
                tt("pool", vtm.v3, vtm.v3, bc3(betas), ALU.mult, r=vtm.K() + ["betas"], w=vtm.K())
                for c in range(8):
                    d = c // 4
                    mmr(ps[A_[d]][:, reg(c)], PTt.c(c), vtm.c(c), r=[PTt.k(d), vtm.k(d)], w=[psk(A_[d])])
                    mmr(ps[B_[d]][:, reg(c)], ktm.c(c), PTt.c(c), r=[PTt.k(d), ktm.k(d)], w=[psk(B_[d])])
                for d in range(2):
                    cp("dve", wvu.h(d), ps[A_[d]][:, :], r=[psk(A_[d])], w=[wvu.k(d)])
                    cp("act", wkT.h(d), ps[B_[d]][:, :], r=[psk(B_[d])], w=[wkT.k(d)])
                tt("dve", Dg.v3, idb, bc3(eg), ALU.mult, r=["consts", "eg"], w=Dg.K())
                for d in range(2):
                    mm(ps[D_[d]][:, :], ones, Dg.h(d), r=[Dg.k(d), "consts"], w=[psk(D_[d])])
                    tt("dve", qdT.h(d), ps[D_[d]][:, :], qkv[:, d, 0:4, :], ALU.mult, r=[psk(D_[d]), ("qkv", d)], w=[qdT.k(d)])
                for c in range(8):
                    d = c // 4
                    mmr(ps[C_[d]][:, reg(c)], wkT.c(c), Sst.c(c), r=[wkT.k(d), Sst.k(d)], w=[psk(C_[d])])
                for d in range(2):
                    tt("dve", wvu.h(d), wvu.h(d), ps[C_[d]][:, :], ALU.subtract, r=[psk(C_[d]), wvu.k(d)], w=[wvu.k(d)])
                for c in range(8):
                    d = c // 4
                    mmr(ps[A_[d]][:, reg(c)], Sst.c(c), qdT.c(c), start=True, stop=False, r=[Sst.k(d), qdT.k(d)], w=[psk(A_[d])])
                    mmr(ps[A_[d]][:, reg(c)], wvu.c(c), aqkT.c(c), start=False, stop=True, r=[wvu.k(d), aqkT.k(d)], w=[psk(A_[d])])
                    mmr(ps[B_[d]][:, reg(c)], kdec.c(c), wvu.c(c), r=[kdec.k(d), wvu.k(d)], w=[psk(B_[d])])
                tt("pool", Sst.v3, Sst.v3, bc3(egl), ALU.mult, r=Sst.K() + ["egl"], w=Sst.K())
                for d in range(2):
                    tt("dve", Sst.h(d), Sst.h(d), ps[B_[d]][:, :], ALU.add, r=[psk(B_[d]), Sst.k(d)], w=[Sst.k(d)])
                    cp("act", ost.h(d), ps[A_[d]][:, :], r=[psk(A_[d])], w=[ost.k(d)])
                    tok0 = cd[d] * 128
                    dma("pool", dno[d, :, :, tok0:tok0 + 128].rearrange("h p t -> p h t"), ost.v3[:, d * 4:(d + 1) * 4, :],
                        r=[ost.k(d)], wa=[("dno", d)])
            fence()
            of_ = aview(0, [512]); ob_ = aview(512, [512]); sq3 = aview(1024, [512]); rs3 = aview(1536, [512]); sz = aview(2048, [512])
            gcol = pv[:, l * PV_N + PV_DNG:l * PV_N + PV_DNG + 1]
            for h in range(4):
                (wz,), wkz = wload([(wb_in[l, :, 2304 + h * 128:2304 + (h + 1) * 128], 8, keys)])
                for (t0, n) in NBLK:
                    dma("sp", of_[:, :n], dno[0, h, :, t0:t0 + n], r=[("dno", 0)], w=["of_"])
                    dma("sp", ob_[:, :n], dno[1, h, :, t0:t0 + n], r=[("dno", 1)], w=["ob_"])
                    tt("pool", of_[:, :n], of_[:, :n], ob_[:, :n], ALU.add, r=["of_", "ob_"], w=["of_"])
                    act(sq3[:, :n], of_[:, :n], AF.Square, r=["of_"], w=["sq3"])
                    bs = bank()
                    mm(ps[bs][:, :n], C["onesm128"], sq3[:, :n], r=["sq3", "consts"], w=[psk(bs)])
                    act(rs3[:, :n], ps[bs][:, :n], AF.Sqrt, r=[psk(bs), "eps"], w=["rs3"], bias=eps_t[:])
                    recip(rs3[:, :n], rs3[:, :n], r=["rs3"], w=["rs3"])
                    stt(of_[:, :n], of_[:, :n], gcol, rs3[:, :n], ALU.mult, ALU.mult, r=["of_", "rs3", "pv"], w=["of_"])
                    bz = bank()
                    proj(bz, 128, wz, 0, t0, n, wkz)
                    act(sz[:, :n], ps[bz][:, :n], AF.Silu, r=[psk(bz)], w=["sz"])
                    yb_, yk = ybuf()
                    tt("pool", yb_[:, :n], of_[:, :n], sz[:, :n], ALU.mult, r=["of_", "sz"], w=[yk])
                    dma("pool", ybr[1, h * 128:(h + 1) * 128, t0:t0 + n], yb_[:, :n], r=[yk], wa=[("ybr", 1)])

        ybufs = [sb("ybuf%d" % i, [128, 512], BF16) for i in range(3)]

        def ybuf():
            i = state.get("yb", 0)
            state["yb"] = (i + 1) % 3
            return ybufs[i], "ybuf%d" % i


        def phase_c(l, b, last):
            act_br = [i for i, nm in enumerate(("att", "dn", "ret", "sc")) if nm in enabled]
            for bi, (t0, n) in enumerate(NBLK):
                col = NB if bi == 0 else b
                for i in act_br:
                    dma("pool", ybl[:, i, :, :n], ybr[i, :, t0:t0 + n].rearrange("(c p) t -> p c t", p=128),
                        r=[("ybr", i)], w=[("ybl", i)])
                for m in range(8):
                    for ii, i in enumerate(act_br):
                        wgb, wk = wload_c(wb_mg[l, i, m], 12, 128, [("wb_mg", l, i, k) for k in range(12)])
                        wg = wgb[:, 0:8, :]; wbr = wgb[:, 8:12, :]
                        bg = bank(); by = bank()
                        proj(bg, 128, wg, 0, t0, n, wk)
                        for k in range(4):
                            mm(ps[by][:, :n], wbr[:, k, :], ybl[:, i, k, :n], start=(k == 0), stop=(k == 3), r=[wk, ("ybl", i)], w=[psk(by)])
                        sg, sk = t5()
                        act(sg[:, :n], ps[bg][:, :n], AF.Sigmoid, r=[psk(bg)], w=[sk])
                        lastb = (ii == len(act_br) - 1)
                        dst, dk = (ubf, ("ubf", m)) if lastb else (wkA, ("wkA", m))
                        if ii == 0:
                            tt("dve", dst[:, m, :n], ps[by][:, :n], sg[:, :n], ALU.mult, r=[psk(by), sk], w=[dk])
                        else:
                            tt("dve", sg[:, :n], ps[by][:, :n], sg[:, :n], ALU.mult, r=[psk(by), sk], w=[sk])
                            tt("pool", dst[:, m, :n], wkA[:, m, :n], sg[:, :n], ALU.add, r=[("wkA", m), sk], w=[dk])
                for mg in range(4):
                    wo, wk = wload_c(wb_out[l, mg], 8, 256, [("wb_out", l, k) for k in range(8)])
                    for m2 in range(2):
                        m = mg * 2 + m2
                        bo = bank()
                        for k in range(8):
                            mm(ps[bo][:, :n], wo[:, k, m2 * 128:(m2 + 1) * 128], ubf[:, k, :n], start=(k == 0), stop=(k == 7),
                               r=[wk] + [("ubf", kk) for kk in range(8)], w=[psk(bo)])
                        cp("act", wkA[:, m, :n], ps[bo][:, :n], r=[psk(bo)], w=[("wkA", m)])
                dma("sp", xblk[:, :, :n], xres_v[:, :, t0:t0 + n], r=[("xres", bi)], w=["xblk"])
                residual_block(l, col, 0, wkA[:, :, :n], n, [("wkA", m) for m in range(8)])
                norm_mod_block(l, col, 1, xblk[:, :, :n], t0, n, ["xblk"])
                for pg in range(16):
                    w1, wk = wload_c(wb_m1[l, pg], 8, 256, [("wb_m1", l, k, pg // 8) for k in range(8)])
                    for m2 in range(2):
                        m = pg * 2 + m2
                        bh = bank()
                        proj(bh, 128, w1, m2 * 128, t0, n, wk)
                        rr, rk = t5()
                        act(rr[:, :n], ps[bh][:, :n], AF.Relu, r=[psk(bh)], w=[rk])
                        tt("pool", hid[:, m, :n], rr[:, :n], rr[:, :n], ALU.mult, r=[rk], w=[("hid", m)])
                for mg in range(4):
                    b0 = bank(); b1 = bank()
                    bb = (b0, b1)
                    for kg in range(4):
                        w2, wk = wload_c(wb_m2[l, kg, mg], 8, 256, [("wb_m2", l, kg * 8 + k) for k in range(8)])
                        for m2 in range(2):
                            for k in range(8):
                                mm(ps[bb[m2]][:, :n], w2[:, k, m2 * 128:(m2 + 1) * 128], hid[:, kg * 8 + k, :n],
                                   start=(kg == 0 and k == 0), stop=(kg == 3 and k == 7),
                                   r=[wk, ("hid", kg * 8 + k)], w=[psk(bb[m2])])
                    for m2 in range(2):
                        cp("act", wkA[:, mg * 2 + m2, :n], ps[bb[m2]][:, :n], r=[psk(bb[m2])], w=[("wkA", mg * 2 + m2)])
                residual_block(l, col, 1, wkA[:, :, :n], n, [("wkA", m) for m in range(8)])
                dma("sp", xres_v[:, :, t0:t0 + n], xblk[:, :, :n], r=["xblk"], w=[("xres", bi)])

        for b in range(NB):
            for bi, (t0, n) in enumerate(NBLK):
                dma("sp", xres[:, t0:t0 + n], xT_in[b, :, t0:t0 + n], w=[("xres", bi)])
            for l in range(L):
                if os.environ.get("RESET_ROT", "0") == "1":
                    for kk_ in ("bank", "ws", "t5", "yb", "od", "odr", "sbk"):
                        state[kk_] = 0
                phase_a(l, b)
                fence()
                if "att" in enabled:
                    mixer_att(l, b)
                    fence()
                if "dn" in enabled:
                    mixer_dn(l, b)
                    fence()
                if "ret" in enabled:
                    mixer_ret(l, b)
                    fence()
                if "sc" in enabled:
                    mixer_sc(l, b)
                    fence()
                phase_c(l, b, l == L - 1)
                if "xl" in dbg_out and b == 0:
                    for bi, (t0, n) in enumerate(NBLK):
                        out_events.append(dma("sp", dbg_out["xl"][l, :, t0:t0 + n], xres[:, t0:t0 + n], r=[("xres", bi)]))
                fence()
            for bi, (t0, n) in enumerate(NBLK[1:]):
                out_events.append(dma("sp", outT[b, :, t0 - TC:t0 - TC + n], xres[:, t0:t0 + n], r=[("xres", bi + 1)]))
        P.finish(out_events)
        P.emit(st)
    return nc, P


def make_in_maps(inp, NB, ncores, L=4):
    pv, bc = _host_params(inp, L)
    cst = _consts()
    const_arr = np.ascontiguousarray(np.concatenate([cst[k] for k in CONST_ORDER], axis=1))
    cosT, sinT = _rope_tables()
    maps = []
    for core in range(ncores):
        bs = slice(core * NB, (core + 1) * NB)
        xcat = np.concatenate([inp["ctx"][bs], inp["x"][bs]], axis=1)
        xT = np.ascontiguousarray(xcat.transpose(0, 2, 1))
        call = np.concatenate([inp["c"][bs], inp["c_ctx"][None, :]], axis=0)
        cT = np.ascontiguousarray(call.reshape(NB + 1, 8, 128).transpose(2, 1, 0))
        maps.append({"xT_in": xT, "cT": cT, "pv": pv, "bc": bc, "consts": const_arr, "cosT": cosT, "sinT": sinT,
                     "w_mod": inp["w_mod"][:L], "w_in": inp["w_in"][:L], "w_branch": inp["w_branch"][:L],
                     "w_out": inp["w_out"][:L], "w_mlp_in": inp["w_mlp_in"][:L], "w_mlp_out": inp["w_mlp_out"][:L]})
    return maps


_CACHE = {}


def kernel(**inputs):
    inp = {k: np.asarray(v) for k, v in inputs.items()}
    B = inp["x"].shape[0]
    ncores = 8
    NB = B // ncores
    if "nc" not in _CACHE:
        _CACHE["nc"] = build(NB)[0]
    nc = _CACHE["nc"]
    maps = make_in_maps(inp, NB, ncores)
    res = run_bass_kernel_spmd(nc, maps, core_ids=list(range(ncores)))
    out = np.concatenate([r["outT"] for r in res.results], axis=0)
    return np.ascontiguousarray(out.transpose(0, 2, 1)).astype(np.float32)
```

```python
import math
import os
from contextlib import ExitStack
import numpy as np
import concourse.bass as bass
import concourse.mybir as mybir
from concourse.bass_utils import run_bass_kernel_spmd

F32 = mybir.dt.float32
BF16 = mybir.dt.bfloat16
AF = mybir.ActivationFunctionType
ALU = mybir.AluOpType

ENGS = ("pe", "act", "dve", "pool", "sp")
NDMA_SEM = 12

D = 1024
T = 2304
TC = 256
TL = 2048
NBLK = [(0, 256), (256, 512), (768, 512), (1280, 512), (1792, 512)]
N_IN = 10000
EPS = 1e-6
BIG = 1.0e5


class Prog:
    def __init__(self, nc):
        self.nc = nc
        self.ops = {e: [] for e in ENGS}
        self.cnt = {e: 0 for e in ENGS}
        self.clock = {e: {} for e in ENGS}
        self.snap = {}
        self.lastw = {}
        self.readers = {}
        self.dma_rr = {e: 0 for e in ENGS}
        self.dma_cnt = {}
        self.n_wait = 0
        self.fence_ev = None
        self.fence_keep = None

    def _need(self, eng, ev, need):
        k, v = ev
        if eng == "pe" and k == "pe":
            return
        if self.clock[eng].get(k, 0) >= v:
            return
        if need.get(k, 0) < v:
            need[k] = v

    def op(self, eng, fn, r=(), w=(), wa=(), dma=False):
        need = {}
        for b in r:
            for ev in self.lastw.get(b, ()):
                self._need(eng, ev, need)
        for b in w:
            for ev in self.lastw.get(b, ()):
                self._need(eng, ev, need)
            for ev in self.readers.get(b, ()):
                self._need(eng, ev, need)
        for b in wa:
            for ev in self.readers.get(b, ()):
                self._need(eng, ev, need)
        if self.fence_ev is not None:
            if not all(self.fence_keep(k) for k in list(r) + list(w) + list(wa)):
                self._need(eng, self.fence_ev, need)
        if dma:
            i = self.dma_rr[eng]
            self.dma_rr[eng] = (i + 1) % NDMA_SEM
            key = ("dma", eng, i)
            prev = self.dma_cnt.get(key, 0)
            if prev:
                self._need(eng, (key, prev), need)
            val = prev + 16
            self.dma_cnt[key] = val
            inc = 16
        else:
            key = eng
            self.cnt[eng] += 1
            val = self.cnt[eng]
            inc = 1
        waits = list(need.items())
        if waits:
            ck = dict(self.clock[eng])
            for k, v in waits:
                if ck.get(k, 0) < v:
                    ck[k] = v
                for k2, v2 in self.snap.get((k, v), {}).items():
                    if ck.get(k2, 0) < v2:
                        ck[k2] = v2
            self.clock[eng] = ck
            self.n_wait += len(waits)
        ev = (key, val)
        self.snap[ev] = self.clock[eng]
        self.ops[eng].append((waits, fn, key, inc))
        for b in w:
            self.lastw[b] = [ev]
            self.readers[b] = []
        for b in wa:
            self.lastw.setdefault(b, []).append(ev)
        for b in r:
            self.readers.setdefault(b, []).append(ev)
        return ev

    def barrier(self, fn, eng="dve", keep=lambda k: False):
        keys = [k for k in set(self.lastw) | set(self.readers) if not keep(k)]
        ev = self.op(eng, fn, (), keys)
        self.fence_ev = ev
        self.fence_keep = keep
        return ev

    def finish(self, out_events, eng="sp"):
        need = {}
        for ev in out_events:
            self._need(eng, ev, need)
        self.ops[eng].append((list(need.items()), None, None, 0))

    def emit(self, stack):
        nc = self.nc
        keys = []
        for e in ENGS:
            for waits, fn, key, inc in self.ops[e]:
                if key is not None and key not in keys:
                    keys.append(key)
        semh = {}
        for k in keys:
            nm = "s_" + ("_".join(map(str, k)) if isinstance(k, tuple) else k)
            semh[k] = stack.enter_context(nc.semaphore(nm))
        block = stack.enter_context(nc.Block())
        engmap = {"pe": block.tensor, "act": block.scalar, "dve": block.vector,
                  "pool": block.gpsimd, "sp": block.sync}
        for e in ENGS:
            def body(eng, ops=self.ops[e]):
                for waits, fn, key, inc in ops:
                    for k, v in waits:
                        eng.wait_ge(semh[k], v)
                    if fn is not None:
                        fn(eng).then_inc(semh[key], inc)
            engmap[e](body)


def _consts():
    c = {}
    p = np.arange(128)[:, None]
    j = np.arange(128)[None, :]
    c["ident"] = (p == j).astype(np.float32)
    c["ones"] = np.ones((128, 128), np.float32)
    blk = ((p // 64) == (j // 64)).astype(np.float32)
    c["blk64m"] = blk / 64.0
    c["onesm128"] = np.ones((128, 128), np.float32) / 128.0
    R = np.zeros((128, 128), np.float32)
    for m in range(128):
        if m % 64 < 32:
            R[m + 32, m] = -1.0
        else:
            R[m - 32, m] = 1.0
    c["rrot"] = R
    c["tri_f"] = (p <= j).astype(np.float32)
    c["tri_b"] = (p >= j).astype(np.float32)
    c["offdiag"] = (p != j).astype(np.float32)
    mf = (j > p).astype(np.float32) * BIG
    mb = (j < p).astype(np.float32) * BIG
    c["mask_s"] = np.concatenate([np.tile(mf[:, None, :], (1, 4, 1)), np.tile(mb[:, None, :], (1, 4, 1))], 1).reshape(128, 1024)
    c["mask_t"] = np.concatenate([np.tile(mb[:, None, :], (1, 4, 1)), np.tile(mf[:, None, :], (1, 4, 1))], 1).reshape(128, 1024)
    jj = np.arange(512)[None, :]
    c["dbase"] = (jj - p).astype(np.float32)
    def bd(m):
        return ((p // m) == (j // m)).astype(np.float32)
    c["bd8"] = bd(8)
    c["off8"] = bd(16) - bd(8)
    c["off16"] = bd(32) - bd(16)
    c["off32"] = bd(64) - bd(32)
    c["off64"] = bd(128) - bd(64)
    e = np.zeros((128, 128), np.float32); e[:, :64] = 1
    c["ones_e"] = e
    c["ones_o"] = 1 - e
    return c


CONST_ORDER = ["ident", "ones", "blk64m", "onesm128", "rrot", "tri_f", "tri_b", "offdiag", "mask_s", "mask_t",
               "dbase", "ones_e", "ones_o", "bd8", "off8", "off16", "off32", "off64"]


def _rope_tables():
    rows = TL // 64
    r, col = np.meshgrid(np.arange(rows), np.arange(64), indexing="ij")
    quarter = 16
    inv_freq = (10000.0 ** (-np.arange(quarter, dtype=np.float32) / quarter)).astype(np.float32)
    ang = np.concatenate([r.reshape(-1, 1).astype(np.float32) * inv_freq,
                          col.reshape(-1, 1).astype(np.float32) * inv_freq], axis=-1)
    cos = np.cos(ang).astype(np.float32).T
    sin = np.sin(ang).astype(np.float32).T
    cosT = np.concatenate([cos, cos, cos, cos], 0)
    sinT = np.concatenate([sin, sin, sin, sin], 0)
    return np.ascontiguousarray(cosT), np.ascontiguousarray(sinT)


PV_G = 0
PV_DNC = 32
PV_SCC = 68
PV_QG = 80
PV_KG = 81
PV_DNG = 82
PV_BM = 83
PV_N = 131
BC_ALOG = 0
BC_DTB = 8
BC_RET = 16
BC_N = 24


def _host_params(inp, L):
    pv = np.zeros((128, L * PV_N), np.float32)
    bc = np.zeros((128, L * BC_N), np.float32)
    for l in range(L):
        o = l * PV_N
        pv[:, o + PV_G:o + PV_G + 32] = inp["g_norm"][l].reshape(32, 128).T
        pv[:, o + PV_DNC:o + PV_DNC + 36] = inp["dn_conv"][l].reshape(36, 128).T
        pv[:, o + PV_SCC:o + PV_SCC + 12] = inp["sc_conv"][l].reshape(12, 128).T
        pv[:, o + PV_QG] = np.concatenate([inp["att_q_gain"][l]] * 2)
        pv[:, o + PV_KG] = np.concatenate([inp["att_k_gain"][l]] * 2)
        pv[:, o + PV_DNG] = inp["dn_norm_gain"][l]
        pv[:, o + PV_BM:o + PV_BM + 48] = inp["b_mod"][l].reshape(48, 128).T
        ob = l * BC_N
        bc[:, ob + BC_ALOG:ob + BC_ALOG + 8] = inp["dn_a_log"][l].reshape(1, 8)
        bc[:, ob + BC_DTB:ob + BC_DTB + 8] = inp["dn_dt_bias"][l].reshape(1, 8)
        bc[:, ob + BC_RET:ob + BC_RET + 8] = inp["ret_decay"][l].reshape(1, 8)
    return pv, bc


def build(NB, L=4, enabled=("att", "dn", "ret", "sc"), dbg=()):
    nc = bass.Bass("TRN2", target_bir_lowering=False)
    P = Prog(nc)
    NCOL = NB + 1
    dram = lambda name, shape, dt=F32, kind="ExternalInput": nc.dram_tensor(name, shape, dt, kind=kind).ap()
    xT_in = dram("xT_in", [NB, D, T])
    cT_in = dram("cT", [128, 8, NCOL])
    pv_in = dram("pv", [128, L * PV_N])
    bc_in = dram("bc", [128, L * BC_N])
    cst = _consts()
    cwid = {k: cst[k].shape[1] for k in CONST_ORDER}
    coff = {}
    o = 0
    for k in CONST_ORDER:
        coff[k] = o
        o += cwid[k]
    NCONST = o
    const_in = dram("consts", [128, NCONST])
    cos_in = dram("cosT", [128, TL])
    sin_in = dram("sinT", [128, TL])
    w_mod = dram("w_mod", [L, D, 6 * D])
    w_in = dram("w_in", [L, D, N_IN])
    w_branch = dram("w_branch", [L, 4, 512, D])
    w_out = dram("w_out", [L, D, D])
    w_m1 = dram("w_mlp_in", [L, D, 4 * D])
    w_m2 = dram("w_mlp_out", [L, 4 * D, D])
    outT = dram("outT", [NB, D, TL], kind="ExternalOutput")
    dbg_out = {}
    for name, shape in dbg:
        dbg_out[name] = dram("dbg_" + name, shape, kind="ExternalOutput")
    wb_in = dram("wb_in", [L, D, N_IN], BF16, kind="Internal")
    wb_mg = dram("wb_mg", [L, 4, 8, 128, 12, 128], BF16, kind="Internal")
    wb_out = dram("wb_out", [L, 4, 128, 8, 256], BF16, kind="Internal")
    wb_m1 = dram("wb_m1", [L, 16, 128, 8, 256], BF16, kind="Internal")
    wb_m2 = dram("wb_m2", [L, 4, 4, 128, 8, 256], BF16, kind="Internal")
    xres = dram("xres", [D, T], F32, kind="Internal")
    ybr = dram("ybr", [4, 512, T], BF16, kind="Internal")
    dnq = dram("dnq", [12, 128, T], F32, kind="Internal")
    dno = dram("dno", [2, 4, 128, T], F32, kind="Internal")

    out_events = []
    with ExitStack() as st:
        def sb(name, shape, dt=F32):
            return st.enter_context(nc.sbuf_tensor(name, shape, dt))

        hT = sb("hT", [128, 8, T], BF16)
        consts = sb("consts_sb", [128, NCONST])
        C = {k: consts[:, coff[k]:coff[k] + cwid[k]] for k in CONST_ORDER}
        pv = sb("pv_sb", [128, L * PV_N])
        bc = sb("bc_sb", [128, L * BC_N])
        modT = sb("modT", [128, L, 48, NCOL])
        modA = sb("modA", [128, L, 4, 8, NCOL])
        cTs = sb("cTs", [128, 8, NCOL])
        eps_t = sb("eps_t", [128, 1])
        ones_bf = sb("ones_bf", [128, 128], BF16)
        ones_eo = sb("ones_eo", [128, 2, 128], BF16)
        NW = 4
        wslot = [sb("wslot%d" % i, [128, 2048], BF16) for i in range(NW)]
        AR_WORDS = 26624
        arena = sb("arena", [128, AR_WORDS])
        dummy = sb("fdummy", [128, 8])

        def aview(off_w, shape, dt=F32):
            n = int(np.prod(shape))
            words = n if dt == F32 else (n + 1) // 2
            assert off_w + words <= AR_WORDS, (off_w, words)
            v = arena[:, off_w:off_w + words]
            if dt != F32:
                v = v.bitcast(dt)
            if len(shape) == 2:
                v = v.rearrange("p (a b) -> p a b", b=shape[1])
            elif len(shape) == 3:
                v = v.rearrange("p (a b c) -> p a b c", b=shape[1], c=shape[2])
            elif len(shape) == 4:
                v = v.rearrange("p (a b c d) -> p a b c d", b=shape[1], c=shape[2], d=shape[3])
            return v

        def fence():
            keep = lambda k: k in ("consts", "cos", "sin", "pv", "bc", "eps", "ones_bf", "ones_eo", "hT") or (
                isinstance(k, tuple) and k[0] in ("ws", "wb_in", "wb_mg", "wb_out", "wb_m1", "wb_m2", "modT", "modA", "xres", "ybr"))
            P.barrier(lambda e: e.memset(dummy[:], 0.0), "dve", keep)
            for kk_ in ("bank", "t5", "yb", "od", "odr", "sbk"):
                state[kk_] = 0

        wst = aview(0, [2, 8, 512])
        xblk = aview(0, [8, 512])
        wkA = aview(4096, [8, 512])
        wkB = aview(8192, [8, 512])
        ybl = aview(12288, [4, 4, 512], BF16)
        ubf = aview(16384, [8, 512], BF16)
        hid = aview(18432, [32, 512], BF16)
        rsA = sb("rsA", [128, 512])
        t512 = [sb("t512_%d" % i, [128, 512]) for i in range(6)]
        ps = [st.enter_context(nc.psum_tensor("ps%d" % i, [128, 512], F32)) for i in range(8)]
        state = {"bank": 0, "ws": 0, "t5": 0}

        def bank():
            b = state["bank"]
            state["bank"] = (b + 1) % 8
            return b

        def psk(b):
            return "ps%d" % b

        def t5():
            i = state["t5"]
            state["t5"] = (i + 1) % 6
            return t512[i], "t512_%d" % i

        def mm(out, lhsT, rhs, start=True, stop=True, r=(), w=()):
            return P.op("pe", lambda e: e.matmul(out, lhsT=lhsT, rhs=rhs, start=start, stop=stop), r, w)

        F32R = mybir.dt.float32r

        def mmr(out, lhsT, rhs, start=True, stop=True, r=(), w=()):
            if True:
                return mm(out, lhsT, rhs, start, stop, r, w)
            l2 = lhsT.bitcast(F32R); r2 = rhs.bitcast(F32R)
            return P.op("pe", lambda e: e.matmul(out, lhsT=l2, rhs=r2, start=start, stop=stop), r, w)

        def tr(out, in_, ident, r=(), w=()):
            return P.op("pe", lambda e: e.transpose(out=out, in_=in_, identity=ident), r, w)

        def act(out, in_, func, r=(), w=(), scale=1.0, bias=None, wa=()):
            if bias is None:
                return P.op("act", lambda e: e.activation(out=out, in_=in_, func=func, scale=scale), r, w, wa)
            return P.op("act", lambda e: e.activation(out=out, in_=in_, func=func, scale=scale, bias=bias), r, w, wa)

        def tt(eng, out, in0, in1, op, r=(), w=(), wa=()):
            return P.op(eng, lambda e: e.tensor_tensor(out=out, in0=in0, in1=in1, op=op), r, w, wa)

        def tsc(eng, out, in0, s1, op0, s2=None, op1=None, r=(), w=(), wa=()):
            if op1 is None:
                return P.op(eng, lambda e: e.tensor_scalar(out=out, in0=in0, scalar1=s1, scalar2=None, op0=op0), r, w, wa)
            return P.op(eng, lambda e: e.tensor_scalar(out=out, in0=in0, scalar1=s1, scalar2=s2, op0=op0, op1=op1), r, w, wa)

        def stt(out, in0, scalar, in1, op0, op1, r=(), w=(), wa=()):
            return P.op("dve", lambda e: e.scalar_tensor_tensor(out=out, in0=in0, scalar=scalar, in1=in1, op0=op0, op1=op1), r, w, wa)

        def cp(eng, out, in_, r=(), w=(), wa=()):
            if eng == "act":
                return act(out, in_, AF.Copy, r, w, wa=wa)
            return P.op(eng, lambda e: e.tensor_copy(out=out, in_=in_), r, w, wa)

        def recip(out, in_, r=(), w=()):
            return P.op("dve", lambda e: e.reciprocal(out=out, in_=in_), r, w)

        def dma(q, out, in_, r=(), w=(), wa=()):
            return P.op(q, lambda e: e.dma_start(out=out, in_=in_), r, w, wa, dma=True)

        def memset(eng, ap, val, w=()):
            return P.op(eng, lambda e: e.memset(ap, val), (), w)

        dma("sp", consts[:], const_in, w=["consts"])
        dma("sp", pv[:], pv_in, w=["pv"])
        dma("sp", bc[:], bc_in, w=["bc"])
        dma("sp", cTs[:], cT_in, w=["cTs"])
        memset("dve", eps_t[:], EPS, w=["eps"])
        cp("dve", ones_bf[:], C["ones"], r=["consts"], w=["ones_bf"])
        cp("dve", ones_eo[:, 0, :], C["ones_e"], r=["consts"], w=["ones_eo"])
        cp("dve", ones_eo[:, 1, :], C["ones_o"], r=["consts"], w=["ones_eo"])

        def cast_jobs(l):
            jobs = []
            for k in range(8):
                rows = slice(k * 128, (k + 1) * 128)
                jobs.append((wb_in[l, rows, 0:5904], w_in[l, rows, 0:5904], ("wb_in", l, k)))
            for k in range(8):
                rows = slice(k * 128, (k + 1) * 128)
                for i in range(4):
                    jobs.append((wb_mg[l, i, :, :, k, :].rearrange("m p c -> p m c"),
                                 w_in[l, rows, 5904 + i * 1024:5904 + (i + 1) * 1024].rearrange("p (m c) -> p m c", c=128),
                                 ("wb_mg", l, i, k)))
            for i in range(4):
                for k in range(4):
                    jobs.append((wb_mg[l, i, :, :, 8 + k, :].rearrange("m p c -> p m c"),
                                 w_branch[l, i, k * 128:(k + 1) * 128, :].rearrange("p (m c) -> p m c", c=128),
                                 ("wb_mg", l, i, 8 + k)))
            for k in range(8):
                rows = slice(k * 128, (k + 1) * 128)
                jobs.append((wb_out[l, :, :, k, :].rearrange("g p m -> p g m"),
                             w_out[l, rows, :].rearrange("p (g m) -> p g m", m=256), ("wb_out", l, k)))
            for k in range(8):
                rows = slice(k * 128, (k + 1) * 128)
                for hf in range(2):
                    jobs.append((wb_m1[l, hf * 8:(hf + 1) * 8, :, k, :].rearrange("g p m -> p g m"),
                                 w_m1[l, rows, hf * 2048:(hf + 1) * 2048].rearrange("p (g m) -> p g m", m=256), ("wb_m1", l, k, hf)))
            for r_ in range(32):
                kg, k = r_ // 8, r_ % 8
                jobs.append((wb_m2[l, kg, :, :, k, :].rearrange("g p m -> p g m"),
                             w_m2[l, r_ * 128:(r_ + 1) * 128, :].rearrange("p (g m) -> p g m", m=256), ("wb_m2", l, r_)))
            return jobs

        pending_casts = {}

        def cast_some(l, frac_idx, nfrac):
            if l >= L:
                return
            if l not in pending_casts:
                pending_casts[l] = cast_jobs(l)
            jobs = pending_casts[l]
            n_ = len(jobs)
            lo_, hi_ = (n_ * frac_idx) // nfrac, (n_ * (frac_idx + 1)) // nfrac
            for (dst, src, key) in jobs[lo_:hi_]:
                dma("pool", dst, src, w=[key])

        cast_some(0, 0, 1)

        act(cTs[:], cTs[:], AF.Silu, r=["cTs"], w=["cTs"])
        for l in range(L):
            mb = bank()
            mps = ps[mb][:, 0:48 * NCOL].rearrange("p (j c) -> p j c", c=NCOL)
            for g in range(12):
                s = g % 2
                dma("sp", wst[:, s], w_mod[l, :, g * 512:(g + 1) * 512].rearrange("(k p) m -> p k m", p=128), w=[("wst", s)])
                for j4 in range(4):
                    j = g * 4 + j4
                    for k in range(8):
                        mm(mps[:, j, :], wst[:, s, k, j4 * 128:(j4 + 1) * 128], cTs[:, k, :], start=(k == 0), stop=(k == 7),
                           r=[("wst", s), "cTs"], w=[psk(mb)])
            bm = pv[:, l * PV_N + PV_BM:l * PV_N + PV_BM + 48]
            tt("dve", modT[:, l], mps, bm.unsqueeze(2).to_broadcast([128, 48, NCOL]), ALU.add, r=[psk(mb), "pv"], w=[("modT", l)])
            gl = pv[:, l * PV_N + PV_G:l * PV_N + PV_G + 32]
            for idx, (gi, mo, plus1) in enumerate([(0, 8, True), (2, 32, True), (1, 16, False), (3, 40, False)]):
                src = modT[:, l, mo:mo + 8, :]
                gb = gl[:, gi * 8:(gi + 1) * 8].unsqueeze(2).to_broadcast([128, 8, NCOL])
                if plus1:
                    tsc("dve", modA[:, l, idx], src, 1.0, ALU.add, r=[("modT", l)], w=[("modA", l, idx)])
                    tt("dve", modA[:, l, idx], modA[:, l, idx], gb, ALU.mult, r=[("modA", l, idx), "pv"], w=[("modA", l, idx)])
                else:
                    tt("dve", modA[:, l, idx], src, gb, ALU.mult, r=[("modT", l), "pv"], w=[("modA", l, idx)])

        fence()
        if "modT" in dbg_out:
            out_events.append(dma("sp", dbg_out["modT"], modT[:], r=[("modT", l) for l in range(L)]))

        def wslot_next():
            s = state["ws"]
            state["ws"] = (s + 1) % NW
            return s

        def wload(parts):
            s = wslot_next()
            key = ("ws", s)
            views = []
            off = 0
            first = True
            for ap, nk, rk in parts:
                M = ap.shape[1]
                v = wslot[s][:, off:off + nk * M].rearrange("p (k m) -> p k m", m=M)
                src = ap.rearrange("(k p) m -> p k m", p=128)
                if first:
                    dma("sp", v, src, r=rk, w=[key])
                else:
                    dma("sp", v, src, r=rk, wa=[key])
                first = False
                views.append(v)
                off += nk * M
            assert off <= 2048
            return views, key

        def wload_c(src, nk, M, rk):
            s_ = wslot_next()
            key = ("ws", s_)
            v = wslot[s_][:, 0:nk * M].rearrange("p (k m) -> p k m", m=M)
            dma("sp", v, src, r=rk, w=[key])
            return v, key

        def win_keys(l):
            return [("wb_in", l, k) for k in range(8)]

        def proj(b, M, wv, m0, t0, n, wkey):
            for k in range(8):
                mm(ps[b][:M, :n], wv[:, k, m0:m0 + M], hT[:, k, t0:t0 + n], start=(k == 0), stop=(k == 7),
                   r=[wkey, "hT"], w=[psk(b)])

        def rms_rstd(src, n, rkeys):
            act(wkB[:, :, :n], src, AF.Square, r=rkeys, w=["wkB"])
            b = bank()
            for k in range(8):
                mmr(ps[b][:, :n], C["ones"], wkB[:, k, :n], start=(k == 0), stop=(k == 7), r=["wkB", "consts"], w=[psk(b)])
            act(rsA[:, :n], ps[b][:, :n], AF.Sqrt, r=[psk(b), "eps"], w=["rsA"], scale=1.0 / D, bias=eps_t[:])
            recip(rsA[:, :n], rsA[:, :n], r=["rsA"], w=["rsA"])

        def mod_col(l, idx, k, col):
            return modA[:, l, idx, k, col:col + 1]

        def norm_mod_block(l, col, which, src, t0, n, rkeys):
            rms_rstd(src, n, rkeys)
            tt("dve", wkB[:, :, :n], src, rsA[:, :n].unsqueeze(1).to_broadcast([128, 8, n]), ALU.mult,
               r=list(rkeys) + ["rsA"], w=["wkB"])
            sh0 = 0 if which == 0 else 24
            for k in range(8):
                act(hT[:, k, t0:t0 + n], wkB[:, k, :n], AF.Identity, r=["wkB", ("modA", l, which), ("modT", l)], w=["hT"],
                    scale=mod_col(l, which, k, col), bias=modT[:, l, sh0 + k, col:col + 1])

        def residual_block(l, col, which, ysrc, n, ykeys):
            rms_rstd(ysrc, n, ykeys)
            tt("dve", wkB[:, :, :n], ysrc, rsA[:, :n].unsqueeze(1).to_broadcast([128, 8, n]), ALU.mult,
               r=list(ykeys) + ["rsA"], w=["wkB"])
            for k in range(8):
                stt(xblk[:, k, :n], wkB[:, k, :n], mod_col(l, 2 + which, k, col), xblk[:, k, :n], ALU.mult, ALU.add,
                    r=["wkB", "xblk", ("modA", l, 2 + which)], w=["xblk"])

        xres_v = xres.rearrange("(k p) t -> p k t", p=128)

        from_mixers = {}

        def phase_a(l, b):
            for bi, (t0, n) in enumerate(NBLK):
                col = NB if bi == 0 else b
                dma("sp", xblk[:, :, :n], xres_v[:, :, t0:t0 + n], r=[("xres", bi)], w=["xblk"])
                norm_mod_block(l, col, 0, xblk[:, :, :n], t0, n, ["xblk"])

        tscv = aview(0, [T + 4])

        def seg_off(t0):
            return t0 + 1 if t0 < TC else t0 + 3

        def mixer_sc(l, b):
            base = 4368
            memset("pool", tscv, 0.0, w=["tsc"])
            for c in range(4):
                (wc, wx), k1 = wload([(wb_in[l, :, base + 512 + c * 128: base + 512 + (c + 1) * 128], 8, win_keys(l)),
                                      (wb_in[l, :, base + 1024 + c * 128: base + 1024 + (c + 1) * 128], 8, win_keys(l))])
                for (t0, n) in NBLK:
                    b1 = bank(); b2 = bank()
                    proj(b1, 128, wc, 0, t0, n, k1)
                    proj(b2, 128, wx, 0, t0, n, k1)
                    xs, xk = t5()
                    cp("act", xs[:, :n], ps[b2][:, :n], r=[psk(b2)], w=[xk])
                    so = seg_off(t0)
                    tt("dve", tscv[:, so:so + n], ps[b1][:, :n], xs[:, :n], ALU.mult, r=[psk(b1), xk], w=["tsc"])
                (wbv,), k2 = wload([(wb_in[l, :, base + c * 128: base + (c + 1) * 128], 8, win_keys(l))])
                wcol = lambda tap: pv[:, l * PV_N + PV_SCC + tap * 4 + c: l * PV_N + PV_SCC + tap * 4 + c + 1]
                for (t0, n) in NBLK:
                    b1 = bank()
                    proj(b1, 128, wbv, 0, t0, n, k2)
                    so = seg_off(t0)
                    cv, ck = t5()
                    tsc("dve", cv[:, :n], tscv[:, so - 1:so - 1 + n], wcol(0), ALU.mult, r=["tsc", "pv"], w=[ck])
                    stt(cv[:, :n], tscv[:, so:so + n], wcol(1), cv[:, :n], ALU.mult, ALU.add, r=["tsc", "pv", ck], w=[ck])
                    stt(cv[:, :n], tscv[:, so + 1:so + 1 + n], wcol(2), cv[:, :n], ALU.mult, ALU.add, r=["tsc", "pv", ck], w=[ck])
                    yb_, yk = ybuf()
                    tt("dve", yb_[:, :n], ps[b1][:, :n], cv[:, :n], ALU.mult, r=[psk(b1), ck], w=[yk])
                    dma("pool", ybr[3, c * 128:(c + 1) * 128, t0:t0 + n], yb_[:, :n], r=[yk], wa=[("ybr", 3)])

        def mixer_att(l, b):
            QT = aview(0, [4, T], BF16)
            KT = aview(4608, [2, T], BF16)
            VP = aview(6912, [18, 2, 2, 128], BF16)
            PT = [aview(11520 + i * 256, [512], BF16) for i in range(4)]
            qn = aview(12544, [512]); sqb = aview(13056, [512]); rsb = aview(13568, [512])
            t1 = aview(14080, [512]); t2 = aview(14592, [512]); rD = aview(15104, [512])
            cosT = aview(15616, [TL]); sinT = aview(17664, [TL])
            dma("sp", cosT, cos_in, w=["cos"])
            dma("sp", sinT, sin_in, w=["sin"])
            memset("pool", VP, 0.0, w=["VP"])
            keys = win_keys(l)
            for kind, ci in [("q", 0), ("q", 1), ("q", 2), ("q", 3), ("k", 0), ("k", 1)]:
                if kind == "q":
                    (wv,), wk = wload([(wb_in[l, :, ci * 128:(ci + 1) * 128], 8, keys)])
                    gain = pv[:, l * PV_N + PV_QG:l * PV_N + PV_QG + 1]
                    dst = QT[:, ci]; dkey = "QT"
                else:
                    s_ = wslot_next(); wk = ("ws", s_)
                    wv = wslot[s_][:, 0:1024].rearrange("p (k m) -> p k m", m=128)
                    src = wb_in[l, :, 512 + ci * 64:512 + (ci + 1) * 64].rearrange("(k p) m -> p k m", p=128)
                    dma("sp", wv[:, :, 0:64], src, r=keys, w=[wk])
                    dma("sp", wv[:, :, 64:128], src, r=keys, wa=[wk])
                    gain = pv[:, l * PV_N + PV_KG:l * PV_N + PV_KG + 1]
                    dst = KT[:, ci]; dkey = "KT"
                for bi, (t0, n) in enumerate(NBLK):
                    bq = bank()
                    proj(bq, 128, wv, 0, t0, n, wk)
                    act(sqb[:, :n], ps[bq][:, :n], AF.Square, r=[psk(bq)], w=["sqb"])
                    bs = bank()
                    mm(ps[bs][:, :n], C["blk64m"], sqb[:, :n], r=["sqb", "consts"], w=[psk(bs)])
                    act(rsb[:, :n], ps[bs][:, :n], AF.Sqrt, r=[psk(bs), "eps"], w=["rsb"], bias=eps_t[:])
                    recip(rsb[:, :n], rsb[:, :n], r=["rsb"], w=["rsb"])
                    if bi == 0:
                        stt(dst[:, t0:t0 + n], ps[bq][:, :n], gain, rsb[:, :n], ALU.mult, ALU.mult,
                            r=[psk(bq), "rsb", "pv"], wa=[dkey])
                    else:
                        stt(qn[:, :n], ps[bq][:, :n], gain, rsb[:, :n], ALU.mult, ALU.mult, r=[psk(bq), "rsb", "pv"], w=["qn"])
                        br = bank()
                        mm(ps[br][:, :n], C["rrot"], qn[:, :n], r=["qn", "consts"], w=[psk(br)])
                        lo = t0 - TC
                        tt("pool", t1[:, :n], qn[:, :n], cosT[:, lo:lo + n], ALU.mult, r=["qn", "cos"], w=["t1"])
                        tt("dve", t2[:, :n], ps[br][:, :n], sinT[:, lo:lo + n], ALU.mult, r=[psk(br), "sin"], w=["t2"])
                        tt("pool", dst[:, t0:t0 + n], t1[:, :n], t2[:, :n], ALU.add, r=["t1", "t2"], wa=[dkey])
            import os
            ATT_STOP = int(os.environ.get("ATT_STOP", "9"))
            if ATT_STOP <= 1:
                return
            (wvv,), wk = wload([(wb_in[l, :, 640:768], 8, keys)])
            for tg in range(5):
                tts = list(range(tg * 4, min(18, tg * 4 + 4)))
                bv = bank()
                for j, tti in enumerate(tts):
                    for k in range(8):
                        mm(ps[bv][:, j * 128:(j + 1) * 128], hT[:, k, tti * 128:(tti + 1) * 128], wvv[:, k, :],
                           start=(k == 0), stop=(k == 7), r=[wk, "hT"], w=[psk(bv)])
                nt = len(tts)
                pv4 = ps[bv][:, :nt * 128].rearrange("p (j m) -> p j m", m=128)
                VCOPY = os.environ.get("VCOPY", "act,act")
                for g in range(2):
                    for e in range(2):
                        if VCOPY == "none":
                            continue
                        cp(VCOPY.split(",")[(g + e) % 2], VP[:, tg * 4:tg * 4 + nt, g, e, e * 64:(e + 1) * 64],
                           pv4[:, :, g * 64:(g + 1) * 64], r=[psk(bv)], wa=["VP"])
            if ATT_STOP <= 2:
                return
            for bi, (t0, n) in enumerate(NBLK):
                ktiles = [0, 1] if bi == 0 else list(range(18))
                for c in range(4):
                    g = c // 2
                    od = state.get("od", 0); state["od"] = 1 - od
                    bo, bd = 4 + 2 * od, 5 + 2 * od
                    steps = [(kt, e) for kt in ktiles for e in (0, 1)]
                    ns = len(steps)
                    SK = 2
                    for i in range(ns + SK):
                        if i < ns:
                            kt, e = steps[i]
                            bsx = state.get("sbk", 0); state["sbk"] = (bsx + 1) % 4
                            mm(ps[bsx][:, :n], KT[e * 64:(e + 1) * 64, g, kt * 128:(kt + 1) * 128],
                               QT[e * 64:(e + 1) * 64, c, t0:t0 + n], r=["KT", "QT"], w=[psk(bsx)])
                            act(PT[i % 4][:, :n], ps[bsx][:, :n], AF.Exp, r=[psk(bsx)], w=[("PT", i % 4)], scale=0.125)
                        if i >= SK:
                            ii = i - SK
                            kt, e = steps[ii]
                            pt = PT[ii % 4]
                            mm(ps[bo][:, :n], VP[:, kt, g, e, :], pt[:, :n], start=(ii == 0), stop=(ii == ns - 1),
                               r=[("PT", ii % 4), "VP"], w=[psk(bo)])
                            mm(ps[bd][:, :n], ones_eo[:, e, :], pt[:, :n], start=(ii == 0), stop=(ii == ns - 1),
                               r=[("PT", ii % 4), "ones_eo"], w=[psk(bd)])
                    recip(rD[:, :n], ps[bd][:, :n], r=[psk(bd)], w=["rD"])
                    yb_, yk = ybuf()
                    tt("dve", yb_[:, :n], ps[bo][:, :n], rD[:, :n], ALU.mult, r=[psk(bo), "rD"], w=[yk])
                    dma("pool", ybr[0, c * 128:(c + 1) * 128, t0:t0 + n], yb_[:, :n], r=[yk], wa=[("ybr", 0)])

        lgE = sb("lgE", [128, 8]); lgN = sb("lgN", [128, 8])
        OFF = 1920
        FW = 3968

        def mixer_ret(l, b):
            RQ = aview(0, [2, T], BF16)
            RK = aview(2304, [2, T], BF16)
            RV = aview(4608, [18, 512], BF16)
            Fm = aview(9216, [FW])
            PT = [aview(13184 + i * 256, [512], BF16) for i in range(4)]
            o_ = 14208
            qn = aview(o_, [512]); t1 = aview(o_ + 512, [512]); t2 = aview(o_ + 1024, [512]); dd = aview(o_ + 1536, [512])
            u2 = aview(o_ + 2048, [512]); msk = aview(o_ + 2560, [512]); osb = aview(o_ + 3072, [512]); sqo = aview(o_ + 3584, [512])
            mean_s = aview(o_ + 4096, [512]); tmpv = aview(o_ + 4608, [512]); rso = aview(o_ + 5120, [512]); sg = aview(o_ + 5632, [512])
            dd2 = aview(o_ + 6144, [512]); e1 = aview(o_ + 6656, [512])
            cmsk = [msk, aview(o_ + 7168 + 2 * TL, [512])]
            cosT = aview(o_ + 7168, [TL]); sinT = aview(o_ + 7168 + TL, [TL])
            dma("sp", cosT, cos_in, w=["cos"])
            dma("sp", sinT, sin_in, w=["sin"])
            keys = win_keys(l)
            dbase = C["dbase"]
            bcr = bc[:, l * BC_N + BC_RET:l * BC_N + BC_RET + 8]
            act(lgE[:], bcr, AF.Exp, r=["bc"], w=["lgE"])
            tsc("dve", lgN[:], lgE[:], -1.0, ALU.mult, r=["lgE"], w=["lgN"])
            for kind, ci in [("q", 0), ("q", 1), ("k", 0), ("k", 1)]:
                c0 = (2832 if kind == "q" else 3088) + ci * 128
                (wv,), wk = wload([(wb_in[l, :, c0:c0 + 128], 8, keys)])
                dst = (RQ if kind == "q" else RK)[:, ci]
                dkey = "RQ" if kind == "q" else "RK"
                sc_ = 1.0 if kind == "q" else 0.125
                for bi, (t0, n) in enumerate(NBLK):
                    bq = bank()
                    proj(bq, 128, wv, 0, t0, n, wk)
                    if bi == 0:
                        act(dst[:, t0:t0 + n], ps[bq][:, :n], AF.Copy, r=[psk(bq)], wa=[dkey], scale=sc_)
                    else:
                        act(qn[:, :n], ps[bq][:, :n], AF.Copy, r=[psk(bq)], w=["qn"], scale=sc_)
                        br = bank()
                        mm(ps[br][:, :n], C["rrot"], qn[:, :n], r=["qn", "consts"], w=[psk(br)])
                        lo = t0 - TC
                        tt("pool", t1[:, :n], qn[:, :n], cosT[:, lo:lo + n], ALU.mult, r=["qn", "cos"], w=["t1"])
                        tt("dve", t2[:, :n], ps[br][:, :n], sinT[:, lo:lo + n], ALU.mult, r=[psk(br), "sin"], w=["t2"])
                        tt("pool", dst[:, t0:t0 + n], t1[:, :n], t2[:, :n], ALU.add, r=["t1", "t2"], wa=[dkey])
            for half in range(2):
                (wvv,), wk = wload([(wb_in[l, :, 3344 + half * 256:3344 + (half + 1) * 256], 8, keys)])
                for tti in range(18):
                    bv = bank()
                    for k in range(8):
                        mm(ps[bv][:, :256], hT[:, k, tti * 128:(tti + 1) * 128], wvv[:, k, :], start=(k == 0), stop=(k == 7),
                           r=[wk, "hT"], w=[psk(bv)])
                    cp("act", RV[:, tti, half * 256:(half + 1) * 256], ps[bv][:, :256], r=[psk(bv)], wa=["RV"])
            for h in range(4):
                cq, e = h // 2, h % 2
                lgf = lgN[:, h:h + 1]; lgb = lgN[:, 4 + h:5 + h]; nlgb = lgE[:, 4 + h:5 + h]
                for pi in range(8):
                    m0 = pi * 512
                    w_ = min(512, FW - m0)
                    tsc("pool", dd[:, :w_], dbase[:, :w_], float(m0 - OFF), ALU.add, r=["consts"], w=["dd"])
                    tsc("pool", u2[:, :w_], dd[:, :w_], nlgb, ALU.mult, r=["dd", "lgE"], w=["u2"])
                    stt(u2[:, :w_], dd[:, :w_], lgf, u2[:, :w_], ALU.mult, ALU.min, r=["dd", "u2", "lgN"], w=["u2"])
                    act(u2[:, :w_], u2[:, :w_], AF.Exp, r=["u2"], w=["u2"])
                    stt(Fm[:, m0:m0 + w_], dd[:, :w_], 0.0, u2[:, :w_], ALU.is_equal, ALU.add, r=["dd", "u2"], w=["Fm"])
                (wg,), wkg = wload([(wb_in[l, :, 3856 + h * 128:3856 + (h + 1) * 128], 8, keys)])
                for bi, (t0, n) in enumerate(NBLK):
                    ktiles = [0, 1] if bi == 0 else list(range(2, 18)) + [0, 1]
                    t0l = t0 - TC
                    od = state.get("odr", 0); state["odr"] = 1 - od
                    bo = 4 + od
                    ns = len(ktiles)
                    SK = 2
                    if bi > 0:
                        for kt in (0, 1):
                            o1 = float(256 + t0l - kt * 128); o2 = float(2048 - t0l + kt * 128)
                            mk_ = cmsk[kt]; mkk = ("cmsk", kt)
                            tsc("pool", dd[:, :n], dbase[:, :n], o1, ALU.add, r=["consts"], w=["dd"])
                            act(e1[:, :n], dd[:, :n], AF.Exp, r=["dd", "lgN"], w=["e1"], scale=lgf)
                            tsc("pool", dd2[:, :n], dbase[:, :n], -1.0, ALU.mult, o2, ALU.add, r=["consts"], w=["dd2"])
                            act(mk_[:, :n], dd2[:, :n], AF.Exp, r=["dd2", "lgN"], w=[mkk], scale=lgb)
                            tt("pool", mk_[:, :n], mk_[:, :n], e1[:, :n], ALU.add, r=[mkk, "e1"], w=[mkk])
                    for i in range(ns + SK):
                        if i < ns:
                            kt = ktiles[i]
                            bsx = state.get("sbk", 0); state["sbk"] = (bsx + 1) % 4
                            mm(ps[bsx][:, :n], RK[e * 64:(e + 1) * 64, cq, kt * 128:(kt + 1) * 128],
                               RQ[e * 64:(e + 1) * 64, cq, t0:t0 + n], r=["RK", "RQ"], w=[psk(bsx)])
                            pt = PT[i % 4]; pk = ("PT", i % 4)
                            if bi == 0:
                                st_ = OFF - kt * 128
                                tt("dve", pt[:, :n], ps[bsx][:, :n], Fm[:, st_:st_ + n], ALU.mult, r=[psk(bsx), "Fm"], w=[pk])
                            elif kt >= 2:
                                st_ = OFF + t0l - (kt - 2) * 128
                                tt("dve", pt[:, :n], ps[bsx][:, :n], Fm[:, st_:st_ + n], ALU.mult, r=[psk(bsx), "Fm"], w=[pk])
                            else:
                                tt("dve", pt[:, :n], ps[bsx][:, :n], cmsk[kt][:, :n], ALU.mult, r=[psk(bsx), ("cmsk", kt)], w=[pk])
                        if i >= SK:
                            ii = i - SK
                            kt = ktiles[ii]
                            mm(ps[bo][:, :n], RV[:, kt, h * 128:(h + 1) * 128], PT[ii % 4][:, :n], start=(ii == 0), stop=(ii == ns - 1),
                               r=[("PT", ii % 4), "RV"], w=[psk(bo)])
                    proj(6, 128, wg, 0, t0, n, wkg)
                    act(sg[:, :n], ps[6][:, :n], AF.Silu, r=[psk(6)], w=["sg"])
                    act(osb[:, :n], ps[bo][:, :n], AF.Copy, r=[psk(bo)], w=["osb"])
                    act(sqo[:, :n], ps[bo][:, :n], AF.Square, r=[psk(bo)], w=["sqo"])
                    mm(ps[7][:, :n], C["onesm128"], osb[:, :n], r=["osb", "consts"], w=[psk(7)])
                    bx = state.get("sbk", 0); state["sbk"] = (bx + 1) % 4
                    mm(ps[bx][:, :n], C["onesm128"], sqo[:, :n], r=["sqo", "consts"], w=[psk(bx)])
                    act(mean_s[:, :n], ps[7][:, :n], AF.Copy, r=[psk(7)], w=["mean_s"])
                    tt("pool", tmpv[:, :n], mean_s[:, :n], mean_s[:, :n], ALU.mult, r=["mean_s"], w=["tmpv"])
                    tt("dve", rso[:, :n], ps[bx][:, :n], tmpv[:, :n], ALU.subtract, r=[psk(bx), "tmpv"], w=["rso"])
                    tsc("dve", rso[:, :n], rso[:, :n], 0.0, ALU.max, r=["rso"], w=["rso"])
                    act(rso[:, :n], rso[:, :n], AF.Sqrt, r=["rso", "eps"], w=["rso"], bias=eps_t[:])
                    recip(rso[:, :n], rso[:, :n], r=["rso"], w=["rso"])
                    tt("pool", osb[:, :n], osb[:, :n], mean_s[:, :n], ALU.subtract, r=["osb", "mean_s"], w=["osb"])
                    tt("pool", osb[:, :n], osb[:, :n], rso[:, :n], ALU.mult, r=["osb", "rso"], w=["osb"])
                    yb_, yk = ybuf()
                    tt("dve", yb_[:, :n], osb[:, :n], sg[:, :n], ALU.mult, r=["osb", "sg"], w=[yk])
                    dma("pool", ybr[2, h * 128:(h + 1) * 128, t0:t0 + n], yb_[:, :n], r=[yk], wa=[("ybr", 2)])

        la_tm = sb("la_tm", [128, 18, 8]); beta_tm = sb("beta_tm", [128, 18, 8]); acoef = sb("acoef", [128, 8])
        FORD = list(range(18))
        BORD = [1, 0] + list(range(17, 1, -1))

        def mixer_dn(l, b):
            keys = win_keys(l)
            ident = C["ident"]; ones = C["ones"]
            pre = aview(0, [T + 4]); cvb = aview(2308, [512]); slb = aview(2820, [512]); sqb = aview(3332, [512])
            rsb = aview(3844, [512]); ab = aview(4356, [18, 16]); tmp8 = aview(4644, [18, 8]); tmp8b = aview(4788, [18, 8])
            stg = [aview(4932 + i * 512, [512]) for i in range(2)]
            memset("pool", pre, 0.0, w=["pre"])
            for j in range(12):
                kind, h = j // 4, j % 4
                c0 = 768 + kind * 512 + h * 128
                (wv,), wk = wload([(wb_in[l, :, c0:c0 + 128], 8, keys)])
                for (t0, n) in NBLK:
                    bq = bank()
                    proj(bq, 128, wv, 0, t0, n, wk)
                    so = seg_off(t0)
                    cp("act", pre[:, so:so + n], ps[bq][:, :n], r=[psk(bq)], w=["pre"])
                wcol = lambda tap: pv[:, l * PV_N + PV_DNC + tap * 12 + j: l * PV_N + PV_DNC + tap * 12 + j + 1]
                for bi, (t0, n) in enumerate(NBLK):
                    so = seg_off(t0)
                    tsc("dve", cvb[:, :n], pre[:, so - 1:so - 1 + n], wcol(0), ALU.mult, r=["pre", "pv"], w=["cvb"])
                    stt(cvb[:, :n], pre[:, so:so + n], wcol(1), cvb[:, :n], ALU.mult, ALU.add, r=["pre", "pv", "cvb"], w=["cvb"])
                    stt(cvb[:, :n], pre[:, so + 1:so + 1 + n], wcol(2), cvb[:, :n], ALU.mult, ALU.add, r=["pre", "pv", "cvb"], w=["cvb"])
                    sg_ = stg[bi % 2]; sgk = ("stg", bi % 2)
                    if kind == 2:
                        act(sg_[:, :n], cvb[:, :n], AF.Silu, r=["cvb"], w=[sgk])
                    else:
                        act(slb[:, :n], cvb[:, :n], AF.Silu, r=["cvb"], w=["slb"])
                        act(sqb[:, :n], slb[:, :n], AF.Square, r=["slb"], w=["sqb"])
                        bs = bank()
                        mm(ps[bs][:, :n], ones, sqb[:, :n], r=["sqb", "consts"], w=[psk(bs)])
                        act(rsb[:, :n], ps[bs][:, :n], AF.Sqrt, r=[psk(bs), "eps"], w=["rsb"], bias=eps_t[:])
                        recip(rsb[:, :n], rsb[:, :n], r=["rsb"], w=["rsb"])
                        stt(sg_[:, :n], slb[:, :n], (128.0 ** -0.5) if kind == 0 else 1.0, rsb[:, :n], ALU.mult, ALU.mult,
                            r=["slb", "rsb"], w=[sgk])
                    dma("pool", dnq[j, :, t0:t0 + n], sg_[:, :n], r=[sgk], w=[("dnq", j, bi)])
            (wab,), wk = wload([(wb_in[l, :, 2816:2832], 8, keys)])
            bab = bank()
            for tti in range(18):
                for k in range(8):
                    mm(ps[bab][:, tti * 16:(tti + 1) * 16], hT[:, k, tti * 128:(tti + 1) * 128], wab[:, k, :], start=(k == 0), stop=(k == 7),
                       r=[wk, "hT"], w=[psk(bab)])
            cp("dve", ab, ps[bab][:, :288].rearrange("p (t c) -> p t c", c=16), r=[psk(bab)], w=["ab"])
            bcl = bc[:, l * BC_N:(l + 1) * BC_N]
            tt("dve", tmp8, ab[:, :, 0:8], bcl[:, BC_DTB:BC_DTB + 8].unsqueeze(1).to_broadcast([128, 18, 8]), ALU.add, r=["ab", "bc"], w=["tmp8"])
            act(tmp8, tmp8, AF.Exp, r=["tmp8"], w=["tmp8"])
            act(tmp8, tmp8, AF.Ln, r=["tmp8", "consts"], w=["tmp8"], bias=ones[:, 0:1])
            act(acoef[:], bcl[:, BC_ALOG:BC_ALOG + 8], AF.Exp, r=["bc"], w=["acoef"])
            tsc("dve", acoef[:], acoef[:], -1.0, ALU.mult, r=["acoef"], w=["acoef"])
            tt("dve", la_tm[:], tmp8, acoef[:].unsqueeze(1).to_broadcast([128, 18, 8]), ALU.mult, r=["tmp8", "acoef"], w=["la_tm"])
            act(beta_tm[:], ab[:, :, 8:16], AF.Sigmoid, r=["ab"], w=["beta_tm"])
            fence()
            class BT:
                def __init__(self, i):
                    self.i = i
                    self.t = aview(i * 1024, [1024])
                    self.v3 = self.t.rearrange("p (c j) -> p c j", j=128)
                    self.tr_ = self.t.bitcast(F32R)
                    self.v3r = self.tr_.rearrange("p (c j) -> p c j", j=128)
                def hr(self, d):
                    return self.tr_[:, d * 512:(d + 1) * 512]
                def h(self, d):
                    return self.t[:, d * 512:(d + 1) * 512]
                def c(self, c):
                    return self.v3[:, c, :]
                def k(self, d):
                    return ("b", self.i, d)
                def K(self):
                    return [("b", self.i, 0), ("b", self.i, 1)]
            Sst = BT(0); ktm = BT(4); vtm = BT(5); Dg = BT(6); Db = BT(7)
            Gm = BT(8); Gm2 = BT(9); E = BT(10); ET = BT(11); N_ = BT(12); NT_ = BT(13)
            Xs = [(BT(14), BT(15)), (BT(6), BT(7))]
            PTt = BT(16); aqkT = BT(17); wvu = BT(18); wkT = BT(19); qdT = BT(20); kdec = BT(21); Pm = BT(22); ost = BT(23)
            No = BT(8); NTo = BT(9); R1s = BT(10); R1ps = BT(11)
            qkv = aview(1024, [2, 12, 128])
            sm = aview(24576, [64])
            gt16 = sm[:, 0:16]; eg = sm[:, 16:24]; kdecs = sm[:, 24:32]; egl = sm[:, 32:40]; negbeta = sm[:, 40:48]
            bexp = sm[:, 48:56]; negg = sm[:, 56:64]
            betas = aview(24640, [8])
            hvc = lambda t2, d: t2[:, d * 512:(d + 1) * 512]
            bc3 = lambda small: small.unsqueeze(2).to_broadcast([128, 8, 128])
            b8 = lambda m: m.unsqueeze(1).to_broadcast([128, 8, 128])
            idb = b8(ident)
            offb = b8(C["offdiag"])
            A_, B_, C_, D_ = (0, 1), (2, 3), (4, 5), (6, 7)
            reg = lambda c: slice((c % 4) * 128, (c % 4 + 1) * 128)
            memset("pool", Sst.t, 0.0, w=Sst.K())
            for s_i in range(18):
                cd = (FORD[s_i], BORD[s_i])
                for d in range(2):
                    tok0 = cd[d] * 128
                    dma("sp", qkv[:, d], dnq[:, :, tok0:tok0 + 128].rearrange("j p t -> p j t"),
                        r=[("dnq", j, bi) for j in range(12) for bi in range(5)], w=[("qkv", d)])
                for d in range(2):
                    for h in range(4):
                        tr(ps[A_[d]][:, h * 128:(h + 1) * 128], qkv[:, d, 4 + h, :], ident, r=[("qkv", d), "consts"], w=[psk(A_[d])])
                    for h in range(4):
                        tr(ps[B_[d]][:, h * 128:(h + 1) * 128], qkv[:, d, 8 + h, :], ident, r=[("qkv", d), "consts"], w=[psk(B_[d])])
                    cp("act", ktm.h(d), ps[A_[d]][:, :], r=[psk(A_[d])], w=[ktm.k(d)])
                    cp("dve", vtm.h(d), ps[B_[d]][:, :], r=[psk(B_[d])], w=[vtm.k(d)])
                for d in range(2):
                    la_d = la_tm[:, cd[d], d * 4:(d + 1) * 4]
                    mm(ps[C_[0]][:, d * 4:(d + 1) * 4], C["tri_f"] if d == 0 else C["tri_b"], la_d, r=["la_tm", "consts"], w=[psk(C_[0])])
                    mm(ps[C_[0]][:, 8 + d * 4:8 + (d + 1) * 4], ones, la_d, r=["la_tm", "consts"], w=[psk(C_[0])])
                cp("dve", gt16, ps[C_[0]][:, 0:16], r=[psk(C_[0])], w=["gt16"])
                act(eg, gt16[:, 0:8], AF.Exp, r=["gt16"], w=["eg"])
                act(egl, gt16[:, 8:16], AF.Exp, r=["gt16"], w=["egl"])
                tsc("dve", negg, gt16[:, 0:8], -1.0, ALU.mult, r=["gt16"], w=["negg"])
                tt("dve", kdecs, gt16[:, 8:16], gt16[:, 0:8], ALU.subtract, r=["gt16"], w=["kdecs"])
                act(kdecs, kdecs, AF.Exp, r=["kdecs"], w=["kdecs"])
                cp("dve", betas[:, 0:4], beta_tm[:, cd[0], 0:4], r=["beta_tm"], w=["betas"])
                cp("dve", betas[:, 4:8], beta_tm[:, cd[1], 4:8], r=["beta_tm", "betas"], w=["betas"])
                tsc("dve", negbeta, betas, -1.0, ALU.mult, r=["betas"], w=["negbeta"])
                tt("dve", bexp, betas, eg, ALU.mult, r=["betas", "eg"], w=["bexp"])
                tt("dve", Dg.v3, idb, bc3(gt16[:, 0:8]), ALU.mult, r=["consts", "gt16"], w=Dg.K())
                tt("pool", Db.v3, idb, bc3(betas), ALU.mult, r=["consts", "betas"], w=Db.K())
                for d in range(2):
                    mm(ps[A_[d]][:, :], ones, Dg.h(d), r=[Dg.k(d), "consts"], w=[psk(A_[d])])
                    mm(ps[B_[d]][:, :], ones, Db.h(d), r=[Db.k(d), "consts"], w=[psk(B_[d])])
                for d in range(2):
                    tt("dve", Gm.h(d), ps[A_[d]][:, :], hvc(C["mask_s"], d), ALU.add, r=[psk(A_[d]), "consts"], w=[Gm.k(d)])
                    tt("dve", Gm2.h(d), ps[A_[d]][:, :], hvc(C["mask_t"], d), ALU.subtract, r=[psk(A_[d]), "consts"], w=[Gm2.k(d)])
                for c in range(8):
                    d = c // 4
                    act(E.c(c), Gm.c(c), AF.Exp, r=[Gm.k(d), "gt16"], w=[E.k(d)], scale=-1.0, bias=gt16[:, c:c + 1])
                for c in range(8):
                    d = c // 4
                    act(ET.c(c), Gm2.c(c), AF.Exp, r=[Gm2.k(d), "negg"], w=[ET.k(d)], scale=1.0, bias=negg[:, c:c + 1])
                for d in range(2):
                    for h in range(4):
                        mm(ps[C_[d]][:, h * 128:(h + 1) * 128], qkv[:, d, 4 + h, :], qkv[:, d, 4 + h, :], r=[("qkv", d)], w=[psk(C_[d])])
                    for h in range(4):
                        mm(ps[D_[d]][:, h * 128:(h + 1) * 128], qkv[:, d, 4 + h, :], qkv[:, d, h, :], r=[("qkv", d)], w=[psk(D_[d])])
                for d in range(2):
                    tt("dve", aqkT.h(d), ps[D_[d]][:, :], ET.h(d), ALU.mult, r=[psk(D_[d]), ET.k(d)], w=[aqkT.k(d)])
                tt("pool", E.v3, E.v3, offb, ALU.mult, r=E.K() + ["consts"], w=E.K())
                tt("pool", E.v3, E.v3, bc3(negbeta), ALU.mult, r=E.K() + ["negbeta"], w=E.K())
                tt("pool", ET.v3, ET.v3, offb, ALU.mult, r=ET.K() + ["consts"], w=ET.K())
                for d in range(2):
                    tt("dve", N_.h(d), ps[C_[d]][:, :], E.h(d), ALU.mult, r=[psk(C_[d]), E.k(d)], w=[N_.k(d)])
                    tt("dve", ET.h(d), ps[B_[d]][:, :], ET.h(d), ALU.mult, r=[psk(B_[d]), ET.k(d)], w=[ET.k(d)])
                    stt(NT_.h(d), ps[C_[d]][:, :], -1.0, ET.h(d), ALU.mult, ALU.mult, r=[psk(C_[d]), ET.k(d)], w=[NT_.k(d)])
                X, XT = Xs[0]
                tt("pool", X.v3, N_.v3, b8(C["bd8"]), ALU.mult, r=N_.K() + ["consts"], w=X.K())
                tt("pool", XT.v3, NT_.v3, b8(C["bd8"]), ALU.mult, r=NT_.K() + ["consts"], w=XT.K())
                tt("pool", Pm.v3, X.v3, idb, ALU.add, r=X.K() + ["consts"], w=Pm.K())
                tt("pool", PTt.v3, XT.v3, idb, ALU.add, r=XT.K() + ["consts"], w=PTt.K())
                cur = 0
                for it in range(2):
                    X, XT = Xs[cur]
                    Xn, XTn = Xs[1 - cur]
                    for c in range(8):
                        d = c // 4
                        mmr(ps[A_[d]][:, reg(c)], XT.c(c), X.c(c), r=[X.k(d), XT.k(d)], w=[psk(A_[d])])
                    for c in range(8):
                        d = c // 4
                        mmr(ps[B_[d]][:, reg(c)], X.c(c), XT.c(c), r=[X.k(d), XT.k(d)], w=[psk(B_[d])])
                    for d in range(2):
                        cp("act", Xn.h(d), ps[A_[d]][:, :], r=[psk(A_[d])], w=[Xn.k(d)])
                        cp("dve", XTn.h(d), ps[B_[d]][:, :], r=[psk(B_[d])], w=[XTn.k(d)])
                    for c in range(8):
                        d = c // 4
                        mmr(ps[C_[d]][:, reg(c)], XTn.c(c), Pm.c(c), r=[XTn.k(d), Pm.k(d)], w=[psk(C_[d])])
                    for c in range(8):
                        d = c // 4
                        mmr(ps[D_[d]][:, reg(c)], Xn.c(c), PTt.c(c), r=[Xn.k(d), PTt.k(d)], w=[psk(D_[d])])
                    for d in range(2):
                        tt("dve", Pm.h(d), ps[C_[d]][:, :], Pm.h(d), ALU.add, r=[psk(C_[d]), Pm.k(d)], w=[Pm.k(d)])
                        tt("dve", PTt.h(d), ps[D_[d]][:, :], PTt.h(d), ALU.add, r=[psk(D_[d]), PTt.k(d)], w=[PTt.k(d)])
                    cur = 1 - cur
                for lv, offn in enumerate(["off8", "off16", "off32", "off64"]):
                    lastlv = (lv == 3)
                    tt("pool", No.v3, N_.v3, b8(C[offn]), ALU.mult, r=N_.K() + ["consts"], w=No.K())
                    if not lastlv:
                        tt("pool", NTo.v3, NT_.v3, b8(C[offn]), ALU.mult, r=NT_.K() + ["consts"], w=NTo.K())
                        for c in range(8):
                            d = c // 4
                            mmr(ps[A_[d]][:, reg(c)], NTo.c(c), Pm.c(c), r=[NTo.k(d), Pm.k(d)], w=[psk(A_[d])])
                    for c in range(8):
                        d = c // 4
                        mmr(ps[B_[d]][:, reg(c)], No.c(c), PTt.c(c), r=[No.k(d), PTt.k(d)], w=[psk(B_[d])])
                    for d in range(2):
                        if not lastlv:
                            cp("act", R1s.h(d), ps[A_[d]][:, :], r=[psk(A_[d])], w=[R1s.k(d)])
                        cp("dve", R1ps.h(d), ps[B_[d]][:, :], r=[psk(B_[d])], w=[R1ps.k(d)])
                    if not lastlv:
                        for c in range(8):
                            d = c // 4
                            mmr(ps[C_[d]][:, reg(c)], PTt.c(c), R1s.c(c), r=[PTt.k(d), R1s.k(d)], w=[psk(C_[d])])
                    for c in range(8):
                        d = c // 4
                        mmr(ps[D_[d]][:, reg(c)], Pm.c(c), R1ps.c(c), r=[Pm.k(d), R1ps.k(d)], w=[psk(D_[d])])
                    for d in range(2):
                        if not lastlv:
                            tt("dve", Pm.h(d), ps[C_[d]][:, :], Pm.h(d), ALU.add, r=[psk(C_[d]), Pm.k(d)], w=[Pm.k(d)])
                        tt("dve", PTt.h(d), ps[D_[d]][:, :], PTt.h(d), ALU.add, r=[psk(D_[d]), PTt.k(d)], w=[PTt.k(d)])
                tt("pool", kdec.v3, ktm.v3, bc3(kdecs), ALU.mult, r=ktm.K() + ["kdecs"], w=kdec.K())
                tt("pool", ktm.v3, ktm.v3, bc3(bexp), ALU.mult, r=ktm.K() + ["bexp"], w=ktm.K())
                tt("pool", vtm.v3, vtm.v3, bc3(betas), ALU.mult, r=vtm.K() + ["betas"], w=vtm.K())
                for c in range(8):
                    d = c // 4
                    mmr(ps[A_[d]][:, reg(c)], PTt.c(c), vtm.c(c), r=[PTt.k(d), vtm.k(d)], w=[psk(A_[d])])
                    mmr(ps[B_[d]][:, reg(c)], ktm.c(c), PTt.c(c), r=[PTt.k(d), ktm.k(d)], w=[psk(B_[d])])
                for d in range(2):
                    cp("dve", wvu.h(d), ps[A_[d]][:, :], r=[psk(A_[d])], w=[wvu.k(d)])
                    cp("act", wkT.h(d), ps[B_[d]][:, :], r=[psk(B_[d])], w=[wkT.k(d)])
                tt("dve", Dg.v3, idb, bc3(eg), ALU.mult, r=["consts", "eg"], w=Dg.K())
                for d in range(2):
                    mm(ps[D_[d]][:, :], ones, Dg.h(d), r=[Dg.k(d), "consts"], w=[psk(D_[d])])
                    tt("dve", qdT.h(d), ps[D_[d]][:, :], qkv[:, d, 0:4, :], ALU.mult, r=[psk(D_[d]), ("qkv", d)], w=[qdT.k(d)])
                for c in range(8):
                    d = c // 4
                    mmr(ps[C_[d]][:, reg(c)], wkT.c(c), Sst.c(c), r=[wkT.k(d), Sst.k(d)], w=[psk(C_[d])])
                for d in range(2):
                    tt("dve", wvu.h(d), wvu.h(d), ps[C_[d]][:, :], ALU.subtract, r=[psk(C_[d]), wvu.k(d)], w=[wvu.k(d)])
                for c in range(8):
                    d = c // 4
                    mmr(ps[A_[d]][:, reg(c)], Sst.c(c), qdT.c(c), start=True, stop=False, r=[Sst.k(d), qdT.k(d)], w=[psk(A_[d])])
                    mmr(ps[A_[d]][:, reg(c)], wvu.c(c), aqkT.c(c), start=False, stop=True, r=[wvu.k(d), aqkT.k(d)], w=[psk(A_[d])])
                    mmr(ps[B_[d]][:, reg(c)], kdec.c(c), wvu.c(c), r=[kdec.k(d), wvu.k(d)], w=[psk(B_[d])])
                tt("pool", Sst.v3, Sst.v3, bc3(egl), ALU.mult, r=Sst.K() + ["egl"], w=Sst.K())
                for d in range(2):
                    tt("dve", Sst.h(d), Sst.h(d), ps[B_[d]][:, :], ALU.add, r=[psk(B_[d]), Sst.k(d)], w=[Sst.k(d)])
                    cp("act", ost.h(d), ps[A_[d]][:, :], r=[psk(A_[d])], w=[ost.k(d)])
                    tok0 = cd[d] * 128
                    dma("pool", dno[d, :, :, tok0:tok0 + 128].rearrange("h p t -> p h t"), ost.v3[:, d * 4:(d + 1) * 4, :],
                        r=[ost.k(d)], wa=[("dno", d)])
            fence()
            of_ = aview(0, [512]); ob_ = aview(512, [512]); sq3 = aview(1024, [512]); rs3 = aview(1536, [512]); sz = aview(2048, [512])
            gcol = pv[:, l * PV_N + PV_DNG:l * PV_N + PV_DNG + 1]
            for h in range(4):
                (wz,), wkz = wload([(wb_in[l, :, 2304 + h * 128:2304 + (h + 1) * 128], 8, keys)])
                for (t0, n) in NBLK:
                    dma("sp", of_[:, :n], dno[0, h, :, t0:t0 + n], r=[("dno", 0)], w=["of_"])
                    dma("sp", ob_[:, :n], dno[1, h, :, t0:t0 + n], r=[("dno", 1)], w=["ob_"])
                    tt("pool", of_[:, :n], of_[:, :n], ob_[:, :n], ALU.add, r=["of_", "ob_"], w=["of_"])
                    act(sq3[:, :n], of_[:, :n], AF.Square, r=["of_"], w=["sq3"])
                    bs = bank()
                    mm(ps[bs][:, :n], C["onesm128"], sq3[:, :n], r=["sq3", "consts"], w=[psk(bs)])
                    act(rs3[:, :n], ps[bs][:, :n], AF.Sqrt, r=[psk(bs), "eps"], w=["rs3"], bias=eps_t[:])
                    recip(rs3[:, :n], rs3[:, :n], r=["rs3"], w=["rs3"])
                    stt(of_[:, :n], of_[:, :n], gcol, rs3[:, :n], ALU.mult, ALU.mult, r=["of_", "rs3", "pv"], w=["of_"])
                    bz = bank()
                    proj(bz, 128, wz, 0, t0, n, wkz)
                    act(sz[:, :n], ps[bz][:, :n], AF.Silu, r=[psk(bz)], w=["sz"])
                    yb_, yk = ybuf()
                    tt("pool", yb_[:, :n], of_[:, :n], sz[:, :n], ALU.mult, r=["of_", "sz"], w=[yk])
                    dma("pool", ybr[1, h * 128:(h + 1) * 128, t0:t0 + n], yb_[:, :n], r=[yk], wa=[("ybr", 1)])

        ybufs = [sb("ybuf%d" % i, [128, 512], BF16) for i in range(3)]

        def ybuf():
            i = state.get("yb", 0)
            state["yb"] = (i + 1) % 3
            return ybufs[i], "ybuf%d" % i


        def phase_c(l, b, last):
            act_br = [i for i, nm in enumerate(("att", "dn", "ret", "sc")) if nm in enabled]
            for bi, (t0, n) in enumerate(NBLK):
                col = NB if bi == 0 else b
                for i in act_br:
                    dma("pool", ybl[:, i, :, :n], ybr[i, :, t0:t0 + n].rearrange("(c p) t -> p c t", p=128),
                        r=[("ybr", i)], w=[("ybl", i)])
                for m in range(8):
                    for ii, i in enumerate(act_br):
                        wgb, wk = wload_c(wb_mg[l, i, m], 12, 128, [("wb_mg", l, i, k) for k in range(12)])
                        wg = wgb[:, 0:8, :]; wbr = wgb[:, 8:12, :]
                        bg = bank(); by = bank()
                        proj(bg, 128, wg, 0, t0, n, wk)
                        for k in range(4):
                            mm(ps[by][:, :n], wbr[:, k, :], ybl[:, i, k, :n], start=(k == 0), stop=(k == 3), r=[wk, ("ybl", i)], w=[psk(by)])
                        sg, sk = t5()
                        act(sg[:, :n], ps[bg][:, :n], AF.Sigmoid, r=[psk(bg)], w=[sk])
                        lastb = (ii == len(act_br) - 1)
                        dst, dk = (ubf, ("ubf", m)) if lastb else (wkA, ("wkA", m))
                        if ii == 0:
                            tt("dve", dst[:, m, :n], ps[by][:, :n], sg[:, :n], ALU.mult, r=[psk(by), sk], w=[dk])
                        else:
                            tt("dve", sg[:, :n], ps[by][:, :n], sg[:, :n], ALU.mult, r=[psk(by), sk], w=[sk])
                            tt("pool", dst[:, m, :n], wkA[:, m, :n], sg[:, :n], ALU.add, r=[("wkA", m), sk], w=[dk])
                for mg in range(4):
                    wo, wk = wload_c(wb_out[l, mg], 8, 256, [("wb_out", l, k) for k in range(8)])
                    for m2 in range(2):
                        m = mg * 2 + m2
                        bo = bank()
                        for k in range(8):
                            mm(ps[bo][:, :n], wo[:, k, m2 * 128:(m2 + 1) * 128], ubf[:, k, :n], start=(k == 0), stop=(k == 7),
                               r=[wk] + [("ubf", kk) for kk in range(8)], w=[psk(bo)])
                        cp("act", wkA[:, m, :n], ps[bo][:, :n], r=[psk(bo)], w=[("wkA", m)])
                dma("sp", xblk[:, :, :n], xres_v[:, :, t0:t0 + n], r=[("xres", bi)], w=["xblk"])
                residual_block(l, col, 0, wkA[:, :, :n], n, [("wkA", m) for m in range(8)])
                norm_mod_block(l, col, 1, xblk[:, :, :n], t0, n, ["xblk"])
                for pg in range(16):
                    w1, wk = wload_c(wb_m1[l, pg], 8, 256, [("wb_m1", l, k, pg // 8) for k in range(8)])
                    for m2 in range(2):
                        m = pg * 2 + m2
                        bh = bank()
                        proj(bh, 128, w1, m2 * 128, t0, n, wk)
                        rr, rk = t5()
                        act(rr[:, :n], ps[bh][:, :n], AF.Relu, r=[psk(bh)], w=[rk])
                        tt("pool", hid[:, m, :n], rr[:, :n], rr[:, :n], ALU.mult, r=[rk], w=[("hid", m)])
                for mg in range(4):
                    b0 = bank(); b1 = bank()
                    bb = (b0, b1)
                    for kg in range(4):
                        w2, wk = wload_c(wb_m2[l, kg, mg], 8, 256, [("wb_m2", l, kg * 8 + k) for k in range(8)])
                        for m2 in range(2):
                            for k in range(8):
                                mm(ps[bb[m2]][:, :n], w2[:, k, m2 * 128:(m2 + 1) * 128], hid[:, kg * 8 + k, :n],
                                   start=(kg == 0 and k == 0), stop=(kg == 3 and k == 7),
                                   r=[wk, ("hid", kg * 8 + k)], w=[psk(bb[m2])])
                    for m2 in range(2):
                        cp("act", wkA[:, mg * 2 + m2, :n], ps[bb[m2]][:, :n], r=[psk(bb[m2])], w=[("wkA", mg * 2 + m2)])
                residual_block(l, col, 1, wkA[:, :, :n], n, [("wkA", m) for m in range(8)])
                dma("sp", xres_v[:, :, t0:t0 + n], xblk[:, :, :n], r=["xblk"], w=[("xres", bi)])

        for b in range(NB):
            for bi, (t0, n) in enumerate(NBLK):
                dma("sp", xres[:, t0:t0 + n], xT_in[b, :, t0:t0 + n], w=[("xres", bi)])
            for l in range(L):
                if os.environ.get("RESET_ROT", "0") == "1":
                    for kk_ in ("bank", "ws", "t5", "yb", "od", "odr", "sbk"):
                        state[kk_] = 0
                phase_a(l, b)
                fence()
                if b == 0:
                    cast_some(l + 1, 0, 4)
                if "att" in enabled:
                    mixer_att(l, b)
                    fence()
                if b == 0:
                    cast_some(l + 1, 1, 4)
                if "dn" in enabled:
                    mixer_dn(l, b)
                    fence()
                if b == 0:
                    cast_some(l + 1, 2, 4)
                if "ret" in enabled:
                    mixer_ret(l, b)
                    fence()
                if b == 0:
                    cast_some(l + 1, 3, 4)
                if "sc" in enabled:
                    mixer_sc(l, b)
                    fence()
                phase_c(l, b, l == L - 1)
                if "xl" in dbg_out and b == 0:
                    for bi, (t0, n) in enumerate(NBLK):
                        out_events.append(dma("sp", dbg_out["xl"][l, :, t0:t0 + n], xres[:, t0:t0 + n], r=[("xres", bi)]))
                fence()
            for bi, (t0, n) in enumerate(NBLK[1:]):
                out_events.append(dma("sp", outT[b, :, t0 - TC:t0 - TC + n], xres[:, t0:t0 + n], r=[("xres", bi + 1)]))
        P.finish(out_events)
        P.emit(st)
    return nc, P


def make_in_maps(inp, NB, ncores, L=4):
    pv, bc = _host_params(inp, L)
    cst = _consts()
    const_arr = np.ascontiguousarray(np.concatenate([cst[k] for k in CONST_ORDER], axis=1))
    cosT, sinT = _rope_tables()
    maps = []
    for core in range(ncores):
        bs = slice(core * NB, (core + 1) * NB)
        xcat = np.concatenate([inp["ctx"][bs], inp["x"][bs]], axis=1)
        xT = np.ascontiguousarray(xcat.transpose(0, 2, 1))
        call = np.concatenate([inp["c"][bs], inp["c_ctx"][None, :]], axis=0)
        cT = np.ascontiguousarray(call.reshape(NB + 1, 8, 128).transpose(2, 1, 0))
        maps.append({"xT_in": xT, "cT": cT, "pv": pv, "bc": bc, "consts": const_arr, "cosT": cosT, "sinT": sinT,
                     "w_mod": inp["w_mod"][:L], "w_in": inp["w_in"][:L], "w_branch": inp["w_branch"][:L],
                     "w_out": inp["w_out"][:L], "w_mlp_in": inp["w_mlp_in"][:L], "w_mlp_out": inp["w_mlp_out"][:L]})
    return maps


_CACHE = {}


def kernel(**inputs):
    inp = {k: np.asarray(v) for k, v in inputs.items()}
    B = inp["x"].shape[0]
    ncores = 8
    NB = B // ncores
    if "nc" not in _CACHE:
        _CACHE["nc"] = build(NB)[0]
    nc = _CACHE["nc"]
    maps = make_in_maps(inp, NB, ncores)
    res = run_bass_kernel_spmd(nc, maps, core_ids=list(range(ncores)))
    out = np.concatenate([r["outT"] for r in res.results], axis=0)
    return np.ascontiguousarray(out.transpose(0, 2, 1)).astype(np.float32)
```
